# Optimizing a Trainium2 kernel written in Bass

```python
import math
import jax, jax.numpy as jnp
from jax import lax
import numpy as np

D_MODEL = 1024
BATCH = 4
SEQ = 8192
DEPTH = 1

N_HEADS = 8
HEAD_DIM = D_MODEL // N_HEADS
ATTN_WIDTH = N_HEADS * HEAD_DIM
MOBA_BLOCK = 256
MOBA_TOPK = 3
Q_CHUNK = 32
ROPE_THETA = 10000.0
SSM_WIDTH = D_MODEL // 2
SSM_GROUP = 16
SSM_GROUPS = SSM_WIDTH // SSM_GROUP
SSM_STATE = 64
DT_MIN = 1e-3
DT_MAX = 1e-1
N_BRANCHES = 2
D_FF = 4 * D_MODEL
RMS_EPS = 1e-6
NEG_BIG = -1e30
IN_WIDTH = 3 * ATTN_WIDTH + SSM_WIDTH + N_BRANCHES * D_MODEL

kernel_name = "hybrid_moba_s5_gated_block"


def rms_norm(x, g):
    xf = x.astype(jnp.float32)
    y = xf * lax.rsqrt(jnp.mean(xf * xf, axis=-1, keepdims=True) + RMS_EPS)
    return (y * g.astype(jnp.float32)).astype(x.dtype)


def rotary(x, pos):
    half = HEAD_DIM // 2
    inv_freq = ROPE_THETA ** (-jnp.arange(half, dtype=jnp.float32) / half)
    ang = pos.astype(jnp.float32)[:, None] * inv_freq[None, :]
    cos = jnp.cos(ang)[None, :, None, :]
    sin = jnp.sin(ang)[None, :, None, :]
    xf = x.astype(jnp.float32)
    x1, x2 = xf[..., :half], xf[..., half:]
    out = jnp.concatenate([x1 * cos - x2 * sin, x2 * cos + x1 * sin], axis=-1)
    return out.astype(x.dtype)


def moba_attention(q, k, v):
    b, l, h, d = q.shape
    lp = -(-l // MOBA_BLOCK) * MOBA_BLOCK
    nb = lp // MOBA_BLOCK
    pad = ((0, 0), (0, lp - l), (0, 0), (0, 0))
    q, k, v = [jnp.pad(t, pad).transpose(0, 2, 1, 3) for t in (q, k, v)]
    kb = k.reshape(b, h, nb, MOBA_BLOCK, d)
    vb = v.reshape(b, h, nb, MOBA_BLOCK, d)
    k_mean = jnp.mean(kb.astype(jnp.float32), axis=3)
    gate = jnp.einsum('bhqd,bhnd->bhqn', q.astype(jnp.float32), k_mean)
    pos = jnp.arange(lp)
    q_blk = pos // MOBA_BLOCK
    past = jnp.arange(nb)[None, :] < q_blk[:, None]
    gate = jnp.where(past, gate, NEG_BIG)
    topk = min(MOBA_TOPK, nb)
    _, sel = lax.top_k(gate, topk)
    sel_valid = sel < q_blk[:, None]
    own = jnp.broadcast_to(q_blk[:, None].astype(sel.dtype), (b, h, lp, 1))
    idx = jnp.concatenate([sel, own], axis=-1)
    valid = jnp.concatenate([sel_valid, jnp.ones((b, h, lp, 1), dtype=bool)], axis=-1)

    nc = lp // Q_CHUNK

    def to_chunks(t):
        return jnp.moveaxis(t.reshape((b, h, nc, Q_CHUNK) + t.shape[3:]), 2, 0)

    bi = jnp.arange(b)[:, None, None, None]
    hi = jnp.arange(h)[None, :, None, None]
    key_off = jnp.arange(MOBA_BLOCK)
    scale = HEAD_DIM ** -0.5

    def attend(args):
        qc, ic, vc, pc = args
        kg = kb[bi, hi, ic]
        vg = vb[bi, hi, ic]
        s = jnp.einsum('bhqd,bhqskd->bhqsk', qc, kg,
                       preferred_element_type=jnp.float32) * scale
        kpos = ic[..., None] * MOBA_BLOCK + key_off
        mask = vc[..., None] & (kpos <= pc[None, None, :, None, None])
        s = jnp.where(mask, s, NEG_BIG)
        p = jax.nn.softmax(s.reshape(b, h, Q_CHUNK, -1), axis=-1).reshape(s.shape)
        return jnp.einsum('bhqsk,bhqskd->bhqd', p.astype(vg.dtype), vg)

    out = lax.map(attend, (to_chunks(q), to_chunks(idx), to_chunks(valid),
                           pos.reshape(nc, Q_CHUNK)))
    out = jnp.moveaxis(out, 0, 2).reshape(b, h, lp, d)[:, :, :l]
    return out.transpose(0, 2, 1, 3).reshape(b, l, h * d)


def s5_branch(u, lam_re, lam_im, log_step, b_re, b_im, c_re, c_im, d_skip, w_glu):
    b, l, _ = u.shape
    f32 = jnp.float32
    uf = u.astype(f32).reshape(b, l, SSM_GROUPS, SSM_GROUP)
    step = jnp.exp(log_step.astype(f32))[:, None]
    lr, li = lam_re.astype(f32), lam_im.astype(f32)
    mag = jnp.exp(lr * step)
    ar = mag * jnp.cos(li * step)
    ai = mag * jnp.sin(li * step)
    den = lr * lr + li * li
    nr, ni = ar - 1.0, ai
    cr = (nr * lr + ni * li) / den
    ci = (ni * lr - nr * li) / den
    bu_r = jnp.einsum('blgc,gnc->blgn', uf, b_re.astype(f32))
    bu_i = jnp.einsum('blgc,gnc->blgn', uf, b_im.astype(f32))
    xr0 = cr * bu_r - ci * bu_i
    xi0 = cr * bu_i + ci * bu_r
    a_r = jnp.broadcast_to(ar[None, None], (1, l, SSM_GROUPS, SSM_STATE))
    a_i = jnp.broadcast_to(ai[None, None], (1, l, SSM_GROUPS, SSM_STATE))

    def combine(e1, e2):
        a1r, a1i, b1r, b1i = e1
        a2r, a2i, b2r, b2i = e2
        return (a2r * a1r - a2i * a1i,
                a2r * a1i + a2i * a1r,
                a2r * b1r - a2i * b1i + b2r,
                a2r * b1i + a2i * b1r + b2i)

    _, _, sr, si = lax.associative_scan(combine, (a_r, a_i, xr0, xi0), axis=1)
    y = (jnp.einsum('blgn,gcn->blgc', sr, c_re.astype(f32))
         - jnp.einsum('blgn,gcn->blgc', si, c_im.astype(f32))
         + d_skip.astype(f32) * uf)
    y = jax.nn.gelu(y.reshape(b, l, SSM_WIDTH)).astype(u.dtype)
    hgl = y @ w_glu
    val, gt = jnp.split(hgl, 2, axis=-1)
    return val * jax.nn.sigmoid(gt)


def setup_inputs(seed: int = 0) -> dict:
    key = jax.random.key(seed)
    ks = jax.random.split(key, 20)
    f32 = jnp.float32
    nrm = lambda k, s, sc: jax.random.normal(k, s, f32) * sc
    x = jax.random.normal(ks[0], (BATCH, SEQ, D_MODEL), f32)
    norm_mix_g = 1.0 + nrm(ks[1], (DEPTH, D_MODEL), 0.02)
    w_in = nrm(ks[2], (DEPTH, D_MODEL, IN_WIDTH), D_MODEL ** -0.5)
    n_idx = jnp.arange(SSM_STATE, dtype=f32)
    lam_re = -0.5 + nrm(ks[3], (DEPTH, SSM_GROUPS, SSM_STATE), 0.01)
    lam_im = jnp.broadcast_to(math.pi * n_idx, (DEPTH, SSM_GROUPS, SSM_STATE)) \
        + nrm(ks[4], (DEPTH, SSM_GROUPS, SSM_STATE), 0.01)
    log_step = jax.random.uniform(ks[5], (DEPTH, SSM_GROUPS), f32,
                                  math.log(DT_MIN), math.log(DT_MAX))
    b_sc = (2.0 * SSM_GROUP) ** -0.5
    b_re = nrm(ks[6], (DEPTH, SSM_GROUPS, SSM_STATE, SSM_GROUP), b_sc)
    b_im = nrm(ks[7], (DEPTH, SSM_GROUPS, SSM_STATE, SSM_GROUP), b_sc)
    c_sc = (2.0 * SSM_STATE) ** -0.5
    c_re = nrm(ks[8], (DEPTH, SSM_GROUPS, SSM_GROUP, SSM_STATE), c_sc)
    c_im = nrm(ks[9], (DEPTH, SSM_GROUPS, SSM_GROUP, SSM_STATE), c_sc)
    d_skip = nrm(ks[10], (DEPTH, SSM_GROUPS, SSM_GROUP), 1.0)
    w_glu = nrm(ks[11], (DEPTH, SSM_WIDTH, 2 * D_MODEL), SSM_WIDTH ** -0.5)
    w_out = nrm(ks[12], (DEPTH, D_MODEL, D_MODEL), D_MODEL ** -0.5)
    norm_mlp_g = 1.0 + nrm(ks[13], (DEPTH, D_MODEL), 0.02)
    w_up = nrm(ks[14], (DEPTH, D_MODEL, D_FF), D_MODEL ** -0.5)
    w_down = nrm(ks[15], (DEPTH, D_FF, D_MODEL), D_FF ** -0.5)
    norm_final_g = 1.0 + nrm(ks[16], (D_MODEL,), 0.02)
    return {"x": x, "norm_mix_g": norm_mix_g, "w_in": w_in, "lam_re": lam_re,
            "lam_im": lam_im, "log_step": log_step, "b_re": b_re, "b_im": b_im,
            "c_re": c_re, "c_im": c_im, "d_skip": d_skip, "w_glu": w_glu,
            "w_out": w_out, "norm_mlp_g": norm_mlp_g, "w_up": w_up,
            "w_down": w_down, "norm_final_g": norm_final_g}


def reference(x, norm_mix_g, w_in, lam_re, lam_im, log_step, b_re, b_im, c_re, c_im,
              d_skip, w_glu, w_out, norm_mlp_g, w_up, w_down, norm_final_g):
    b, l, _ = x.shape
    pos = jnp.arange(l)
    splits = [ATTN_WIDTH, 2 * ATTN_WIDTH, 3 * ATTN_WIDTH, 3 * ATTN_WIDTH + SSM_WIDTH]
    for i in range(DEPTH):
        h = rms_norm(x, norm_mix_g[i])
        proj = h @ w_in[i]
        q, k, v, u, g = jnp.split(proj, splits, axis=-1)
        q = rotary(q.reshape(b, l, N_HEADS, HEAD_DIM), pos)
        k = rotary(k.reshape(b, l, N_HEADS, HEAD_DIM), pos)
        v = v.reshape(b, l, N_HEADS, HEAD_DIM)
        o_a = moba_attention(q, k, v)
        o_b = s5_branch(u, lam_re[i], lam_im[i], log_step[i], b_re[i], b_im[i],
                        c_re[i], c_im[i], d_skip[i], w_glu[i])
        g_a, g_b = jnp.split(g, 2, axis=-1)
        mixed = jax.nn.sigmoid(g_a) * o_a + jax.nn.sigmoid(g_b) * o_b
        x = x + mixed @ w_out[i]
        h = rms_norm(x, norm_mlp_g[i])
        x = x + jnp.square(jax.nn.relu(h @ w_up[i])) @ w_down[i]
    return rms_norm(x, norm_final_g)
```

```python
import math
from contextlib import ExitStack

import numpy as np
import concourse.bass as bass
import concourse.mybir as mybir
from concourse.bass_utils import run_bass_kernel_spmd

F32 = mybir.dt.float32
BF16 = mybir.dt.bfloat16
I32 = mybir.dt.int32
ALU = mybir.AluOpType
AF = mybir.ActivationFunctionType
AX = mybir.AxisListType

D = 1024
SEQ = 8192
HALF = 4096
NH = 8
HD = 128
INW = 5632
SSMW = 512
DFF = 4096
EPS = 1e-6
TT = 512
PI = math.pi
TWO_PI = 2.0 * math.pi
MASKV = 32768.0


class Buf:
    __slots__ = ("t", "w", "r", "name", "psum")

    def __init__(self, t, name="", psum=False):
        self.t = t
        self.w = {}
        self.r = {}
        self.name = name
        self.psum = psum

    def __getitem__(self, idx):
        return self.t[idx]


class KB:
    def __init__(self, nc, es):
        self.nc = nc
        self.es = es
        self.E = {"pe": nc.tensor, "act": nc.scalar, "dve": nc.vector,
                  "pool": nc.gpsimd, "sp": nc.sync}
        self.sem = {}
        self.cnt = {}
        self.seen = {}
        for n in ["pe", "act", "dve", "pool"]:
            self.sem[n] = es.enter_context(nc.semaphore("s_" + n))
            self.cnt[n] = 0
        self.ndq = 0

    def new_dma_sem(self, name):
        s = "dq_" + name
        self.sem[s] = self.es.enter_context(self.nc.semaphore(s))
        self.cnt[s] = 0
        return s

    def mult(self, s):
        return 16 if s.startswith("dq_") else 1

    def wait(self, e, tok):
        s, v = tok
        if s == e and e == "pe":
            return
        key = (e, s)
        if self.seen.get(key, 0) >= v:
            return
        self.seen[key] = v
        self.E[e].wait_ge(self.sem[s], v * self.mult(s))

    def sync(self, e, reads=(), writes=(), deps=(), issuer=None):
        we = issuer or e
        for b in reads:
            for s, v in b.w.items():
                self.wait(we, (s, v))
            if b.psum:
                for s, v in b.r.items():
                    if s != e:
                        self.wait(we, (s, v))
        for b in writes:
            for s, v in b.w.items():
                if s != e or s.startswith("dq_"):
                    self.wait(we, (s, v))
            for s, v in b.r.items():
                if s != e or s.startswith("dq_"):
                    self.wait(we, (s, v))
        for t in deps:
            if t is not None:
                self.wait(we, t)

    def note(self, tok, reads=(), writes=()):
        s, v = tok
        for b in reads:
            b.r[s] = v
        for b in writes:
            if b.r:
                b.w = {s: v}
                b.r = {}
            else:
                b.w[s] = v

    def bump(self, e, ins):
        self.cnt[e] += 1
        ins.then_inc(self.sem[e], self.mult(e))
        return (e, self.cnt[e])

    def op(self, e, fn, reads=(), writes=(), deps=()):
        self.sync(e, reads, writes, deps)
        ins = fn(self.E[e])
        tok = self.bump(e, ins)
        self.note(tok, reads, writes)
        return tok

    def dma(self, q, sbuf_buf, out, in_, reads=(), writes=(), deps=()):
        dsem = "dq_" + sbuf_buf.name
        if dsem not in self.sem:
            self.new_dma_sem(sbuf_buf.name)
        self.sync(dsem, reads, writes, deps, issuer=q)
        ins = self.E[q].dma_start(out=out, in_=in_)
        tok = self.bump(dsem, ins)
        self.note(tok, reads, writes)
        return tok

    def drain_dma(self, engines=("sp", "act")):
        for s, v in self.cnt.items():
            if s.startswith("dq_") and v > 0:
                for e in engines:
                    self.wait(e, (s, v))

    def barrier(self):
        toks = [(s, v) for s, v in self.cnt.items() if v > 0]
        for e in ["pe", "act", "dve", "pool", "sp"]:
            for t in toks:
                if t[0] != e:
                    self.wait(e, t)


def _sb(nc, es, name, shape, dt):
    return Buf(es.enter_context(nc.sbuf_tensor(name, shape, dt)), name)


STOP = [None]
PHASES = set("ABCDE")


class _Stop(Exception):
    pass


def _chk(tag):
    if STOP[0] == tag:
        raise _Stop()


def build(debug=False):
    nc = bass.Bass("TRN2", target_bir_lowering=False)
    dk = "ExternalOutput" if debug else "Internal"

    x_all = nc.dram_tensor("x_all", [SEQ, D], F32, kind="ExternalInput").ap()
    w_in = nc.dram_tensor("w_in", [D, INW], F32, kind="ExternalInput").ap()
    g_mix = nc.dram_tensor("g_mix", [128, 8], F32, kind="ExternalInput").ap()
    pos_all = nc.dram_tensor("pos_all", [SEQ], F32, kind="ExternalInput").ap()
    cst = nc.dram_tensor("cst", [128, 260], F32, kind="ExternalInput").ap()

    KT = nc.dram_tensor("KT", [NH, HD, SEQ], BF16, kind=dk).ap()
    QT = nc.dram_tensor("QT", [NH, HD, HALF], BF16, kind=dk).ap()
    Vs = nc.dram_tensor("Vs", [SEQ, D], BF16, kind=dk).ap()
    UT = nc.dram_tensor("UT", [SSMW, 16, SEQ // 16], BF16, kind=dk).ap()
    Gs = nc.dram_tensor("Gs", [HALF, 2 * D], BF16, kind=dk).ap()
    Os = nc.dram_tensor("Os", [HALF, D], BF16, kind=dk).ap()
    vbq_d = nc.dram_tensor("vbq", [128, 32, 32], F32, kind="ExternalInput").ap()
    SEL = nc.dram_tensor("SEL", [NH, 16, 32 * 256], BF16, kind="Internal").ap()
    tri_d = nc.dram_tensor("tri", [128, 128], F32, kind="ExternalInput").ap()
    w_glu = nc.dram_tensor("w_glu", [SSMW, 2 * D], F32, kind="ExternalInput").ap()
    w_out = nc.dram_tensor("w_out", [D, D], F32, kind="ExternalInput").ap()
    w_up = nc.dram_tensor("w_up", [D, DFF], F32, kind="ExternalInput").ap()
    w_down = nc.dram_tensor("w_down", [DFF, D], F32, kind="ExternalInput").ap()
    g_mlp = nc.dram_tensor("g_mlp", [128, 8], F32, kind="ExternalInput").ap()
    g_fin = nc.dram_tensor("g_fin", [D], F32, kind="ExternalInput").ap()
    YT = nc.dram_tensor("YT", [SSMW, HALF], BF16, kind=dk).ap()
    WB = {
        "glu": nc.dram_tensor("WB_glu", [128, 4 * 2 * D], BF16, kind="Internal").ap(),
        "out": nc.dram_tensor("WB_out", [128, 8 * D], BF16, kind="Internal").ap(),
        "up": nc.dram_tensor("WB_up", [128, 8 * DFF], BF16, kind="Internal").ap(),
        "down": nc.dram_tensor("WB_down", [128, 32 * D], BF16, kind="Internal").ap(),
    }
    WSRC = {"glu": (w_glu, 4, 2 * D), "out": (w_out, 8, D), "up": (w_up, 8, DFF), "down": (w_down, 32, D)}
    X1 = nc.dram_tensor("X1", [HALF, D], F32, kind=dk).ap()
    out = nc.dram_tensor("out", [HALF, D], F32, kind="ExternalOutput").ap()
    s5c_d = nc.dram_tensor("s5c", [128, S5C_COLS], F32, kind="ExternalInput").ap()

    with ExitStack() as es:
        kb = KB(nc, es)
        ps = [Buf(es.enter_context(nc.psum_tensor("ps%d" % i, [128, 512], F32)), "ps%d" % i, psum=True)
              for i in range(8)]
        if "A" in PHASES:
            try:
                phase_a(nc, kb, ps, x_all, w_in, g_mix, pos_all, cst, KT, QT, Vs, UT, Gs)
            except _Stop:
                pass
            kb.barrier()
        if "B" in PHASES:
            phase_b(nc, kb, ps, KT, QT, Vs, Os, cst, vbq_d, SEL, tri_d, WB, WSRC)
            kb.barrier()
        if "C" in PHASES:
            phase_c(nc, kb, ps, UT, YT, cst, s5c_d)
            kb.barrier()
        with ExitStack() as es2:
            wu = _sb(nc, es2, "wu", [128, 8, DFF], BF16)
            wd = _sb(nc, es2, "wd", [128, 32, D], BF16)

            def mlp_w_gen():
                for c0 in range(0, 8, 2):
                    kb.dma("sp", wu, wu[:, c0:c0 + 2, :].rearrange("p c n -> p (c n)"),
                           WB["up"][:, c0 * DFF:(c0 + 2) * DFF], writes=[wu])
                    yield
                for c0 in range(0, 32, 8):
                    kb.dma("sp", wd, wd[:, c0:c0 + 8, :].rearrange("p c n -> p (c n)"),
                           WB["down"][:, c0 * D:(c0 + 8) * D], writes=[wd])
                    yield
            wgen = mlp_w_gen()

            def load_mlp_w(n=100):
                for _ in range(n):
                    try:
                        next(wgen)
                    except StopIteration:
                        return
            loaded = False
            if "D" in PHASES:
                phase_d1(nc, kb, ps, x_all, WB, cst, YT, Os, Gs, X1, load_mlp_w)
                loaded = True
                kb.barrier()
            if "E" in PHASES:
                if not loaded:
                    load_mlp_w()
                phase_d2(nc, kb, ps, wu, wd, g_mlp, g_fin, cst, X1, out)
                kb.barrier()
    return nc


def phase_a(nc, kb, ps, x_all, w_in, g_mix, pos_all, cst, KT, QT, Vs, UT, Gs):
    with ExitStack() as es:
        sb = lambda name, shape, dt: _sb(nc, es, name, shape, dt)
        wb = sb("wb", [128, 8, INW], BF16)
        stg = [sb("stg%d" % i, [128, 8, 128], F32) for i in range(2)]
        cstt = sb("cstt", [128, 260], F32)
        identb = sb("identb", [128, 128], BF16)
        swapb = sb("swapb", [128, 128], BF16)
        gcol = sb("gcol", [128, 8], F32)
        invf = sb("invf", [128, 1], F32)
        xt = [sb("xt%d" % i, [128, D], F32) for i in range(4)]
        junk = sb("junk", [128, D], BF16)
        ss = [sb("ss%d" % i, [128, 4], F32) for i in range(2)]
        rstd = [sb("rstd%d" % i, [128, 4], F32) for i in range(2)]
        hb = [sb("hb%d" % i, [128, 4, D], BF16) for i in range(1)]
        hT = [sb("hT%d" % i, [128, 8, TT], BF16) for i in range(2)]
        posb = [sb("posb%d" % i, [128, TT], F32) for i in range(1)]
        ang = sb("ang", [128, TT], F32)
        kf = sb("kf", [128, TT], F32)
        ki = sb("ki", [128, TT], I32)
        rr = sb("rr", [128, TT], F32)
        rc = sb("rc", [128, TT], F32)
        mm = sb("mm", [128, TT], F32)
        cosT = [sb("cosT%d" % i, [128, TT], F32) for i in range(1)]
        sinT = [sb("sinT%d" % i, [128, TT], F32) for i in range(1)]
        t1 = [sb("t1_%d" % i, [128, TT], F32) for i in range(3)]
        t2 = [sb("t2_%d" % i, [128, TT], F32) for i in range(3)]
        qraw = [sb("qraw%d" % i, [128, TT], BF16) for i in range(3)]
        kqst = [sb("kqst%d" % i, [128, NH, TT], BF16) for i in range(2)]
        vst = [sb("vst%d" % i, [128, D], BF16) for i in range(2)]
        ust = [sb("ust%d" % i, [128, 4, TT], BF16) for i in range(1)]
        gst = [sb("gst%d" % i, [128, 2 * D], BF16) for i in range(2)]

        kb.dma("sp", cstt, cstt[:], cst[:, :], writes=[cstt])
        kb.dma("sp", gcol, gcol[:], g_mix[:, :], writes=[gcol])
        kb.op("dve", lambda e: e.tensor_copy(out=identb[:], in_=cstt[:, 0:128]), reads=[cstt], writes=[identb])
        kb.op("dve", lambda e: e.tensor_copy(out=swapb[:], in_=cstt[:, 128:256]), reads=[cstt], writes=[swapb])
        kb.op("act", lambda e: e.activation(out=invf[:], in_=cstt[:, 256:257], func=AF.Exp,
                                            scale=-math.log(10000.0) / 64.0), reads=[cstt], writes=[invf])

        wv = w_in.rearrange("(c p) n -> p c n", p=128)
        wbq = Buf(wb.t, "wbq")

        def wsel(col0):
            return wb if 1024 <= col0 < 3584 else wbq
        wci = [0]

        def conv_w(n0):
            ci = wci[0]
            wci[0] += 1
            st = stg[ci % 2]
            kb.dma("sp", st, st[:], wv[:, :, n0:n0 + 128], writes=[st])
            eng = ["dve", "pool"][ci % 2]
            kb.op(eng, lambda e: e.tensor_copy(out=wb[:, :, n0:n0 + 128], in_=st[:]), reads=[st], writes=[wsel(n0)])
        for n0 in range(1024, 3584, 128):
            conv_w(n0)
        late_cols = list(range(0, 1024, 128)) + list(range(3584, INW, 128))

        _chk('w')
        QC, KC, VC, UC, GC = 0, 1024, 2048, 3072, 3584
        evac_rr = [0]

        def evac_engine():
            evac_rr[0] += 1
            return ["act", "dve"][evac_rr[0] % 2]

        psrot = [0]

        def next_ps():
            psrot[0] = (psrot[0] + 1) % 6
            return ps[2 + psrot[0]]

        def evac_copy(dst_buf, dst_ap, pbuf):
            eng = evac_engine()
            if eng == "act":
                kb.op("act", lambda e: e.activation(out=dst_ap, in_=pbuf[:], func=AF.Copy),
                      reads=[pbuf], writes=[dst_buf])
            else:
                kb.op("dve", lambda e: e.tensor_copy(out=dst_ap, in_=pbuf[:]), reads=[pbuf], writes=[dst_buf])

        ntiles = SEQ // TT
        xcount = [0]
        kq_i = [0]
        rot_i = [0]
        v_i = [0]
        g_i = [0]
        hbt = hb[0]
        pb_ = posb[0]

        def norm_tile(it):
            p2 = it % 2
            tok0 = it * TT
            for s in range(4):
                xs = xt[xcount[0] % 4]
                xcount[0] += 1
                kb.dma("sp", xs, xs[:], x_all[tok0 + s * 128:tok0 + (s + 1) * 128, :], writes=[xs])
                kb.op("act", lambda e: e.activation(out=junk[:], in_=xs[:], func=AF.Square,
                                                    accum_out=ss[p2][:, s:s + 1]),
                      reads=[xs], writes=[junk, ss[p2]])
                kb.op("dve", lambda e: e.tensor_scalar(out=rstd[p2][:, s:s + 1], in0=ss[p2][:, s:s + 1],
                                                       scalar1=1.0 / D, scalar2=EPS,
                                                       op0=ALU.mult, op1=ALU.add), reads=[ss[p2]], writes=[rstd[p2]])
                kb.op("act", lambda e: e.activation(out=rstd[p2][:, s:s + 1], in_=rstd[p2][:, s:s + 1], func=AF.Sqrt),
                      reads=[rstd[p2]], writes=[rstd[p2]])
                kb.op("dve", lambda e: e.reciprocal(out=rstd[p2][:, s:s + 1], in_=rstd[p2][:, s:s + 1]),
                      reads=[rstd[p2]], writes=[rstd[p2]])
                if s % 2 == 0:
                    kb.op("dve", lambda e: e.tensor_scalar(out=hbt[:, s, :], in0=xs[:],
                                                           scalar1=rstd[p2][:, s:s + 1], scalar2=None, op0=ALU.mult),
                          reads=[xs, rstd[p2]], writes=[hbt])
                else:
                    kb.op("act", lambda e: e.activation(out=hbt[:, s, :], in_=xs[:], func=AF.Copy,
                                                        scale=rstd[p2][:, s:s + 1]),
                          reads=[xs, rstd[p2]], writes=[hbt])

        def rope_tables(it):
            tok0 = it * TT
            kb.dma("sp", pb_, pb_[:], pos_all[tok0:tok0 + TT].partition_broadcast(128), writes=[pb_])
            kb.op("dve", lambda e: e.tensor_scalar(out=ang[:], in0=pb_[:], scalar1=invf[:, 0:1], scalar2=None,
                                                   op0=ALU.mult), reads=[pb_, invf], writes=[ang])
            kb.op("dve", lambda e: e.tensor_scalar(out=kf[:], in0=ang[:], scalar1=1.0 / TWO_PI, scalar2=None,
                                                   op0=ALU.mult), reads=[ang], writes=[kf])
            kb.op("dve", lambda e: e.tensor_copy(out=ki[:], in_=kf[:]), reads=[kf], writes=[ki])
            kb.op("dve", lambda e: e.tensor_copy(out=kf[:], in_=ki[:]), reads=[ki], writes=[kf])
            kb.op("dve", lambda e: e.scalar_tensor_tensor(out=rr[:], in0=kf[:], scalar=-TWO_PI, in1=ang[:],
                                                          op0=ALU.mult, op1=ALU.add), reads=[kf, ang], writes=[rr])

            def wrap(dst, src):
                kb.op("dve", lambda e: e.tensor_scalar(out=mm[:], in0=src[:], scalar1=PI, scalar2=-TWO_PI,
                                                       op0=ALU.is_gt, op1=ALU.mult), reads=[src], writes=[mm])
                kb.op("dve", lambda e: e.tensor_tensor(out=dst[:], in0=src[:], in1=mm[:], op=ALU.add),
                      reads=[src, mm], writes=[dst])
                kb.op("dve", lambda e: e.tensor_scalar(out=mm[:], in0=dst[:], scalar1=-PI, scalar2=TWO_PI,
                                                       op0=ALU.is_lt, op1=ALU.mult), reads=[dst], writes=[mm])
                kb.op("dve", lambda e: e.tensor_tensor(out=dst[:], in0=dst[:], in1=mm[:], op=ALU.add),
                      reads=[dst, mm], writes=[dst])
                kb.op("dve", lambda e: e.tensor_scalar(out=dst[:], in0=dst[:], scalar1=3.14159, scalar2=-3.14159,
                                                       op0=ALU.min, op1=ALU.max), reads=[dst], writes=[dst])

            wrap(rr, rr)
            kb.op("dve", lambda e: e.tensor_scalar(out=rc[:], in0=rr[:], scalar1=PI / 2, scalar2=None, op0=ALU.add),
                  reads=[rr], writes=[rc])
            wrap(rc, rc)
            kb.op("act", lambda e: e.activation(out=sinT[0][:], in_=rr[:], func=AF.Sin, scale=cstt[:, 257:258]),
                  reads=[rr, cstt], writes=[sinT[0]])
            kb.op("act", lambda e: e.activation(out=cosT[0][:], in_=rc[:], func=AF.Sin),
                  reads=[rc], writes=[cosT[0]])

        def transposes(it):
            p2 = it % 2
            for c in range(8):
                pt = ps[c % 2]
                ptb = pt[:].bitcast(BF16)
                kb.sync("pe", reads=[hbt, identb], writes=[pt])
                ins = None
                for s in range(4):
                    ins = nc.tensor.transpose(out=ptb[:, s * 128:(s + 1) * 128],
                                              in_=hbt[:, s, c * 128:(c + 1) * 128], identity=identb[:])
                tok = kb.bump("pe", ins)
                kb.note(tok, reads=[hbt, identb], writes=[pt])
                eng = evac_engine()
                if eng == "act":
                    kb.op("act", lambda e: e.activation(out=hT[p2][:, c, :], in_=ptb[:, 0:TT], func=AF.Copy,
                                                        scale=gcol[:, c:c + 1]),
                          reads=[pt, gcol], writes=[hT[p2]])
                else:
                    kb.op("dve", lambda e: e.tensor_scalar(out=hT[p2][:, c, :], in0=ptb[:, 0:TT],
                                                           scalar1=gcol[:, c:c + 1], scalar2=None, op0=ALU.mult),
                          reads=[pt, gcol], writes=[hT[p2]])

        def fm_group(p2, col0, pbuf):
            wb_ = wsel(col0)
            kb.sync("pe", reads=[wb_, hT[p2]], writes=[pbuf])
            ins = None
            for c in range(8):
                ins = nc.tensor.matmul(pbuf[:], lhsT=wb[:, c, col0:col0 + 128], rhs=hT[p2][:, c, :],
                                       start=(c == 0), stop=(c == 7))
            tok = kb.bump("pe", ins)
            kb.note(tok, reads=[wb_, hT[p2]], writes=[pbuf])

        def tm_group(p2, s, col0, pbuf):
            wb_ = wsel(col0)
            kb.sync("pe", reads=[wb_, hT[p2]], writes=[pbuf])
            ins = None
            for c in range(8):
                ins = nc.tensor.matmul(pbuf[:], lhsT=hT[p2][:, c, s * 128:(s + 1) * 128],
                                       rhs=wb[:, c, col0:col0 + 512], start=(c == 0), stop=(c == 7))
            tok = kb.bump("pe", ins)
            kb.note(tok, reads=[wb_, hT[p2]], writes=[pbuf])

        def rope_heads(p2, col_base, dst):
            pend = None

            def finish(pa, j, h):
                pb = next_ps()
                kb.sync("pe", reads=[qraw[j], swapb], writes=[pb])
                ins = nc.tensor.matmul(pb[:], lhsT=swapb[:], rhs=qraw[j][:], start=True, stop=True)
                tok = kb.bump("pe", ins)
                kb.note(tok, reads=[qraw[j], swapb], writes=[pb])
                kb.op("dve", lambda e: e.tensor_tensor(out=t1[j][:], in0=pa[:], in1=cosT[0][:], op=ALU.mult),
                      reads=[pa, cosT[0]], writes=[t1[j]])
                kb.op("dve", lambda e: e.tensor_tensor(out=t2[j][:], in0=pb[:], in1=sinT[0][:], op=ALU.mult),
                      reads=[pb, sinT[0]], writes=[t2[j]])
                kb.op("pool", lambda e: e.tensor_tensor(out=dst[:, h, :], in0=t1[j][:], in1=t2[j][:], op=ALU.add),
                      reads=[t1[j], t2[j]], writes=[dst])

            pend = []
            for h in range(NH):
                pa = next_ps()
                fm_group(p2, col_base + h * 128, pa)
                j = rot_i[0] % 3
                rot_i[0] += 1
                kb.op("act", lambda e: e.activation(out=qraw[j][:], in_=pa[:], func=AF.Copy),
                      reads=[pa], writes=[qraw[j]])
                pend.append((pa, j, h))
                if len(pend) > 2:
                    finish(*pend.pop(0))
            while pend:
                finish(*pend.pop(0))

        def proj_k(it):
            p2 = it % 2
            tok0 = it * TT
            kst = kqst[kq_i[0] % 2]
            kq_i[0] += 1
            rope_heads(p2, KC, kst)
            kb.dma("act", kst, KT[:, :, tok0:tok0 + TT].rearrange("h d t -> d h t"), kst[:], reads=[kst])

        def proj_rest(it):
            own = it >= ntiles // 2
            p2 = it % 2
            tok0 = it * TT
            for s in range(4):
                vb = vst[v_i[0] % 2]
                v_i[0] += 1
                for hf in range(2):
                    pbuf = next_ps()
                    tm_group(p2, s, VC + hf * 512, pbuf)
                    evac_copy(vb, vb[:, hf * 512:(hf + 1) * 512], pbuf)
                kb.dma("act", vb, Vs[tok0 + s * 128:tok0 + (s + 1) * 128, :], vb[:], reads=[vb])
            j0 = tok0 // 16
            for m in range(4):
                pbuf = next_ps()
                fm_group(p2, UC + m * 128, pbuf)
                src = pbuf[:].rearrange("p (j s) -> p s j", s=16)
                dstv = ust[0][:, m, :].rearrange("p (s j) -> p s j", s=16)
                if evac_engine() == "act":
                    kb.op("act", lambda e: e.activation(out=dstv, in_=src, func=AF.Copy), reads=[pbuf], writes=[ust[0]])
                else:
                    kb.op("dve", lambda e: e.tensor_copy(out=dstv, in_=src), reads=[pbuf], writes=[ust[0]])
            for m in range(4):
                usrc = ust[0][:, m, :].rearrange("p (s j) -> p s j", s=16)
                for s0 in (0, 8):
                    kb.dma("act", ust[0], UT[m * 128:(m + 1) * 128, s0:s0 + 8, j0:j0 + TT // 16],
                           usrc[:, s0:s0 + 8, :], reads=[ust[0]])
            if own:
                o0 = tok0 - HALF
                qst = kqst[kq_i[0] % 2]
                kq_i[0] += 1
                rope_heads(p2, QC, qst)
                kb.dma("act", qst, QT[:, :, o0:o0 + TT].rearrange("h d t -> d h t"), qst[:], reads=[qst])
                for s in range(4):
                    gb = gst[g_i[0] % 2]
                    g_i[0] += 1
                    for cb in range(4):
                        pbuf = next_ps()
                        tm_group(p2, s, GC + cb * 512, pbuf)
                        kb.op("act", lambda e: e.activation(out=gb[:, cb * 512:(cb + 1) * 512], in_=pbuf[:],
                                                            func=AF.Sigmoid), reads=[pbuf], writes=[gb])
                    kb.dma("act", gb, Gs[o0 + s * 128:o0 + (s + 1) * 128, :], gb[:], reads=[gb])

        norm_tile(0)
        transposes(0)
        for it in range(ntiles):
            rope_tables(it)
            if it + 1 < ntiles:
                norm_tile(it + 1)
            proj_k(it)
            if it + 1 < ntiles:
                transposes(it + 1)
            for _ in range(3):
                if late_cols:
                    conv_w(late_cols.pop(0))
            proj_rest(it)
        kb.drain_dma()


def phase_b(nc, kb, ps, KT, QT, Vs, Os, cst, vbq_d, SEL, tri_d, WB, WSRC):
    SCALE = 1.0 / math.sqrt(HD)
    with ExitStack() as es:
        sb = lambda name, shape, dt: _sb(nc, es, name, shape, dt)
        kth = [sb("kth%d" % i, [128, SEQ], BF16) for i in range(2)]
        vh = [sb("vh%d" % i, [128, 64, 129], BF16) for i in range(2)]
        qth = [sb("qth%d" % i, [128, HALF], BF16) for i in range(2)]
        cstt = sb("b_cstt", [128, 260], F32)
        identb = sb("b_identb", [128, 128], BF16)
        vbq = sb("vbq_sb", [128, 32, 32], F32)
        selr = [sb("selr%d" % i, [128, 32, 256], BF16) for i in range(2)]
        seld = [Buf(None, "seld%d" % i) for i in range(2)]
        trif = sb("trif", [128, 128], F32)
        trib = sb("trib", [128, 128], BF16)
        kmf = sb("kmf", [128, 32], F32)
        kmb = sb("kmb", [128, 32], BF16)
        gv = sb("gv", [128, 32, 32], F32)
        mx = sb("mx", [128, 32, 8], F32)
        thr = sb("thr", [128, 32], F32)
        biasb = sb("biasb", [128, 32, 32], BF16)
        biasT = sb("biasT", [32, HALF], BF16)
        ptsb = [sb("ptsb%d" % i, [128, 512], BF16) for i in range(14)]
        pst_bufs = [ps[0], ps[1], ps[7]]
        rec = sb("rec", [128, 2], F32)
        ost = [sb("ost%d" % i, [128, 32, 128], BF16) for i in range(2)]

        kb.dma("sp", cstt, cstt[:], cst[:, :], writes=[cstt])
        kb.dma("sp", vbq, vbq[:], vbq_d[:, :, :], writes=[vbq])
        kb.dma("sp", trif, trif[:], tri_d[:, :], writes=[trif])
        kb.op("dve", lambda e: e.tensor_copy(out=identb[:], in_=cstt[:, 0:128]), reads=[cstt], writes=[identb])
        kb.op("dve", lambda e: e.tensor_copy(out=trib[:], in_=trif[:]), reads=[trif], writes=[trib])
        for i in range(2):
            kb.op("pool", lambda e: e.memset(vh[i][:, :, 128:129], 1.0), writes=[vh[i]])

        def load_head(h):
            p = h % 2
            kb.dma("sp", kth[p], kth[p][:], KT[h, :, :], writes=[kth[p]])
            kb.dma("sp", qth[p], qth[p][:], QT[h, :, :], writes=[qth[p]])
            vsrc = Vs[:, h * 128:(h + 1) * 128].rearrange("(t p) c -> p t c", p=128)
            for t0_ in range(0, 64, 8):
                kb.dma("sp", vh[p], vh[p][:, t0_:t0_ + 8, 0:128], vsrc[:, t0_:t0_ + 8, :], writes=[vh[p]])

        cstg = [sb("cstg%d" % i, [128, 2, 512], F32) for i in range(2)]
        cbf = [sb("cbf%d" % i, [128, 2, 512], BF16) for i in range(2)]

        def conv_gen():
            ci = 0
            for name in ("glu", "out", "up", "down"):
                wsrc, kc, ncols = WSRC[name]
                wv = wsrc.rearrange("(c p) n -> p c n", p=128)
                dst = WB[name].rearrange("p (c n) -> p c n", n=ncols)
                for k0 in range(0, kc, 2):
                    for n0 in range(0, ncols, 512):
                        st, bf = cstg[ci % 2], cbf[ci % 2]
                        kb.dma("sp", st, st[:], wv[:, k0:k0 + 2, n0:n0 + 512], writes=[st])
                        kb.op("dve", lambda e: e.tensor_copy(out=bf[:], in_=st[:]), reads=[st], writes=[bf])
                        kb.dma("sp", bf, dst[:, k0:k0 + 2, n0:n0 + 512], bf[:], reads=[bf])
                        ci += 1
                        yield
        conv = conv_gen()

        def conv_step(n):
            for _ in range(n):
                try:
                    next(conv)
                except StopIteration:
                    return

        load_head(0)
        unit = [0]
        pcount = [0]
        for h in range(NH):
            p = h % 2
            if h + 1 < NH:
                load_head(h + 1)
            K_, V_, Q_ = kth[p], vh[p], qth[p]
            kb.op("dve", lambda e: e.tensor_reduce(out=kmf[:], in_=K_[:].rearrange("p (n k) -> p n k", k=256),
                                                   axis=AX.X, op=ALU.add), reads=[K_], writes=[kmf])
            kb.op("dve", lambda e: e.tensor_scalar(out=kmb[:], in0=kmf[:], scalar1=1.0 / 256.0, scalar2=None,
                                                   op0=ALU.mult), reads=[kmf], writes=[kmb])
            for bnk in range(2):
                pg = ps[6]
                kb.sync("pe", reads=[Q_, kmb], writes=[pg])
                ins = None
                for j in range(16):
                    qt = bnk * 16 + j
                    ins = nc.tensor.matmul(pg[:, j * 32:(j + 1) * 32], lhsT=Q_[:, qt * 128:(qt + 1) * 128],
                                           rhs=kmb[:, :], start=True, stop=True)
                tok = kb.bump("pe", ins)
                kb.note(tok, reads=[Q_, kmb], writes=[pg])
                kb.op("dve", lambda e: e.tensor_tensor(out=gv[:, bnk * 16:(bnk + 1) * 16, :].rearrange("p a b -> p (a b)"),
                                                       in0=pg[:], in1=vbq[:, bnk * 16:(bnk + 1) * 16, :].rearrange("p a b -> p (a b)"),
                                                       op=ALU.add), reads=[pg, vbq], writes=[gv])
            for qt in range(32):
                kb.op("dve", lambda e: e.max(out=mx[:, qt, :], in_=gv[:, qt, :]), reads=[gv], writes=[mx])
            kb.op("dve", lambda e: e.tensor_scalar(out=thr[:], in0=mx[:, :, 2], scalar1=-1e29, scalar2=None,
                                                   op0=ALU.max), reads=[mx], writes=[thr])
            for qt in range(32):
                kb.op("dve", lambda e: e.tensor_scalar(out=biasb[:, qt, :], in0=gv[:, qt, :], scalar1=thr[:, qt:qt + 1],
                                                       scalar2=1.0, op0=ALU.is_ge, op1=ALU.mult),
                      reads=[gv, thr], writes=[biasb])
            for g in range(4):
                pt = ps[6]
                ptb = pt[:].bitcast(BF16)
                kb.sync("pe", reads=[biasb, identb], writes=[pt])
                ins = None
                for j in range(8):
                    qt = g * 8 + j
                    ins = nc.tensor.transpose(out=ptb[0:32, j * 128:(j + 1) * 128], in_=biasb[:, qt, :],
                                              identity=identb[:])
                tok = kb.bump("pe", ins)
                kb.note(tok, reads=[biasb, identb], writes=[pt])
                kb.op("dve", lambda e: e.tensor_copy(out=biasT[0:32, g * 1024:(g + 1) * 1024], in_=ptb[0:32, 0:1024]),
                      reads=[pt], writes=[biasT])

            sd = seld[h % 2]
            kb.dma("sp", biasT, SEL[h, :, :].rearrange("i (n q) -> n i q", q=256),
                   biasT[:].rearrange("n (i q) -> n i q", q=256), reads=[biasT], writes=[sd])

            def load_sel(i):
                sr = selr[i % 2]
                kb.dma("sp", sr, sr[:].rearrange("p n q -> p (n q)"), SEL[h, i, :].partition_broadcast(128),
                       reads=[sd], writes=[sr])
            load_sel(0)

            O_ = ost[p]
            units = []
            for i in range(16):
                for n in range(16 + i + 1):
                    units.append((i, n, n == 16 + i))
            state = {}

            def stage1(u):
                i, n, diag = units[u]
                q0 = i * 256
                pst = pst_bufs[unit[0] % 3]
                P_ = ptsb[unit[0] % 14]
                unit[0] += 1
                state[u] = P_
                ka = n * 256
                if not diag:
                    if n == 0 and i + 1 < 16:
                        load_sel(i + 1)
                    kb.sync("pe", reads=[K_, Q_], writes=[pst])
                    ins = None
                    for kt in range(2):
                        ins = nc.tensor.matmul(pst[:, kt * 256:(kt + 1) * 256], lhsT=K_[:, ka + kt * 128:ka + (kt + 1) * 128],
                                               rhs=Q_[:, q0:q0 + 256], start=True, stop=True)
                    tok = kb.bump("pe", ins)
                    kb.note(tok, reads=[K_, Q_], writes=[pst])
                    kb.op("act", lambda e: e.activation(out=P_[:], in_=pst[:], func=AF.Exp, scale=SCALE),
                          reads=[pst], writes=[P_])
                    sr = selr[i % 2]
                    kb.op("dve", lambda e: e.tensor_tensor(out=P_[:].rearrange("p (a b) -> p a b", a=2),
                                                           in0=P_[:].rearrange("p (a b) -> p a b", a=2),
                                                           in1=sr[:, n, :].unsqueeze(1).to_broadcast([128, 2, 256]),
                                                           op=ALU.mult), reads=[P_, sr], writes=[P_])
                else:
                    kb.sync("pe", reads=[K_, Q_], writes=[pst])
                    nc.tensor.matmul(pst[:, 0:256], lhsT=K_[:, ka:ka + 128], rhs=Q_[:, q0:q0 + 256],
                                     start=True, stop=True)
                    ins = nc.tensor.matmul(pst[:, 256:384], lhsT=K_[:, ka + 128:ka + 256],
                                           rhs=Q_[:, q0 + 128:q0 + 256], start=True, stop=True)
                    tok = kb.bump("pe", ins)
                    kb.note(tok, reads=[K_, Q_], writes=[pst])
                    kb.op("act", lambda e: e.activation(out=P_[:, 0:384], in_=pst[:, 0:384], func=AF.Exp, scale=SCALE),
                          reads=[pst], writes=[P_])
                    kb.op("pool", lambda e: e.tensor_tensor(out=P_[:, 0:128], in0=P_[:, 0:128], in1=trib[:],
                                                            op=ALU.mult), reads=[P_, trib], writes=[P_])
                    kb.op("pool", lambda e: e.tensor_tensor(out=P_[:, 256:384], in0=P_[:, 256:384], in1=trib[:],
                                                            op=ALU.mult), reads=[P_, trib], writes=[P_])

            def stage2(u):
                i, n, diag = units[u]
                P_ = state.pop(u)
                po = [ps[2 + 2 * (i % 2)], ps[3 + 2 * (i % 2)]]
                if not diag:
                    pv = [(0, 0, 0), (0, 1, 128), (1, 0, 256), (1, 1, 384)]
                else:
                    pv = [(0, 0, 0), (0, 1, 128), (1, 1, 256)]
                kb.sync("pe", reads=[P_, V_], writes=po)
                ins = None
                for idx, (kt, qs, c0) in enumerate(pv):
                    last = diag and ((qs == 0 and idx == 0) or (qs == 1 and idx == 2))
                    first = (n == 0 and kt == 0)
                    ins = nc.tensor.matmul(po[qs][:, 0:129], lhsT=P_[:, c0:c0 + 128], rhs=V_[:, 2 * n + kt, :],
                                           start=first, stop=last)
                tok = kb.bump("pe", ins)
                kb.note(tok, reads=[P_, V_], writes=po)
                if diag:
                    for qs in range(2):
                        kb.op("dve", lambda e: e.reciprocal(out=rec[:, qs:qs + 1], in_=po[qs][:, 128:129]),
                              reads=[po[qs]], writes=[rec])
                        kb.op("dve", lambda e: e.tensor_scalar(out=O_[:, 2 * i + qs, :], in0=po[qs][:, 0:128],
                                                               scalar1=rec[:, qs:qs + 1], scalar2=None, op0=ALU.mult),
                              reads=[po[qs], rec], writes=[O_])

            DEPTH = 13
            nu = len(units)
            for u in range(nu + DEPTH):
                if u < nu:
                    stage1(u)
                if u - DEPTH >= 0:
                    stage2(u - DEPTH)
                if u % 32 == 16:
                    conv_step(1)
            odst = Os[:, h * 128:(h + 1) * 128].rearrange("(t p) c -> p t c", p=128)
            for t0_ in range(0, 32, 8):
                kb.dma("act", O_, odst[:, t0_:t0_ + 8, :], O_[:, t0_:t0_ + 8, :], reads=[O_])
        conv_step(1000)
        kb.drain_dma()


def load_weight_bf16(nc, kb, wsrc, wdst, stg, ncols, kc):
    wv = wsrc.rearrange("(c p) n -> p c n", p=128)
    step = stg[0].t.shape[2]
    kstep = stg[0].t.shape[1]
    ci = 0
    for k0 in range(0, kc, kstep):
        for n0 in range(0, ncols, step):
            st = stg[ci % len(stg)]
            kb.dma("sp", st, st[:], wv[:, k0:k0 + kstep, n0:n0 + step], writes=[st])
            eng = ["dve", "pool"][ci % 2]
            kb.op(eng, lambda e: e.tensor_copy(out=wdst[:, k0:k0 + kstep, n0:n0 + step], in_=st[:]),
                  reads=[st], writes=[wdst])
            ci += 1


def phase_d1(nc, kb, ps, x_all, WB, cst, YT, Os, Gs, X1, load_next_w=None):
    with ExitStack() as es:
        sb = lambda name, shape, dt: _sb(nc, es, name, shape, dt)
        wg = sb("wg", [128, 4, 2 * D], BF16)
        wo = sb("wo", [128, 8, D], BF16)
        cstt = sb("d1_cstt", [128, 260], F32)
        identb = sb("d1_identb", [128, 128], BF16)
        yT = [sb("yT%d" % i, [128, 4, TT], BF16) for i in range(1)]
        xs_ = [sb("d1x%d" % i, [128, D], F32) for i in range(2)]
        oa = [sb("oa%d" % i, [128, D], BF16) for i in range(2)]
        gg = [sb("gg%d" % i, [128, 2 * D], BF16) for i in range(2)]
        sg = [sb("sg%d" % i, [128, 512], F32) for i in range(2)]
        ob = sb("ob", [128, D], F32)
        m1 = sb("m1", [128, D], F32)
        mixed = [sb("mixed%d" % i, [128, D], BF16) for i in range(2)]
        mixT = sb("mixT", [128, 8, 128], BF16)
        x1 = [sb("x1_%d" % i, [128, D], F32) for i in range(1)]

        kb.dma("sp", cstt, cstt[:], cst[:, :], writes=[cstt])
        kb.op("dve", lambda e: e.tensor_copy(out=identb[:], in_=cstt[:, 0:128]), reads=[cstt], writes=[identb])
        kb.dma("sp", wg, wg[:].rearrange("p c n -> p (c n)"), WB["glu"][:, :], writes=[wg])
        kb.dma("sp", wo, wo[:].rearrange("p c n -> p (c n)"), WB["out"][:, :], writes=[wo])

        yts = {}

        def stage_a(idx):
            it, s = idx // 4, idx % 4
            t0 = it * TT
            if s == 0:
                y_ = yT[0]
                kb.dma("sp", y_, y_[:], YT[:, t0:t0 + TT].rearrange("(m p) t -> p m t", p=128), writes=[y_])
                yts[it] = y_
            y_ = yts[it]
            j = idx % 2
            r0 = t0 + s * 128
            kb.dma("sp", xs_[j], xs_[j][:], x_all[HALF + r0:HALF + r0 + 128, :], writes=[xs_[j]])
            kb.dma("sp", oa[j], oa[j][:], Os[r0:r0 + 128, :], writes=[oa[j]])
            kb.dma("sp", gg[j], gg[j][:], Gs[r0:r0 + 128, :], writes=[gg[j]])
            for hb_ in range(2):
                pv = ps[(2 * hb_) % 8]
                pgt = ps[(2 * hb_ + 1) % 8]
                for (pbuf, cb) in ((pv, hb_), (pgt, 2 + hb_)):
                    kb.sync("pe", reads=[y_, wg], writes=[pbuf])
                    ins = None
                    for m in range(4):
                        ins = nc.tensor.matmul(pbuf[:], lhsT=y_[:, m, s * 128:(s + 1) * 128],
                                               rhs=wg[:, m, cb * 512:(cb + 1) * 512], start=(m == 0), stop=(m == 3))
                    tok = kb.bump("pe", ins)
                    kb.note(tok, reads=[y_, wg], writes=[pbuf])
                sgb = sg[hb_]
                kb.op("act", lambda e: e.activation(out=sgb[:], in_=pgt[:], func=AF.Sigmoid),
                      reads=[pgt], writes=[sgb])
                kb.op("dve", lambda e: e.tensor_tensor(out=ob[:, hb_ * 512:(hb_ + 1) * 512], in0=pv[:], in1=sgb[:],
                                                       op=ALU.mult), reads=[pv, sgb], writes=[ob])
            mx_ = mixed[j]
            kb.op("pool", lambda e: e.tensor_tensor(out=m1[:], in0=gg[j][:, 0:D], in1=oa[j][:], op=ALU.mult),
                  reads=[gg[j], oa[j]], writes=[m1])
            kb.op("dve", lambda e: e.tensor_tensor(out=ob[:], in0=ob[:], in1=gg[j][:, D:2 * D], op=ALU.mult),
                  reads=[ob, gg[j]], writes=[ob])
            kb.op("pool", lambda e: e.tensor_tensor(out=mx_[:], in0=m1[:], in1=ob[:], op=ALU.add),
                  reads=[m1, ob], writes=[mx_])

        def stage_b(idx):
            it, s = idx // 4, idx % 4
            t0 = it * TT
            j = idx % 2
            r0 = t0 + s * 128
            mx_ = mixed[j]
            for g in range(2):
                pt = ps[4 + g]
                ptb = pt[:].bitcast(BF16)
                kb.sync("pe", reads=[mx_, identb], writes=[pt])
                ins = None
                for c4 in range(4):
                    c = g * 4 + c4
                    ins = nc.tensor.transpose(out=ptb[:, c4 * 128:(c4 + 1) * 128], in_=mx_[:, c * 128:(c + 1) * 128],
                                              identity=identb[:])
                tok = kb.bump("pe", ins)
                kb.note(tok, reads=[mx_, identb], writes=[pt])
                dst = mixT[:, g * 4:(g + 1) * 4, :].rearrange("p a b -> p (a b)")
                if g == 0:
                    kb.op("act", lambda e: e.activation(out=dst, in_=ptb[:, 0:512], func=AF.Copy),
                          reads=[pt], writes=[mixT])
                else:
                    kb.op("dve", lambda e: e.tensor_copy(out=dst, in_=ptb[:, 0:512]), reads=[pt], writes=[mixT])
            xo = x1[0]
            for hf in range(2):
                pbuf = ps[6 + hf]
                kb.sync("pe", reads=[mixT, wo], writes=[pbuf])
                ins = None
                for c in range(8):
                    ins = nc.tensor.matmul(pbuf[:], lhsT=mixT[:, c, :], rhs=wo[:, c, hf * 512:(hf + 1) * 512],
                                           start=(c == 0), stop=(c == 7))
                tok = kb.bump("pe", ins)
                kb.note(tok, reads=[mixT, wo], writes=[pbuf])
                kb.op("dve", lambda e: e.tensor_tensor(out=xo[:, hf * 512:(hf + 1) * 512], in0=pbuf[:],
                                                       in1=xs_[j][:, hf * 512:(hf + 1) * 512], op=ALU.add),
                      reads=[pbuf, xs_[j]], writes=[xo])
            kb.dma("act", xo, X1[r0:r0 + 128, :], xo[:], reads=[xo])

        nidx = HALF // 128
        stage_a(0)
        for idx in range(nidx):
            if idx + 1 < nidx:
                stage_a(idx + 1)
            if load_next_w is not None and idx % 3 == 2:
                load_next_w(1)
            stage_b(idx)
        if load_next_w is not None:
            load_next_w(100)
        kb.drain_dma()


def phase_d2(nc, kb, ps, wu, wd, g_mlp, g_fin, cst, X1, out):
    with ExitStack() as es:
        sb = lambda name, shape, dt: _sb(nc, es, name, shape, dt)
        cstt = sb("d2_cstt", [128, 260], F32)
        identb = sb("d2_identb", [128, 128], BF16)
        gcol = sb("d2_gcol", [128, 8], F32)
        gfin = sb("gfin", [128, D], F32)
        xt = [sb("d2x%d" % i, [128, D], F32) for i in range(5)]
        junk = sb("d2junk", [128, D], BF16)
        ss = sb("d2ss", [128, 4], F32)
        rstd = sb("d2rstd", [128, 4], F32)
        hbt = sb("d2hb", [128, 4, D], BF16)
        hT = sb("d2hT", [128, 8, TT], BF16)
        rl = [sb("rl%d" % i, [128, TT], BF16) for i in range(2)]
        aT = sb("aT", [128, 32, TT], BF16)
        ss2 = sb("d2ss2", [128, 2], F32)
        rstd2 = sb("d2rstd2", [128, 2], F32)

        kb.dma("sp", cstt, cstt[:], cst[:, :], writes=[cstt])
        kb.dma("sp", gcol, gcol[:], g_mlp[:, :], writes=[gcol])
        kb.dma("sp", gfin, gfin[:], g_fin.partition_broadcast(128), writes=[gfin])
        kb.op("dve", lambda e: e.tensor_copy(out=identb[:], in_=cstt[:, 0:128]), reads=[cstt], writes=[identb])
        xn_i = [0]
        xr_i = [0]

        def norm_tile(it):
            t0 = it * TT
            for s in range(4):
                xs = xt[xn_i[0] % 2]
                xn_i[0] += 1
                kb.dma("sp", xs, xs[:], X1[t0 + s * 128:t0 + (s + 1) * 128, :], writes=[xs])
                kb.op("act", lambda e: e.activation(out=junk[:], in_=xs[:], func=AF.Square, accum_out=ss[:, s:s + 1]),
                      reads=[xs], writes=[junk, ss])
                kb.op("dve", lambda e: e.tensor_scalar(out=rstd[:, s:s + 1], in0=ss[:, s:s + 1], scalar1=1.0 / D,
                                                       scalar2=EPS, op0=ALU.mult, op1=ALU.add), reads=[ss], writes=[rstd])
                kb.op("act", lambda e: e.activation(out=rstd[:, s:s + 1], in_=rstd[:, s:s + 1], func=AF.Sqrt),
                      reads=[rstd], writes=[rstd])
                kb.op("dve", lambda e: e.reciprocal(out=rstd[:, s:s + 1], in_=rstd[:, s:s + 1]),
                      reads=[rstd], writes=[rstd])
                if s % 2 == 0:
                    kb.op("dve", lambda e: e.tensor_scalar(out=hbt[:, s, :], in0=xs[:], scalar1=rstd[:, s:s + 1],
                                                           scalar2=None, op0=ALU.mult), reads=[xs, rstd], writes=[hbt])
                else:
                    kb.op("act", lambda e: e.activation(out=hbt[:, s, :], in_=xs[:], func=AF.Copy,
                                                        scale=rstd[:, s:s + 1]), reads=[xs, rstd], writes=[hbt])

        def trans_tile(it):
            for c in range(8):
                pt = ps[c % 2]
                ptb = pt[:].bitcast(BF16)
                kb.sync("pe", reads=[hbt, identb], writes=[pt])
                ins = None
                for s in range(4):
                    ins = nc.tensor.transpose(out=ptb[:, s * 128:(s + 1) * 128], in_=hbt[:, s, c * 128:(c + 1) * 128],
                                              identity=identb[:])
                tok = kb.bump("pe", ins)
                kb.note(tok, reads=[hbt, identb], writes=[pt])
                if c % 2 == 0:
                    kb.op("act", lambda e: e.activation(out=hT[:, c, :], in_=ptb[:, 0:TT], func=AF.Copy,
                                                        scale=gcol[:, c:c + 1]), reads=[pt, gcol], writes=[hT])
                else:
                    kb.op("dve", lambda e: e.tensor_scalar(out=hT[:, c, :], in0=ptb[:, 0:TT], scalar1=gcol[:, c:c + 1],
                                                           scalar2=None, op0=ALU.mult), reads=[pt, gcol], writes=[hT])

        def up_tile(it):
            for f in range(32):
                pbuf = ps[2 + f % 3]
                kb.sync("pe", reads=[wu, hT], writes=[pbuf])
                ins = None
                for c in range(8):
                    ins = nc.tensor.matmul(pbuf[:], lhsT=wu[:, c, f * 128:(f + 1) * 128], rhs=hT[:, c, :],
                                           start=(c == 0), stop=(c == 7))
                tok = kb.bump("pe", ins)
                kb.note(tok, reads=[wu, hT], writes=[pbuf])
                r_ = rl[f % 2]
                kb.op("act", lambda e: e.activation(out=r_[:], in_=pbuf[:], func=AF.Relu), reads=[pbuf], writes=[r_])
                eng = ["pool", "dve", "dve"][f % 3]
                kb.op(eng, lambda e: e.tensor_tensor(out=aT[:, f, :], in0=r_[:], in1=r_[:], op=ALU.mult),
                      reads=[r_], writes=[aT])

        def down_tile(it):
            t0 = it * TT
            for s in range(4):
                xo = xt[2 + xr_i[0] % 3]
                xr_i[0] += 1
                kb.dma("sp", xo, xo[:], X1[t0 + s * 128:t0 + (s + 1) * 128, :], writes=[xo])
                for hf in range(2):
                    pbuf = ps[5 + (2 * s + hf) % 3]
                    kb.sync("pe", reads=[aT, wd], writes=[pbuf])
                    ins = None
                    for f in range(32):
                        ins = nc.tensor.matmul(pbuf[:], lhsT=aT[:, f, s * 128:(s + 1) * 128],
                                               rhs=wd[:, f, hf * 512:(hf + 1) * 512], start=(f == 0), stop=(f == 31))
                    tok = kb.bump("pe", ins)
                    kb.note(tok, reads=[aT, wd], writes=[pbuf])
                    kb.op("dve", lambda e: e.tensor_tensor(out=xo[:, hf * 512:(hf + 1) * 512], in0=pbuf[:],
                                                           in1=xo[:, hf * 512:(hf + 1) * 512], op=ALU.add),
                          reads=[pbuf, xo], writes=[xo])
                k2 = s % 2
                kb.op("act", lambda e: e.activation(out=junk[:], in_=xo[:], func=AF.Square, accum_out=ss2[:, k2:k2 + 1]),
                      reads=[xo], writes=[junk, ss2])
                kb.op("dve", lambda e: e.tensor_scalar(out=rstd2[:, k2:k2 + 1], in0=ss2[:, k2:k2 + 1], scalar1=1.0 / D,
                                                       scalar2=EPS, op0=ALU.mult, op1=ALU.add), reads=[ss2], writes=[rstd2])
                kb.op("act", lambda e: e.activation(out=rstd2[:, k2:k2 + 1], in_=rstd2[:, k2:k2 + 1], func=AF.Sqrt),
                      reads=[rstd2], writes=[rstd2])
                kb.op("dve", lambda e: e.reciprocal(out=rstd2[:, k2:k2 + 1], in_=rstd2[:, k2:k2 + 1]),
                      reads=[rstd2], writes=[rstd2])
                kb.op("dve", lambda e: e.scalar_tensor_tensor(out=xo[:], in0=xo[:], scalar=rstd2[:, k2:k2 + 1],
                                                              in1=gfin[:], op0=ALU.mult, op1=ALU.mult),
                      reads=[xo, rstd2, gfin], writes=[xo])
                kb.dma("act", xo, out[t0 + s * 128:t0 + (s + 1) * 128, :], xo[:], reads=[xo])

        ntl = HALF // TT
        norm_tile(0)
        trans_tile(0)
        for it in range(ntl):
            up_tile(it)
            if it + 1 < ntl:
                norm_tile(it + 1)
                trans_tile(it + 1)
            down_tile(it)
        kb.drain_dma()


S5C_COLS = 32 * 4 + 512 * 6


def phase_c(nc, kb, ps, UT, YT, cst, s5c_d):
    NJ = SEQ // 16
    NO = HALF // 16
    with ExitStack() as es:
        sb = lambda name, shape, dt: _sb(nc, es, name, shape, dt)
        cstt = sb("c_cstt", [128, 260], F32)
        identb = sb("c_identb", [128, 128], BF16)
        s5c = sb("s5c_sb", [128, S5C_COLS], F32)
        dt_ = sb("c_dt", [128, 32], F32)
        lrdt = sb("c_lrdt", [128, 32], F32)
        lidt = sb("c_lidt", [128, 32], F32)
        mv = sb("c_mv", [128, 32, 16], F32)
        tA = [sb("c_tA%d" % i, [128, 32, 16], F32) for i in range(6)]
        tI = sb("c_tI", [128, 32, 16], I32)
        PWr = sb("PWr", [128, 32, 17], F32)
        PWi = sb("PWi", [128, 32, 17], F32)
        PVr = sb("PVr", [128, 32, 16], F32)
        PVi = sb("PVi", [128, 32, 16], F32)
        PMr = sb("PMr", [128, 32, 16], F32)
        PMi = sb("PMi", [128, 32, 16], F32)
        Btr = sb("Btr", [128, 32, 16], F32)
        Bti = sb("Bti", [128, 32, 16], F32)
        sm = [sb("c_sm%d" % i, [128, 32], F32) for i in range(8)]
        LVr = sb("LVr", [128, 32, 9], F32)
        LVi = sb("LVi", [128, 32, 9], F32)
        LVs = sb("LVs", [128, 32, 9], F32)
        Sel = sb("Sel", [128, 64, 128], BF16)
        SelR = sb("SelR", [128, 64, 128], BF16)
        NSET = 8
        WT = [sb("WT%d" % i, [128, 16, 16], BF16) for i in range(NSET)]
        Fg = [sb("Fg%d" % i, [128, 16, 16], BF16) for i in range(NSET)]
        Gg = [sb("Gg%d" % i, [128, 16, 16], BF16) for i in range(NSET)]
        Wg = [sb("Wg%d" % i, [128, 2, 128], BF16) for i in range(NSET)]
        M0g = [sb("M0g%d" % i, [128, 2, 256], BF16) for i in range(NSET)]
        ALg = [sb("ALg%d" % i, [128, 9, 128], BF16) for i in range(NSET)]
        ct1 = [sb("ct1_%d" % i, [128, 16, 16], F32) for i in range(2)]
        ct2 = [sb("ct2_%d" % i, [128, 16, 16], F32) for i in range(2)]
        mtmp = sb("mtmp", [128, 512], F32)
        atmp = [sb("atmp%d" % i, [128, 128], F32) for i in range(2)]
        Ug2 = [sb("Ug%d" % i, [128, 2, NJ], BF16) for i in range(8)]
        Ug = list(Ug2[0:4])
        Z = [[sb("Z%d_%d" % (i, j), [128, NJ + 1], BF16) for j in range(2)] for i in range(4)]
        ysb = [sb("ysb%d" % i, [128, NO], F32) for i in range(8)]
        yt_ = [sb("yt%d" % i, [128, NO], F32) for i in range(4)]
        ysg = [sb("ysg%d" % i, [128, NO], F32) for i in range(4)]
        ygel = sb("ygel", [128, 8, 2, NO], BF16)
        YTm = [sb("YTm%d" % i, [128, HALF], BF16) for i in range(1)]

        C_LR, C_LI, C_LS, C_DS = 0, 32, 64, 96
        C_BR, C_BI, C_CR, C_CI, C_MK, C_DI = 128, 640, 1152, 1664, 2176, 2688

        kb.dma("sp", cstt, cstt[:], cst[:, :], writes=[cstt])
        kb.dma("sp", s5c, s5c[:], s5c_d[:, :], writes=[s5c])
        kb.op("dve", lambda e: e.tensor_copy(out=identb[:], in_=cstt[:, 0:128]), reads=[cstt], writes=[identb])
        identf = cstt[:, 0:128]
        swapf = cstt[:, 128:256]

        def V(e, fn, reads, writes):
            return kb.op(e, fn, reads=reads, writes=writes)

        def tt(e, out_b, out_ap, a_b, a_ap, b_b, b_ap, op):
            return kb.op(e, lambda en: en.tensor_tensor(out=out_ap, in0=a_ap, in1=b_ap, op=op),
                         reads=[a_b, b_b], writes=[out_b])

        def bc_g(ap2):
            return ap2.unsqueeze(2).to_broadcast([128, 32, 16])

        V("pool", lambda e: e.memset(Sel[:], 0.0), [], [Sel])
        V("pool", lambda e: e.memset(SelR[:], 0.0), [], [SelR])
        for gl in range(8):
            for s8 in range(8):
                eng = ["dve", "pool"][s8 % 2]
                V(eng, lambda e: e.tensor_copy(out=Sel[:, gl * 8 + s8, s8 * 16:(s8 + 1) * 16],
                                               in_=cstt[:, gl * 16:(gl + 1) * 16]), [cstt, Sel], [Sel])
                V(eng, lambda e: e.tensor_copy(out=SelR[:, gl * 8 + s8, gl * 16:(gl + 1) * 16],
                                               in_=cstt[:, s8 * 16:(s8 + 1) * 16]), [cstt, SelR], [SelR])
        for i in range(4):
            for j in range(2):
                V("pool", lambda e: e.memset(Z[i][j][:, 0:1], 0.0), [], [Z[i][j]])

        V("act", lambda e: e.activation(out=dt_[:], in_=s5c[:, C_LS:C_LS + 32], func=AF.Exp), [s5c], [dt_])
        tt("dve", lrdt, lrdt[:], s5c, s5c[:, C_LR:C_LR + 32], dt_, dt_[:], ALU.mult)
        tt("dve", lidt, lidt[:], s5c, s5c[:, C_LI:C_LI + 32], dt_, dt_[:], ALU.mult)
        for m in range(16):
            V("pool", lambda e: e.memset(mv[:, :, m:m + 1], float(m + 1)), [], [mv])
        ang, kf, rr, rc, mm, mg = tA
        tt("dve", ang, ang[:], mv, mv[:], lidt, bc_g(lidt[:]), ALU.mult)
        tt("dve", mg, mg[:], mv, mv[:], lrdt, bc_g(lrdt[:]), ALU.mult)
        V("act", lambda e: e.activation(out=mg[:], in_=mg[:], func=AF.Exp), [mg], [mg])
        V("dve", lambda e: e.tensor_scalar(out=kf[:], in0=ang[:], scalar1=1.0 / TWO_PI, scalar2=None, op0=ALU.mult),
          [ang], [kf])
        V("dve", lambda e: e.tensor_copy(out=tI[:], in_=kf[:]), [kf], [tI])
        V("dve", lambda e: e.tensor_copy(out=kf[:], in_=tI[:]), [tI], [kf])
        V("dve", lambda e: e.scalar_tensor_tensor(out=rr[:], in0=kf[:], scalar=-TWO_PI, in1=ang[:],
                                                  op0=ALU.mult, op1=ALU.add), [kf, ang], [rr])

        def wrap(dst, src):
            V("dve", lambda e: e.tensor_scalar(out=mm[:], in0=src[:], scalar1=PI, scalar2=-TWO_PI,
                                               op0=ALU.is_gt, op1=ALU.mult), [src], [mm])
            tt("dve", dst, dst[:], src, src[:], mm, mm[:], ALU.add)
            V("dve", lambda e: e.tensor_scalar(out=mm[:], in0=dst[:], scalar1=-PI, scalar2=TWO_PI,
                                               op0=ALU.is_lt, op1=ALU.mult), [dst], [mm])
            tt("dve", dst, dst[:], dst, dst[:], mm, mm[:], ALU.add)
            V("dve", lambda e: e.tensor_scalar(out=dst[:], in0=dst[:], scalar1=3.14159, scalar2=-3.14159,
                                               op0=ALU.min, op1=ALU.max), [dst], [dst])

        wrap(rr, rr)
        V("dve", lambda e: e.tensor_scalar(out=rc[:], in0=rr[:], scalar1=PI / 2, scalar2=None, op0=ALU.add), [rr], [rc])
        wrap(rc, rc)
        V("act", lambda e: e.activation(out=rr[:], in_=rr[:], func=AF.Sin), [rr], [rr])
        V("act", lambda e: e.activation(out=rc[:], in_=rc[:], func=AF.Sin), [rc], [rc])
        V("dve", lambda e: e.memset(PWr[:, :, 0:1], 1.0), [], [PWr])
        V("dve", lambda e: e.memset(PWi[:, :, 0:1], 0.0), [], [PWi])
        tt("dve", PWr, PWr[:, :, 1:17], mg, mg[:], rc, rc[:], ALU.mult)
        tt("dve", PWi, PWi[:, :, 1:17], mg, mg[:], rr, rr[:], ALU.mult)
        for s_ in range(16):
            V("dve", lambda e: e.tensor_copy(out=PVr[:, :, s_:s_ + 1], in_=PWr[:, :, 15 - s_:16 - s_]), [PWr], [PVr])
            V("pool", lambda e: e.tensor_copy(out=PVi[:, :, s_:s_ + 1], in_=PWi[:, :, 15 - s_:16 - s_]), [PWi], [PVi])
        d16, ir, ii, nr, den, cr, ci, tq = sm
        p16r = PWr[:, :, 16]
        p16i = PWi[:, :, 16]
        tt("dve", d16, d16[:], PWr, p16r, PWr, p16r, ALU.mult)
        tt("dve", tq, tq[:], PWi, p16i, PWi, p16i, ALU.mult)
        tt("dve", d16, d16[:], d16, d16[:], tq, tq[:], ALU.add)
        V("dve", lambda e: e.reciprocal(out=d16[:], in_=d16[:]), [d16], [d16])
        tt("dve", ir, ir[:], PWr, p16r, d16, d16[:], ALU.mult)
        tt("dve", ii, ii[:], PWi, p16i, d16, d16[:], ALU.mult)
        V("dve", lambda e: e.tensor_scalar(out=ii[:], in0=ii[:], scalar1=-1.0, scalar2=None, op0=ALU.mult), [ii], [ii])
        x1_, x2_ = tA[0], tA[1]
        tt("dve", x1_, x1_[:], PWr, PWr[:, :, 1:17], ir, bc_g(ir[:]), ALU.mult)
        tt("dve", x2_, x2_[:], PWi, PWi[:, :, 1:17], ii, bc_g(ii[:]), ALU.mult)
        tt("dve", PMr, PMr[:], x1_, x1_[:], x2_, x2_[:], ALU.subtract)
        tt("dve", x1_, x1_[:], PWr, PWr[:, :, 1:17], ii, bc_g(ii[:]), ALU.mult)
        tt("dve", x2_, x2_[:], PWi, PWi[:, :, 1:17], ir, bc_g(ir[:]), ALU.mult)
        tt("dve", PMi, PMi[:], x1_, x1_[:], x2_, x2_[:], ALU.add)
        lamr = s5c[:, C_LR:C_LR + 32]
        lami = s5c[:, C_LI:C_LI + 32]
        V("dve", lambda e: e.tensor_scalar(out=nr[:], in0=PWr[:, :, 1], scalar1=-1.0, scalar2=None, op0=ALU.add),
          [PWr], [nr])
        ni = PWi[:, :, 1]
        tt("dve", den, den[:], s5c, lamr, s5c, lamr, ALU.mult)
        tt("dve", tq, tq[:], s5c, lami, s5c, lami, ALU.mult)
        tt("dve", den, den[:], den, den[:], tq, tq[:], ALU.add)
        V("dve", lambda e: e.reciprocal(out=den[:], in_=den[:]), [den], [den])
        tt("dve", cr, cr[:], nr, nr[:], s5c, lamr, ALU.mult)
        tt("dve", tq, tq[:], PWi, ni, s5c, lami, ALU.mult)
        tt("dve", cr, cr[:], cr, cr[:], tq, tq[:], ALU.add)
        tt("dve", cr, cr[:], cr, cr[:], den, den[:], ALU.mult)
        tt("dve", ci, ci[:], PWi, ni, s5c, lamr, ALU.mult)
        tt("dve", tq, tq[:], nr, nr[:], s5c, lami, ALU.mult)
        tt("dve", ci, ci[:], ci, ci[:], tq, tq[:], ALU.subtract)
        tt("dve", ci, ci[:], ci, ci[:], den, den[:], ALU.mult)
        bre = s5c[:, C_BR:C_BR + 512].rearrange("p (g c) -> p g c", c=16)
        bim = s5c[:, C_BI:C_BI + 512].rearrange("p (g c) -> p g c", c=16)
        tt("dve", x1_, x1_[:], s5c, bre, cr, bc_g(cr[:]), ALU.mult)
        tt("dve", x2_, x2_[:], s5c, bim, ci, bc_g(ci[:]), ALU.mult)
        tt("dve", Btr, Btr[:], x1_, x1_[:], x2_, x2_[:], ALU.subtract)
        tt("dve", x1_, x1_[:], s5c, bim, cr, bc_g(cr[:]), ALU.mult)
        tt("dve", x2_, x2_[:], s5c, bre, ci, bc_g(ci[:]), ALU.mult)
        tt("dve", Bti, Bti[:], x1_, x1_[:], x2_, x2_[:], ALU.add)
        V("dve", lambda e: e.tensor_copy(out=LVr[:, :, 0:1], in_=PWr[:, :, 16:17]), [PWr], [LVr])
        V("dve", lambda e: e.tensor_copy(out=LVi[:, :, 0:1], in_=PWi[:, :, 16:17]), [PWi], [LVi])
        q1, q2 = sm[0], sm[1]
        for l in range(8):
            ar_, ai_ = LVr[:, :, l], LVi[:, :, l]
            tt("dve", q1, q1[:], LVr, ar_, LVr, ar_, ALU.mult)
            tt("dve", q2, q2[:], LVi, ai_, LVi, ai_, ALU.mult)
            tt("dve", LVr, LVr[:, :, l + 1], q1, q1[:], q2, q2[:], ALU.subtract)
            tt("dve", q1, q1[:], LVr, ar_, LVi, ai_, ALU.mult)
            V("dve", lambda e: e.tensor_scalar(out=LVi[:, :, l + 1], in0=q1[:], scalar1=2.0, scalar2=None,
                                               op0=ALU.mult), [q1], [LVi])
        V("dve", lambda e: e.tensor_scalar(out=LVs[:], in0=LVi[:], scalar1=cstt[:, 258:259], scalar2=None,
                                           op0=ALU.mult), [LVi, cstt], [LVs])

        cre = s5c[:, C_CR:C_CR + 512].rearrange("p (g c) -> p g c", c=16)
        cim = s5c[:, C_CI:C_CI + 512].rearrange("p (g c) -> p g c", c=16)
        ncre_b = sb("ncre", [128, 32, 16], F32)
        ncim_b = sb("ncim", [128, 32, 16], F32)
        V("dve", lambda e: e.tensor_scalar(out=ncre_b[:], in0=cre, scalar1=-1.0, scalar2=None, op0=ALU.mult), [s5c], [ncre_b])
        V("dve", lambda e: e.tensor_scalar(out=ncim_b[:], in0=cim, scalar1=-1.0, scalar2=None, op0=ALU.mult), [s5c], [ncim_b])

        def cmat(dst, k, tabr_b, tabr, tabi_b, tabi, mr_b, mr, mi_b, mi, nmr_b=None, nmr=None, nmi_b=None, nmi=None):
            def b_a(ap, lo, hi):
                return ap[lo:hi].unsqueeze(2).to_broadcast([64, 16, 16])

            def b_b(ap, lo, hi):
                return ap[lo:hi].unsqueeze(1).to_broadcast([64, 16, 16])

            t1, t2 = ct1[k], ct2[k]
            tt("dve", t1, t1[0:64], tabr_b, b_a(tabr, 0, 64), mr_b, b_b(mr, 0, 64), ALU.mult)
            tt("dve", t2, t2[0:64], tabi_b, b_a(tabi, 0, 64), mi_b, b_b(mi, 0, 64), ALU.mult)
            tt("dve", dst, dst[0:64], t1, t1[0:64], t2, t2[0:64], ALU.subtract)
            if nmr is not None:
                mr_b, mr, mi_b, mi = nmr_b, nmr, nmi_b, nmi
            tt("pool", t1, t1[64:128], tabr_b, b_a(tabr, 64, 128), mi_b, b_b(mi, 64, 128), ALU.mult)
            tt("pool", t2, t2[64:128], tabi_b, b_a(tabi, 64, 128), mr_b, b_b(mr, 64, 128), ALU.mult)
            tt("pool", dst, dst[64:128], t1, t1[64:128], t2, t2[64:128], ALU.add)

        ev = [0]

        def evac(dst_b, dst_ap, src_b, src_ap):
            ev[0] += 1
            if ev[0] % 2:
                kb.op("act", lambda e: e.activation(out=dst_ap, in_=src_ap, func=AF.Copy), reads=[src_b], writes=[dst_b])
            else:
                kb.op("dve", lambda e: e.tensor_copy(out=dst_ap, in_=src_ap), reads=[src_b], writes=[dst_b])

        def pe(reads, writes, fn):
            kb.sync("pe", reads=reads, writes=writes)
            ins = fn()
            tok = kb.bump("pe", ins)
            kb.note(tok, reads=reads, writes=writes)

        NB = 4
        batches = [(m, half) for m in range(4) for half in range(2)]

        def slots_of(m, half):
            sl = []
            for b in range(NB):
                gl = half * NB + b
                sl.append((b, gl, m * 8 + gl, (half * NB + b) % NSET))
            return sl

        def do_setup(m, half):
            slots = slots_of(m, half)
            for (b, gl, g, k) in slots:
                cmat(WT[k], b % 2, PVr, PVr[:, g, :], PVi, PVi[:, g, :], Btr, Btr[:, g, :], Bti, Bti[:, g, :])
                cmat(Fg[k], b % 2, PWr, PWr[:, g, 1:17], PWi, PWi[:, g, 1:17], s5c, cre[:, g, :], s5c, cim[:, g, :],
                     ncre_b, ncre_b[:, g, :], ncim_b, ncim_b[:, g, :])
                cmat(Gg[k], b % 2, PMr, PMr[:, g, :], PMi, PMi[:, g, :], s5c, cre[:, g, :], s5c, cim[:, g, :],
                     ncre_b, ncre_b[:, g, :], ncim_b, ncim_b[:, g, :])
                wt2 = WT[k][:].rearrange("p a b -> p (a b)")
                gg2 = Gg[k][:].rearrange("p a b -> p (a b)")
                p0 = ps[0]
                p0b = p0[:].bitcast(BF16)

                def f_tr():
                    ins = None
                    for kt in range(2):
                        ins = nc.tensor.transpose(out=p0b[:, kt * 128:(kt + 1) * 128],
                                                  in_=wt2[:, kt * 128:(kt + 1) * 128], identity=identb[:])
                    return ins
                pe([WT[k], identb], [p0], f_tr)
                evac(Wg[k], Wg[k][:].rearrange("p a b -> p (a b)"), p0, p0b[:, 0:256])
                p1 = ps[1]

                def f_m0():
                    ins = None
                    for kt in range(2):
                        ins = nc.tensor.matmul(p1[:, kt * 256:(kt + 1) * 256], lhsT=wt2[:, kt * 128:(kt + 1) * 128],
                                               rhs=gg2, start=True, stop=True)
                    return ins
                pe([WT[k], Gg[k]], [p1], f_m0)
                tt("dve", mtmp, mtmp[:], p1, p1[:], s5c, s5c[:, C_MK:C_MK + 512], ALU.mult)
                kb.op("dve", lambda e: e.scalar_tensor_tensor(out=M0g[k][:].rearrange("p a b -> p (a b)"),
                                                              in0=s5c[:, C_DI:C_DI + 512],
                                                              scalar=s5c[:, C_DS + g:C_DS + g + 1],
                                                              in1=mtmp[:], op0=ALU.mult, op1=ALU.add),
                      reads=[s5c, mtmp], writes=[M0g[k]])
                for l in range(9):
                    at = atmp[l % 2]
                    kb.op("act", lambda e: e.activation(out=at[:], in_=swapf, func=AF.Copy, scale=LVs[:, g, l:l + 1]),
                          reads=[cstt, LVs], writes=[at])
                    kb.op("dve", lambda e: e.scalar_tensor_tensor(out=ALg[k][:, l, :], in0=identf,
                                                                  scalar=LVr[:, g, l:l + 1],
                                                                  in1=at[:], op0=ALU.mult, op1=ALU.add),
                          reads=[cstt, LVr, at], writes=[ALg[k]])

        def do_relayout(m, half, um):
            slots = slots_of(m, half)
            for (b, gl, g, k) in slots:
                for s8 in range(8):
                    kb.dma("sp", Ug[b], Ug[b][s8 * 16:(s8 + 1) * 16, :, :], UT[16 * g:16 * g + 16, s8:16:8, :],
                           writes=[Ug[b]])

        curs = {}

        def do_ds(m, half):
            slots = slots_of(m, half)
            cur = [0] * NB
            for (b, gl, g, k) in slots:
                pz = ps[4 + b]

                def f_ds():
                    ins = None
                    for kt in range(2):
                        ins = nc.tensor.matmul(pz[:], lhsT=Wg[k][:, kt, :], rhs=Ug[b][:, kt, :],
                                               start=(kt == 0), stop=(kt == 1))
                    return ins
                pe([Wg[k], Ug[b]], [pz], f_ds)
                evac(Z[b][0], Z[b][0][:, 1:NJ + 1], pz, pz[:])
            return cur

        def do_ks(m, half, cur):
            slots = slots_of(m, half)
            for l in range(9):
                d = 1 << l
                for (b, gl, g, k) in slots:
                    pz = ps[4 + b]
                    src = Z[b][cur[b]]

                    def f_ks():
                        nc.tensor.matmul(pz[:], lhsT=identb[:], rhs=src[:, 1:NJ + 1], start=True, stop=False)
                        return nc.tensor.matmul(pz[:, d:NJ], lhsT=ALg[k][:, l, :], rhs=src[:, 1:NJ + 1 - d],
                                                start=False, stop=True)
                    pe([identb, ALg[k], src], [pz], f_ks)
                    cur[b] = 1 - cur[b]
                    dstz = Z[b][cur[b]]
                    evac(dstz, dstz[:, 1:NJ + 1], pz, pz[:])
            return cur

        def do_ymm(m, half, cur):
            slots = slots_of(m, half)
            for (b, gl, g, k) in slots:
                zf = Z[b][cur[b]]
                fg2 = Fg[k][:].rearrange("p a b -> p (a b)")
                for mt in range(2):
                    py = ps[4 + b]

                    def f_y():
                        for kt in range(mt + 1):
                            nc.tensor.matmul(py[:, 0:NO], lhsT=M0g[k][:, kt, mt * 128:(mt + 1) * 128],
                                             rhs=Ug[b][:, kt, NO:NJ], start=(kt == 0), stop=False)
                        return nc.tensor.matmul(py[:, 0:NO], lhsT=fg2[:, mt * 128:(mt + 1) * 128], rhs=zf[:, NO:NJ],
                                                start=False, stop=True)
                    pe([M0g[k], Ug[b], Fg[k], zf], [py], f_y)
                    ys = ysb[b * 2 + mt]
                    kb.op("act", lambda e: e.activation(out=ys[:], in_=py[:, 0:NO], func=AF.Copy),
                          reads=[py], writes=[ys])

        def do_gelu(m, half):
            slots = slots_of(m, half)
            for (b, gl, g, k) in slots:
                for mt in range(2):
                    ys = ysb[b * 2 + mt]
                    yi = (b * 2 + mt) % 4
                    yt2, yg = yt_[yi], ysg[yi]
                    tt("dve", yt2, yt2[:], ys, ys[:], ys, ys[:], ALU.mult)
                    kb.op("dve", lambda e: e.tensor_scalar(out=yt2[:], in0=yt2[:], scalar1=0.044715, scalar2=1.0,
                                                           op0=ALU.mult, op1=ALU.add), reads=[yt2], writes=[yt2])
                    tt("dve", yt2, yt2[:], yt2, yt2[:], ys, ys[:], ALU.mult)
                    kb.op("act", lambda e: e.activation(out=yg[:], in_=yt2[:], func=AF.Sigmoid, scale=1.5957691216),
                          reads=[yt2], writes=[yg])
                    tt("pool", ygel, ygel[:, gl, mt, :], ys, ys[:], yg, yg[:], ALU.mult)

        def do_reverse(m, ym):
            for mt in range(2):
                for t8 in range(8):
                    pr = ps[(mt * 8 + t8) % 2]

                    def f_r():
                        ins = None
                        for gl in range(8):
                            ins = nc.tensor.matmul(pr[:, 0:NO], lhsT=SelR[:, gl * 8 + t8, :], rhs=ygel[:, gl, mt, :],
                                                   start=(gl == 0), stop=(gl == 7))
                        return ins
                    pe([SelR, ygel], [pr], f_r)
                    off = mt * 8 + t8
                    evac(ym, ym[:, off:HALF:16], pr, pr[:, 0:NO])
            kb.dma("act", ym, YT[m * 128:(m + 1) * 128, :], ym[:], reads=[ym])

        do_setup(*batches[0])
        um = None
        for bi, (m, half) in enumerate(batches):
            for b_ in range(NB):
                Ug[b_] = Ug2[(bi % 2) * 4 + b_]
            do_relayout(m, half, um)
            cur = do_ds(m, half)
            cur = do_ks(m, half, cur)
            do_ymm(m, half, cur)
            if bi + 1 < len(batches):
                do_setup(*batches[bi + 1])
            do_gelu(m, half)
            if half == 1:
                do_reverse(m, YTm[0])
        kb.drain_dma()


def make_consts():
    c = np.zeros((128, 260), np.float32)
    c[:, 0:128] = np.eye(128, dtype=np.float32)
    for m in range(128):
        c[(m + 64) % 128, 128 + m] = 1.0
    c[:, 256] = np.arange(128) % 64
    c[:, 257] = np.where(np.arange(128) < 64, -1.0, 1.0)
    c[:, 258] = np.where(np.arange(128) < 64, 1.0, -1.0)
    return c


def make_vbq(hf):
    v = np.full((32, 32), -1e30, np.float32)
    for qt in range(32):
        i = qt // 2
        for n in range(32):
            if n < 16 + i and (n >= 16 or hf == 1):
                v[qt, n] = 0.0
    return np.ascontiguousarray(np.broadcast_to(v[None], (128, 32, 32)))


def make_oneh():
    o = np.zeros((32, 32, 128), np.float32)
    for n in range(32):
        o[n, n, :] = 1.0
    return o.reshape(32, 32 * 128)


def rot_perm():
    idx = []
    for base in (0, 1024):
        for h in range(NH):
            for d in range(HD):
                idx.append(base + h * HD + (d + 64) % HD)
    return np.array(idx)


def make_s5c(inputs):
    p = np.arange(128)
    n = p % 64
    lam_re = f32c(inputs["lam_re"])[0]
    lam_im = f32c(inputs["lam_im"])[0]
    log_step = f32c(inputs["log_step"])[0]
    b_re = f32c(inputs["b_re"])[0]
    b_im = f32c(inputs["b_im"])[0]
    c_re = f32c(inputs["c_re"])[0]
    c_im = f32c(inputs["c_im"])[0]
    d_skip = f32c(inputs["d_skip"])[0]
    out = np.zeros((128, S5C_COLS), np.float32)
    out[:, 0:32] = lam_re[:, n].T
    out[:, 32:64] = lam_im[:, n].T
    out[:, 64:96] = log_step[None, :]
    out[:, 96:128] = d_skip[:, p % 16].T
    out[:, 128:640] = b_re[:, n, :].transpose(1, 0, 2).reshape(128, 512)
    out[:, 640:1152] = b_im[:, n, :].transpose(1, 0, 2).reshape(128, 512)
    out[:, 1152:1664] = c_re[:, :, n].transpose(2, 0, 1).reshape(128, 512)
    out[:, 1664:2176] = c_im[:, :, n].transpose(2, 0, 1).reshape(128, 512)
    mk = np.zeros((128, 2, 16, 16), np.float32)
    di = np.zeros((128, 2, 256), np.float32)
    for r in range(128):
        s8 = r // 16
        for kt in range(2):
            sfull = kt * 8 + s8
            mk[r, kt, sfull:, :] = 1.0
            di[r, kt, kt * 128 + r] = 1.0
    out[:, 2176:2688] = mk.reshape(128, 512)
    out[:, 2688:3200] = di.reshape(128, 512)
    return out


def f32c(a):
    return np.ascontiguousarray(np.asarray(a, np.float32))


def make_in_maps(inputs):
    x = np.asarray(inputs["x"], np.float32)
    w_in = np.ascontiguousarray(np.asarray(inputs["w_in"], np.float32)[0])
    s5c = make_s5c(inputs)
    maps = []
    for core in range(8):
        b, hf = core // 2, core % 2
        xa = np.zeros((SEQ, D), np.float32)
        if hf == 1:
            xa[:] = x[b]
            pos = np.arange(SEQ, dtype=np.float32)
        else:
            xa[HALF:] = x[b, :HALF]
            pos = np.concatenate([np.zeros(HALF, np.float32), np.arange(HALF, dtype=np.float32)])
        maps.append({
            "x_all": xa, "w_in": w_in,
            "g_mix": np.ascontiguousarray(np.asarray(inputs["norm_mix_g"], np.float32)[0].reshape(8, 128).T),
            "pos_all": pos, "cst": make_consts(),
            "vbq": make_vbq(hf), "tri": np.triu(np.ones((128, 128), np.float32)),
            "w_glu": f32c(inputs["w_glu"][0]), "w_out": f32c(inputs["w_out"][0]),
            "w_up": f32c(inputs["w_up"][0]), "w_down": f32c(inputs["w_down"][0]),
            "g_mlp": np.ascontiguousarray(np.asarray(inputs["norm_mlp_g"], np.float32)[0].reshape(8, 128).T),
            "g_fin": f32c(inputs["norm_final_g"]), "s5c": s5c,
        })
    return maps


def kernel(**inputs):
    nc = build(debug=False)
    maps = make_in_maps(inputs)
    res = run_bass_kernel_spmd(nc, maps, core_ids=list(range(8)))
    out = np.zeros((4, SEQ, D), np.float32)
    for core in range(8):
        b, hf = core // 2, core % 2
        out[b, hf * HALF:(hf + 1) * HALF] = np.asarray(res.results[core]["out"], np.float32)
    return out
```

```python
import math
from contextlib import ExitStack

import numpy as np
import concourse.bass as bass
import concourse.mybir as mybir
from concourse.bass_utils import run_bass_kernel_spmd

F32 = mybir.dt.float32
BF16 = mybir.dt.bfloat16
I32 = mybir.dt.int32
ALU = mybir.AluOpType
AF = mybir.ActivationFunctionType
AX = mybir.AxisListType

D = 1024
SEQ = 8192
HALF = 4096
NH = 8
HD = 128
INW = 5632
SSMW = 512
DFF = 4096
EPS = 1e-6
TT = 512
PI = math.pi
TWO_PI = 2.0 * math.pi
MASKV = 32768.0


class Buf:
    __slots__ = ("t", "w", "r", "name", "psum")

    def __init__(self, t, name="", psum=False):
        self.t = t
        self.w = {}
        self.r = {}
        self.name = name
        self.psum = psum

    def __getitem__(self, idx):
        return self.t[idx]


class KB:
    def __init__(self, nc, es):
        self.nc = nc
        self.es = es
        self.E = {"pe": nc.tensor, "act": nc.scalar, "dve": nc.vector,
                  "pool": nc.gpsimd, "sp": nc.sync}
        self.sem = {}
        self.cnt = {}
        self.seen = {}
        for n in ["pe", "act", "dve", "pool"]:
            self.sem[n] = es.enter_context(nc.semaphore("s_" + n))
            self.cnt[n] = 0
        self.ndq = 0

    def new_dma_sem(self, name):
        s = "dq_" + name
        self.sem[s] = self.es.enter_context(self.nc.semaphore(s))
        self.cnt[s] = 0
        return s

    def mult(self, s):
        return 16 if s.startswith("dq_") else 1

    def wait(self, e, tok):
        s, v = tok
        if s == e and e == "pe":
            return
        key = (e, s)
        if self.seen.get(key, 0) >= v:
            return
        self.seen[key] = v
        self.E[e].wait_ge(self.sem[s], v * self.mult(s))

    def sync(self, e, reads=(), writes=(), deps=(), issuer=None):
        we = issuer or e
        for b in reads:
            for s, v in b.w.items():
                self.wait(we, (s, v))
            if b.psum:
                for s, v in b.r.items():
                    if s != e:
                        self.wait(we, (s, v))
        for b in writes:
            for s, v in b.w.items():
                if s != e or s.startswith("dq_"):
                    self.wait(we, (s, v))
            for s, v in b.r.items():
                if s != e or s.startswith("dq_"):
                    self.wait(we, (s, v))
        for t in deps:
            if t is not None:
                self.wait(we, t)

    def note(self, tok, reads=(), writes=()):
        s, v = tok
        for b in reads:
            b.r[s] = v
        for b in writes:
            if b.r:
                b.w = {s: v}
                b.r = {}
            else:
                b.w[s] = v

    def bump(self, e, ins):
        self.cnt[e] += 1
        ins.then_inc(self.sem[e], self.mult(e))
        return (e, self.cnt[e])

    def op(self, e, fn, reads=(), writes=(), deps=()):
        self.sync(e, reads, writes, deps)
        ins = fn(self.E[e])
        tok = self.bump(e, ins)
        self.note(tok, reads, writes)
        return tok

    def dma(self, q, sbuf_buf, out, in_, reads=(), writes=(), deps=()):
        dsem = "dq_" + sbuf_buf.name
        if dsem not in self.sem:
            self.new_dma_sem(sbuf_buf.name)
        self.sync(dsem, reads, writes, deps, issuer=q)
        ins = self.E[q].dma_start(out=out, in_=in_)
        tok = self.bump(dsem, ins)
        self.note(tok, reads, writes)
        return tok

    def drain_dma(self, engines=("sp", "act")):
        for s, v in self.cnt.items():
            if s.startswith("dq_") and v > 0:
                for e in engines:
                    self.wait(e, (s, v))

    def barrier(self):
        toks = [(s, v) for s, v in self.cnt.items() if v > 0]
        for e in ["pe", "act", "dve", "pool", "sp"]:
            for t in toks:
                if t[0] != e:
                    self.wait(e, t)


def _sb(nc, es, name, shape, dt):
    return Buf(es.enter_context(nc.sbuf_tensor(name, shape, dt)), name)


STOP = [None]
PHASES = set("ABCDE")


class _Stop(Exception):
    pass


def _chk(tag):
    if STOP[0] == tag:
        raise _Stop()


def build(debug=False):
    nc = bass.Bass("TRN2", target_bir_lowering=False)
    dk = "ExternalOutput" if debug else "Internal"

    x_all = nc.dram_tensor("x_all", [SEQ, D], F32, kind="ExternalInput").ap()
    w_in = nc.dram_tensor("w_in", [D, INW], F32, kind="ExternalInput").ap()
    g_mix = nc.dram_tensor("g_mix", [128, 8], F32, kind="ExternalInput").ap()
    pos_all = nc.dram_tensor("pos_all", [SEQ], F32, kind="ExternalInput").ap()
    cst = nc.dram_tensor("cst", [128, 260], F32, kind="ExternalInput").ap()

    KT = nc.dram_tensor("KT", [NH, HD, SEQ], BF16, kind=dk).ap()
    QT = nc.dram_tensor("QT", [NH, HD, HALF], BF16, kind=dk).ap()
    Vs = nc.dram_tensor("Vs", [SEQ, D], BF16, kind=dk).ap()
    UT = nc.dram_tensor("UT", [SSMW, 16, SEQ // 16], BF16, kind=dk).ap()
    Gs = nc.dram_tensor("Gs", [HALF, 2 * D], BF16, kind=dk).ap()
    Os = nc.dram_tensor("Os", [HALF, D], BF16, kind=dk).ap()
    vbq_d = nc.dram_tensor("vbq", [128, 32, 32], F32, kind="ExternalInput").ap()
    SEL = nc.dram_tensor("SEL", [NH, 16, 32 * 256], BF16, kind="Internal").ap()
    tri_d = nc.dram_tensor("tri", [128, 128], F32, kind="ExternalInput").ap()
    w_glu = nc.dram_tensor("w_glu", [SSMW, 2 * D], F32, kind="ExternalInput").ap()
    w_out = nc.dram_tensor("w_out", [D, D], F32, kind="ExternalInput").ap()
    w_up = nc.dram_tensor("w_up", [D, DFF], F32, kind="ExternalInput").ap()
    w_down = nc.dram_tensor("w_down", [DFF, D], F32, kind="ExternalInput").ap()
    g_mlp = nc.dram_tensor("g_mlp", [128, 8], F32, kind="ExternalInput").ap()
    g_fin = nc.dram_tensor("g_fin", [D], F32, kind="ExternalInput").ap()
    YT = nc.dram_tensor("YT", [SSMW, HALF], BF16, kind=dk).ap()
    WB = {
        "glu": nc.dram_tensor("WB_glu", [128, 4 * 2 * D], BF16, kind="Internal").ap(),
        "out": nc.dram_tensor("WB_out", [128, 8 * D], BF16, kind="Internal").ap(),
        "up": nc.dram_tensor("WB_up", [128, 8 * DFF], BF16, kind="Internal").ap(),
        "down": nc.dram_tensor("WB_down", [128, 32 * D], BF16, kind="Internal").ap(),
    }
    WSRC = {"glu": (w_glu, 4, 2 * D), "out": (w_out, 8, D), "up": (w_up, 8, DFF), "down": (w_down, 32, D)}
    X1 = nc.dram_tensor("X1", [HALF, D], F32, kind=dk).ap()
    out = nc.dram_tensor("out", [HALF, D], F32, kind="ExternalOutput").ap()
    s5c_d = nc.dram_tensor("s5c", [128, S5C_COLS], F32, kind="ExternalInput").ap()

    with ExitStack() as es:
        kb = KB(nc, es)
        ps = [Buf(es.enter_context(nc.psum_tensor("ps%d" % i, [128, 512], F32)), "ps%d" % i, psum=True)
              for i in range(8)]
        if "A" in PHASES:
            try:
                phase_a(nc, kb, ps, x_all, w_in, g_mix, pos_all, cst, KT, QT, Vs, UT, Gs)
            except _Stop:
                pass
            kb.barrier()
        if "B" in PHASES:
            phase_b(nc, kb, ps, KT, QT, Vs, Os, cst, vbq_d, SEL, tri_d, WB, WSRC)
            kb.barrier()
        if "C" in PHASES:
            phase_c(nc, kb, ps, UT, YT, cst, s5c_d)
            kb.barrier()
        with ExitStack() as es2:
            wu = _sb(nc, es2, "wu", [128, 8, DFF], BF16)
            wd = _sb(nc, es2, "wd", [128, 32, D], BF16)

            def mlp_w_gen():
                for c0 in range(0, 8, 2):
                    kb.dma("sp", wu, wu[:, c0:c0 + 2, :].rearrange("p c n -> p (c n)"),
                           WB["up"][:, c0 * DFF:(c0 + 2) * DFF], writes=[wu])
                    yield
                for c0 in range(0, 32, 8):
                    kb.dma("sp", wd, wd[:, c0:c0 + 8, :].rearrange("p c n -> p (c n)"),
                           WB["down"][:, c0 * D:(c0 + 8) * D], writes=[wd])
                    yield
            wgen = mlp_w_gen()

            def load_mlp_w(n=100):
                for _ in range(n):
                    try:
                        next(wgen)
                    except StopIteration:
                        return
            loaded = False
            if "D" in PHASES:
                phase_d1(nc, kb, ps, x_all, WB, cst, YT, Os, Gs, X1, load_mlp_w)
                loaded = True
                kb.barrier()
            if "E" in PHASES:
                if not loaded:
                    load_mlp_w()
                phase_d2(nc, kb, ps, wu, wd, g_mlp, g_fin, cst, X1, out)
                kb.barrier()
    return nc


def phase_a(nc, kb, ps, x_all, w_in, g_mix, pos_all, cst, KT, QT, Vs, UT, Gs):
    with ExitStack() as es:
        sb = lambda name, shape, dt: _sb(nc, es, name, shape, dt)
        wb = sb("wb", [128, 8, INW], BF16)
        stg = [sb("stg%d" % i, [128, 8, 128], F32) for i in range(2)]
        cstt = sb("cstt", [128, 260], F32)
        identb = sb("identb", [128, 128], BF16)
        swapb = sb("swapb", [128, 128], BF16)
        gcol = sb("gcol", [128, 8], F32)
        invf = sb("invf", [128, 1], F32)
        xt = [sb("xt%d" % i, [128, D], F32) for i in range(4)]
        junk = sb("junk", [128, D], BF16)
        ss = [sb("ss%d" % i, [128, 4], F32) for i in range(2)]
        rstd = [sb("rstd%d" % i, [128, 4], F32) for i in range(2)]
        hb = [sb("hb%d" % i, [128, 4, D], BF16) for i in range(1)]
        hT = [sb("hT%d" % i, [128, 8, TT], BF16) for i in range(2)]
        posb = [sb("posb%d" % i, [128, TT], F32) for i in range(1)]
        ang = sb("ang", [128, TT], F32)
        kf = sb("kf", [128, TT], F32)
        ki = sb("ki", [128, TT], I32)
        rr = sb("rr", [128, TT], F32)
        rc = sb("rc", [128, TT], F32)
        mm = sb("mm", [128, TT], F32)
        cosT = [sb("cosT%d" % i, [128, TT], F32) for i in range(1)]
        sinT = [sb("sinT%d" % i, [128, TT], F32) for i in range(1)]
        t1 = [sb("t1_%d" % i, [128, TT], F32) for i in range(3)]
        t2 = [sb("t2_%d" % i, [128, TT], F32) for i in range(3)]
        qraw = [sb("qraw%d" % i, [128, TT], BF16) for i in range(3)]
        kqst = [sb("kqst%d" % i, [128, NH, TT], BF16) for i in range(2)]
        vst = [sb("vst%d" % i, [128, D], BF16) for i in range(2)]
        ust = [sb("ust%d" % i, [128, 4, TT], BF16) for i in range(1)]
        gst = [sb("gst%d" % i, [128, 2 * D], BF16) for i in range(2)]

        kb.dma("sp", cstt, cstt[:], cst[:, :], writes=[cstt])
        kb.dma("sp", gcol, gcol[:], g_mix[:, :], writes=[gcol])
        kb.op("dve", lambda e: e.tensor_copy(out=identb[:], in_=cstt[:, 0:128]), reads=[cstt], writes=[identb])
        kb.op("dve", lambda e: e.tensor_copy(out=swapb[:], in_=cstt[:, 128:256]), reads=[cstt], writes=[swapb])
        kb.op("act", lambda e: e.activation(out=invf[:], in_=cstt[:, 256:257], func=AF.Exp,
                                            scale=-math.log(10000.0) / 64.0), reads=[cstt], writes=[invf])

        wv = w_in.rearrange("(c p) n -> p c n", p=128)
        wbq = Buf(wb.t, "wbq")

        def wsel(col0):
            return wb if 1024 <= col0 < 3584 else wbq
        wci = [0]

        def conv_w(n0):
            ci = wci[0]
            wci[0] += 1
            st = stg[ci % 2]
            kb.dma("sp", st, st[:], wv[:, :, n0:n0 + 128], writes=[st])
            eng = ["dve", "pool"][ci % 2]
            kb.op(eng, lambda e: e.tensor_copy(out=wb[:, :, n0:n0 + 128], in_=st[:]), reads=[st], writes=[wsel(n0)])
        for n0 in range(1024, 3584, 128):
            conv_w(n0)
        late_cols = list(range(0, 1024, 128)) + list(range(3584, INW, 128))

        _chk('w')
        QC, KC, VC, UC, GC = 0, 1024, 2048, 3072, 3584
        evac_rr = [0]

        def evac_engine():
            evac_rr[0] += 1
            return ["act", "dve"][evac_rr[0] % 2]

        psrot = [0]

        def next_ps():
            psrot[0] = (psrot[0] + 1) % 6
            return ps[2 + psrot[0]]

        def evac_copy(dst_buf, dst_ap, pbuf):
            eng = evac_engine()
            if eng == "act":
                kb.op("act", lambda e: e.activation(out=dst_ap, in_=pbuf[:], func=AF.Copy),
                      reads=[pbuf], writes=[dst_buf])
            else:
                kb.op("dve", lambda e: e.tensor_copy(out=dst_ap, in_=pbuf[:]), reads=[pbuf], writes=[dst_buf])

        ntiles = SEQ // TT
        xcount = [0]
        kq_i = [0]
        rot_i = [0]
        v_i = [0]
        g_i = [0]
        hbt = hb[0]
        pb_ = posb[0]

        def norm_tile(it):
            p2 = it % 2
            tok0 = it * TT
            for s in range(4):
                xs = xt[xcount[0] % 4]
                xcount[0] += 1
                kb.dma("sp", xs, xs[:], x_all[tok0 + s * 128:tok0 + (s + 1) * 128, :], writes=[xs])
                kb.op("act", lambda e: e.activation(out=junk[:], in_=xs[:], func=AF.Square,
                                                    accum_out=ss[p2][:, s:s + 1]),
                      reads=[xs], writes=[junk, ss[p2]])
                kb.op("dve", lambda e: e.tensor_scalar(out=rstd[p2][:, s:s + 1], in0=ss[p2][:, s:s + 1],
                                                       scalar1=1.0 / D, scalar2=EPS,
                                                       op0=ALU.mult, op1=ALU.add), reads=[ss[p2]], writes=[rstd[p2]])
                kb.op("act", lambda e: e.activation(out=rstd[p2][:, s:s + 1], in_=rstd[p2][:, s:s + 1], func=AF.Sqrt),
                      reads=[rstd[p2]], writes=[rstd[p2]])
                kb.op("dve", lambda e: e.reciprocal(out=rstd[p2][:, s:s + 1], in_=rstd[p2][:, s:s + 1]),
                      reads=[rstd[p2]], writes=[rstd[p2]])
                if s % 2 == 0:
                    kb.op("dve", lambda e: e.tensor_scalar(out=hbt[:, s, :], in0=xs[:],
                                                           scalar1=rstd[p2][:, s:s + 1], scalar2=None, op0=ALU.mult),
                          reads=[xs, rstd[p2]], writes=[hbt])
                else:
                    kb.op("act", lambda e: e.activation(out=hbt[:, s, :], in_=xs[:], func=AF.Copy,
                                                        scale=rstd[p2][:, s:s + 1]),
                          reads=[xs, rstd[p2]], writes=[hbt])

        def rope_tables(it):
            tok0 = it * TT
            kb.dma("sp", pb_, pb_[:], pos_all[tok0:tok0 + TT].partition_broadcast(128), writes=[pb_])
            kb.op("dve", lambda e: e.tensor_scalar(out=ang[:], in0=pb_[:], scalar1=invf[:, 0:1], scalar2=None,
                                                   op0=ALU.mult), reads=[pb_, invf], writes=[ang])
            kb.op("dve", lambda e: e.tensor_scalar(out=kf[:], in0=ang[:], scalar1=1.0 / TWO_PI, scalar2=None,
                                                   op0=ALU.mult), reads=[ang], writes=[kf])
            kb.op("dve", lambda e: e.tensor_copy(out=ki[:], in_=kf[:]), reads=[kf], writes=[ki])
            kb.op("dve", lambda e: e.tensor_copy(out=kf[:], in_=ki[:]), reads=[ki], writes=[kf])
            kb.op("dve", lambda e: e.scalar_tensor_tensor(out=rr[:], in0=kf[:], scalar=-TWO_PI, in1=ang[:],
                                                          op0=ALU.mult, op1=ALU.add), reads=[kf, ang], writes=[rr])

            def wrap(dst, src):
                kb.op("dve", lambda e: e.tensor_scalar(out=mm[:], in0=src[:], scalar1=PI, scalar2=-TWO_PI,
                                                       op0=ALU.is_gt, op1=ALU.mult), reads=[src], writes=[mm])
                kb.op("dve", lambda e: e.tensor_tensor(out=dst[:], in0=src[:], in1=mm[:], op=ALU.add),
                      reads=[src, mm], writes=[dst])
                kb.op("dve", lambda e: e.tensor_scalar(out=mm[:], in0=dst[:], scalar1=-PI, scalar2=TWO_PI,
                                                       op0=ALU.is_lt, op1=ALU.mult), reads=[dst], writes=[mm])
                kb.op("dve", lambda e: e.tensor_tensor(out=dst[:], in0=dst[:], in1=mm[:], op=ALU.add),
                      reads=[dst, mm], writes=[dst])
                kb.op("dve", lambda e: e.tensor_scalar(out=dst[:], in0=dst[:], scalar1=3.14159, scalar2=-3.14159,
                                                       op0=ALU.min, op1=ALU.max), reads=[dst], writes=[dst])

            wrap(rr, rr)
            kb.op("dve", lambda e: e.tensor_scalar(out=rc[:], in0=rr[:], scalar1=PI / 2, scalar2=None, op0=ALU.add),
                  reads=[rr], writes=[rc])
            wrap(rc, rc)
            kb.op("act", lambda e: e.activation(out=sinT[0][:], in_=rr[:], func=AF.Sin, scale=cstt[:, 257:258]),
                  reads=[rr, cstt], writes=[sinT[0]])
            kb.op("act", lambda e: e.activation(out=cosT[0][:], in_=rc[:], func=AF.Sin),
                  reads=[rc], writes=[cosT[0]])

        def transposes(it):
            p2 = it % 2
            for c in range(8):
                pt = ps[c % 2]
                ptb = pt[:].bitcast(BF16)
                kb.sync("pe", reads=[hbt, identb], writes=[pt])
                ins = None
                for s in range(4):
                    ins = nc.tensor.transpose(out=ptb[:, s * 128:(s + 1) * 128],
                                              in_=hbt[:, s, c * 128:(c + 1) * 128], identity=identb[:])
                tok = kb.bump("pe", ins)
                kb.note(tok, reads=[hbt, identb], writes=[pt])
                eng = evac_engine()
                if eng == "act":
                    kb.op("act", lambda e: e.activation(out=hT[p2][:, c, :], in_=ptb[:, 0:TT], func=AF.Copy,
                                                        scale=gcol[:, c:c + 1]),
                          reads=[pt, gcol], writes=[hT[p2]])
                else:
                    kb.op("dve", lambda e: e.tensor_scalar(out=hT[p2][:, c, :], in0=ptb[:, 0:TT],
                                                           scalar1=gcol[:, c:c + 1], scalar2=None, op0=ALU.mult),
                          reads=[pt, gcol], writes=[hT[p2]])

        def fm_group(p2, col0, pbuf):
            wb_ = wsel(col0)
            kb.sync("pe", reads=[wb_, hT[p2]], writes=[pbuf])
            ins = None
            for c in range(8):
                ins = nc.tensor.matmul(pbuf[:], lhsT=wb[:, c, col0:col0 + 128], rhs=hT[p2][:, c, :],
                                       start=(c == 0), stop=(c == 7))
            tok = kb.bump("pe", ins)
            kb.note(tok, reads=[wb_, hT[p2]], writes=[pbuf])

        def tm_group(p2, s, col0, pbuf):
            wb_ = wsel(col0)
            kb.sync("pe", reads=[wb_, hT[p2]], writes=[pbuf])
            ins = None
            for c in range(8):
                ins = nc.tensor.matmul(pbuf[:], lhsT=hT[p2][:, c, s * 128:(s + 1) * 128],
                                       rhs=wb[:, c, col0:col0 + 512], start=(c == 0), stop=(c == 7))
            tok = kb.bump("pe", ins)
            kb.note(tok, reads=[wb_, hT[p2]], writes=[pbuf])

        def rope_heads(p2, col_base, dst):
            pend = None

            def finish(pa, j, h):
                pb = next_ps()
                kb.sync("pe", reads=[qraw[j], swapb], writes=[pb])
                ins = nc.tensor.matmul(pb[:], lhsT=swapb[:], rhs=qraw[j][:], start=True, stop=True)
                tok = kb.bump("pe", ins)
                kb.note(tok, reads=[qraw[j], swapb], writes=[pb])
                kb.op("dve", lambda e: e.tensor_tensor(out=t1[j][:], in0=pa[:], in1=cosT[0][:], op=ALU.mult),
                      reads=[pa, cosT[0]], writes=[t1[j]])
                kb.op("dve", lambda e: e.tensor_tensor(out=t2[j][:], in0=pb[:], in1=sinT[0][:], op=ALU.mult),
                      reads=[pb, sinT[0]], writes=[t2[j]])
                kb.op("pool", lambda e: e.tensor_tensor(out=dst[:, h, :], in0=t1[j][:], in1=t2[j][:], op=ALU.add),
                      reads=[t1[j], t2[j]], writes=[dst])

            pend = []
            for h in range(NH):
                pa = next_ps()
                fm_group(p2, col_base + h * 128, pa)
                j = rot_i[0] % 3
                rot_i[0] += 1
                kb.op("act", lambda e: e.activation(out=qraw[j][:], in_=pa[:], func=AF.Copy),
                      reads=[pa], writes=[qraw[j]])
                pend.append((pa, j, h))
                if len(pend) > 2:
                    finish(*pend.pop(0))
            while pend:
                finish(*pend.pop(0))

        def proj_k(it):
            p2 = it % 2
            tok0 = it * TT
            kst = kqst[kq_i[0] % 2]
            kq_i[0] += 1
            rope_heads(p2, KC, kst)
            kb.dma("act", kst, KT[:, :, tok0:tok0 + TT].rearrange("h d t -> d h t"), kst[:], reads=[kst])

        def proj_rest(it):
            own = it >= ntiles // 2
            p2 = it % 2
            tok0 = it * TT
            for s in range(4):
                vb = vst[v_i[0] % 2]
                v_i[0] += 1
                for hf in range(2):
                    pbuf = next_ps()
                    tm_group(p2, s, VC + hf * 512, pbuf)
                    evac_copy(vb, vb[:, hf * 512:(hf + 1) * 512], pbuf)
                kb.dma("act", vb, Vs[tok0 + s * 128:tok0 + (s + 1) * 128, :], vb[:], reads=[vb])
            j0 = tok0 // 16
            for m in range(4):
                pbuf = next_ps()
                fm_group(p2, UC + m * 128, pbuf)
                src = pbuf[:].rearrange("p (j s) -> p s j", s=16)
                dstv = ust[0][:, m, :].rearrange("p (s j) -> p s j", s=16)
                if evac_engine() == "act":
                    kb.op("act", lambda e: e.activation(out=dstv, in_=src, func=AF.Copy), reads=[pbuf], writes=[ust[0]])
                else:
                    kb.op("dve", lambda e: e.tensor_copy(out=dstv, in_=src), reads=[pbuf], writes=[ust[0]])
            for m in range(4):
                usrc = ust[0][:, m, :].rearrange("p (s j) -> p s j", s=16)
                for s0 in (0, 8):
                    kb.dma("act", ust[0], UT[m * 128:(m + 1) * 128, s0:s0 + 8, j0:j0 + TT // 16],
                           usrc[:, s0:s0 + 8, :], reads=[ust[0]])
            if own:
                o0 = tok0 - HALF
                qst = kqst[kq_i[0] % 2]
                kq_i[0] += 1
                rope_heads(p2, QC, qst)
                kb.dma("act", qst, QT[:, :, o0:o0 + TT].rearrange("h d t -> d h t"), qst[:], reads=[qst])
                for s in range(4):
                    gb = gst[g_i[0] % 2]
                    g_i[0] += 1
                    for cb in range(4):
                        pbuf = next_ps()
                        tm_group(p2, s, GC + cb * 512, pbuf)
                        kb.op("act", lambda e: e.activation(out=gb[:, cb * 512:(cb + 1) * 512], in_=pbuf[:],
                                                            func=AF.Sigmoid), reads=[pbuf], writes=[gb])
                    kb.dma("act", gb, Gs[o0 + s * 128:o0 + (s + 1) * 128, :], gb[:], reads=[gb])

        norm_tile(0)
        transposes(0)
        for it in range(ntiles):
            rope_tables(it)
            if it + 1 < ntiles:
                norm_tile(it + 1)
            proj_k(it)
            if it + 1 < ntiles:
                transposes(it + 1)
            for _ in range(3):
                if late_cols:
                    conv_w(late_cols.pop(0))
            proj_rest(it)
        kb.drain_dma()


def phase_b(nc, kb, ps, KT, QT, Vs, Os, cst, vbq_d, SEL, tri_d, WB, WSRC):
    SCALE = 1.0 / math.sqrt(HD)
    with ExitStack() as es:
        sb = lambda name, shape, dt: _sb(nc, es, name, shape, dt)
        kth = [sb("kth%d" % i, [128, SEQ], BF16) for i in range(2)]
        vh = [sb("vh%d" % i, [128, 64, 129], BF16) for i in range(2)]
        qth = [sb("qth%d" % i, [128, HALF], BF16) for i in range(2)]
        cstt = sb("b_cstt", [128, 260], F32)
        identb = sb("b_identb", [128, 128], BF16)
        vbq = sb("vbq_sb", [128, 32, 32], F32)
        selr = [sb("selr%d" % i, [128, 32, 256], BF16) for i in range(2)]
        seld = [Buf(None, "seld%d" % i) for i in range(2)]
        trif = sb("trif", [128, 128], F32)
        trib = sb("trib", [128, 128], BF16)
        kmf = sb("kmf", [128, 32], F32)
        kmb = sb("kmb", [128, 32], BF16)
        gv = sb("gv", [128, 32, 32], F32)
        mx = sb("mx", [128, 32, 8], F32)
        thr = sb("thr", [128, 32], F32)
        biasb = sb("biasb", [128, 32, 32], BF16)
        biasT = sb("biasT", [32, HALF], BF16)
        ptsb = [sb("ptsb%d" % i, [128, 512], BF16) for i in range(20)]
        pst_bufs = [ps[0], ps[1], ps[7]]
        rec = sb("rec", [128, 2], F32)
        ost = [sb("ost%d" % i, [128, 32, 128], BF16) for i in range(2)]

        kb.dma("sp", cstt, cstt[:], cst[:, :], writes=[cstt])
        kb.dma("sp", vbq, vbq[:], vbq_d[:, :, :], writes=[vbq])
        kb.dma("sp", trif, trif[:], tri_d[:, :], writes=[trif])
        kb.op("dve", lambda e: e.tensor_copy(out=identb[:], in_=cstt[:, 0:128]), reads=[cstt], writes=[identb])
        kb.op("dve", lambda e: e.tensor_copy(out=trib[:], in_=trif[:]), reads=[trif], writes=[trib])
        for i in range(2):
            kb.op("pool", lambda e: e.memset(vh[i][:, :, 128:129], 1.0), writes=[vh[i]])

        def load_head(h):
            p = h % 2
            kb.dma("sp", kth[p], kth[p][:], KT[h, :, :], writes=[kth[p]])
            kb.dma("sp", qth[p], qth[p][:], QT[h, :, :], writes=[qth[p]])
            vsrc = Vs[:, h * 128:(h + 1) * 128].rearrange("(t p) c -> p t c", p=128)
            for t0_ in range(0, 64, 8):
                kb.dma("sp", vh[p], vh[p][:, t0_:t0_ + 8, 0:128], vsrc[:, t0_:t0_ + 8, :], writes=[vh[p]])

        cstg = [sb("cstg%d" % i, [128, 2, 512], F32) for i in range(2)]
        cbf = [sb("cbf%d" % i, [128, 2, 512], BF16) for i in range(2)]

        def conv_gen():
            ci = 0
            for name in ("glu", "out", "up", "down"):
                wsrc, kc, ncols = WSRC[name]
                wv = wsrc.rearrange("(c p) n -> p c n", p=128)
                dst = WB[name].rearrange("p (c n) -> p c n", n=ncols)
                for k0 in range(0, kc, 2):
                    for n0 in range(0, ncols, 512):
                        st, bf = cstg[ci % 2], cbf[ci % 2]
                        kb.dma("sp", st, st[:], wv[:, k0:k0 + 2, n0:n0 + 512], writes=[st])
                        kb.op("dve", lambda e: e.tensor_copy(out=bf[:], in_=st[:]), reads=[st], writes=[bf])
                        kb.dma("sp", bf, dst[:, k0:k0 + 2, n0:n0 + 512], bf[:], reads=[bf])
                        ci += 1
                        yield
        conv = conv_gen()

        def conv_step(n):
            for _ in range(n):
                try:
                    next(conv)
                except StopIteration:
                    return

        load_head(0)
        unit = [0]
        pcount = [0]
        for h in range(NH):
            p = h % 2
            if h + 1 < NH:
                load_head(h + 1)
            K_, V_, Q_ = kth[p], vh[p], qth[p]
            kb.op("dve", lambda e: e.tensor_reduce(out=kmf[:], in_=K_[:].rearrange("p (n k) -> p n k", k=256),
                                                   axis=AX.X, op=ALU.add), reads=[K_], writes=[kmf])
            kb.op("dve", lambda e: e.tensor_scalar(out=kmb[:], in0=kmf[:], scalar1=1.0 / 256.0, scalar2=None,
                                                   op0=ALU.mult), reads=[kmf], writes=[kmb])
            for bnk in range(2):
                pg = ps[6]
                kb.sync("pe", reads=[Q_, kmb], writes=[pg])
                ins = None
                for j in range(16):
                    qt = bnk * 16 + j
                    ins = nc.tensor.matmul(pg[:, j * 32:(j + 1) * 32], lhsT=Q_[:, qt * 128:(qt + 1) * 128],
                                           rhs=kmb[:, :], start=True, stop=True)
                tok = kb.bump("pe", ins)
                kb.note(tok, reads=[Q_, kmb], writes=[pg])
                kb.op("dve", lambda e: e.tensor_tensor(out=gv[:, bnk * 16:(bnk + 1) * 16, :].rearrange("p a b -> p (a b)"),
                                                       in0=pg[:], in1=vbq[:, bnk * 16:(bnk + 1) * 16, :].rearrange("p a b -> p (a b)"),
                                                       op=ALU.add), reads=[pg, vbq], writes=[gv])
            for qt in range(32):
                kb.op("dve", lambda e: e.max(out=mx[:, qt, :], in_=gv[:, qt, :]), reads=[gv], writes=[mx])
            kb.op("dve", lambda e: e.tensor_scalar(out=thr[:], in0=mx[:, :, 2], scalar1=-1e29, scalar2=None,
                                                   op0=ALU.max), reads=[mx], writes=[thr])
            for qt in range(32):
                kb.op("dve", lambda e: e.tensor_scalar(out=biasb[:, qt, :], in0=gv[:, qt, :], scalar1=thr[:, qt:qt + 1],
                                                       scalar2=1.0, op0=ALU.is_ge, op1=ALU.mult),
                      reads=[gv, thr], writes=[biasb])
            for g in range(4):
                pt = ps[6]
                ptb = pt[:].bitcast(BF16)
                kb.sync("pe", reads=[biasb, identb], writes=[pt])
                ins = None
                for j in range(8):
                    qt = g * 8 + j
                    ins = nc.tensor.transpose(out=ptb[0:32, j * 128:(j + 1) * 128], in_=biasb[:, qt, :],
                                              identity=identb[:])
                tok = kb.bump("pe", ins)
                kb.note(tok, reads=[biasb, identb], writes=[pt])
                kb.op("dve", lambda e: e.tensor_copy(out=biasT[0:32, g * 1024:(g + 1) * 1024], in_=ptb[0:32, 0:1024]),
                      reads=[pt], writes=[biasT])

            sd = seld[h % 2]
            kb.dma("sp", biasT, SEL[h, :, :].rearrange("i (n q) -> n i q", q=256),
                   biasT[:].rearrange("n (i q) -> n i q", q=256), reads=[biasT], writes=[sd])

            def load_sel(i):
                sr = selr[i % 2]
                kb.dma("sp", sr, sr[:].rearrange("p n q -> p (n q)"), SEL[h, i, :].partition_broadcast(128),
                       reads=[sd], writes=[sr])
            load_sel(0)

            O_ = ost[p]
            units = []
            for i in range(16):
                for n in range(16 + i + 1):
                    units.append((i, n, n == 16 + i))
            state = {}

            def stage1(u):
                i, n, diag = units[u]
                q0 = i * 256
                pst = pst_bufs[unit[0] % 3]
                P_ = ptsb[unit[0] % 20]
                unit[0] += 1
                state[u] = P_
                ka = n * 256
                if not diag:
                    if n == 0 and i + 1 < 16:
                        load_sel(i + 1)
                    kb.sync("pe", reads=[K_, Q_], writes=[pst])
                    ins = None
                    for kt in range(2):
                        ins = nc.tensor.matmul(pst[:, kt * 256:(kt + 1) * 256], lhsT=K_[:, ka + kt * 128:ka + (kt + 1) * 128],
                                               rhs=Q_[:, q0:q0 + 256], start=True, stop=True)
                    tok = kb.bump("pe", ins)
                    kb.note(tok, reads=[K_, Q_], writes=[pst])
                    kb.op("act", lambda e: e.activation(out=P_[:], in_=pst[:], func=AF.Exp, scale=SCALE),
                          reads=[pst], writes=[P_])
                    sr = selr[i % 2]
                    kb.op("dve", lambda e: e.tensor_tensor(out=P_[:].rearrange("p (a b) -> p a b", a=2),
                                                           in0=P_[:].rearrange("p (a b) -> p a b", a=2),
                                                           in1=sr[:, n, :].unsqueeze(1).to_broadcast([128, 2, 256]),
                                                           op=ALU.mult), reads=[P_, sr], writes=[P_])
                else:
                    kb.sync("pe", reads=[K_, Q_], writes=[pst])
                    nc.tensor.matmul(pst[:, 0:256], lhsT=K_[:, ka:ka + 128], rhs=Q_[:, q0:q0 + 256],
                                     start=True, stop=True)
                    ins = nc.tensor.matmul(pst[:, 256:384], lhsT=K_[:, ka + 128:ka + 256],
                                           rhs=Q_[:, q0 + 128:q0 + 256], start=True, stop=True)
                    tok = kb.bump("pe", ins)
                    kb.note(tok, reads=[K_, Q_], writes=[pst])
                    kb.op("act", lambda e: e.activation(out=P_[:, 0:384], in_=pst[:, 0:384], func=AF.Exp, scale=SCALE),
                          reads=[pst], writes=[P_])
                    kb.op("pool", lambda e: e.tensor_tensor(out=P_[:, 0:128], in0=P_[:, 0:128], in1=trib[:],
                                                            op=ALU.mult), reads=[P_, trib], writes=[P_])
                    kb.op("pool", lambda e: e.tensor_tensor(out=P_[:, 256:384], in0=P_[:, 256:384], in1=trib[:],
                                                            op=ALU.mult), reads=[P_, trib], writes=[P_])

            def stage2(u):
                i, n, diag = units[u]
                P_ = state.pop(u)
                po = [ps[2 + 2 * (i % 2)], ps[3 + 2 * (i % 2)]]
                if not diag:
                    pv = [(0, 0, 0), (0, 1, 128), (1, 0, 256), (1, 1, 384)]
                else:
                    pv = [(0, 0, 0), (0, 1, 128), (1, 1, 256)]
                kb.sync("pe", reads=[P_, V_], writes=po)
                ins = None
                for idx, (kt, qs, c0) in enumerate(pv):
                    last = diag and ((qs == 0 and idx == 0) or (qs == 1 and idx == 2))
                    first = (n == 0 and kt == 0)
                    ins = nc.tensor.matmul(po[qs][:, 0:129], lhsT=P_[:, c0:c0 + 128], rhs=V_[:, 2 * n + kt, :],
                                           start=first, stop=last)
                tok = kb.bump("pe", ins)
                kb.note(tok, reads=[P_, V_], writes=po)
                if diag:
                    for qs in range(2):
                        kb.op("dve", lambda e: e.reciprocal(out=rec[:, qs:qs + 1], in_=po[qs][:, 128:129]),
                              reads=[po[qs]], writes=[rec])
                        kb.op("dve", lambda e: e.tensor_scalar(out=O_[:, 2 * i + qs, :], in0=po[qs][:, 0:128],
                                                               scalar1=rec[:, qs:qs + 1], scalar2=None, op0=ALU.mult),
                              reads=[po[qs], rec], writes=[O_])

            DEPTH = 19
            nu = len(units)
            for u in range(nu + DEPTH):
                if u < nu:
                    stage1(u)
                if u - DEPTH >= 0:
                    stage2(u - DEPTH)
                if u % 32 == 16:
                    conv_step(1)
            odst = Os[:, h * 128:(h + 1) * 128].rearrange("(t p) c -> p t c", p=128)
            for t0_ in range(0, 32, 8):
                kb.dma("act", O_, odst[:, t0_:t0_ + 8, :], O_[:, t0_:t0_ + 8, :], reads=[O_])
        conv_step(1000)
        kb.drain_dma()


def load_weight_bf16(nc, kb, wsrc, wdst, stg, ncols, kc):
    wv = wsrc.rearrange("(c p) n -> p c n", p=128)
    step = stg[0].t.shape[2]
    kstep = stg[0].t.shape[1]
    ci = 0
    for k0 in range(0, kc, kstep):
        for n0 in range(0, ncols, step):
            st = stg[ci % len(stg)]
            kb.dma("sp", st, st[:], wv[:, k0:k0 + kstep, n0:n0 + step], writes=[st])
            eng = ["dve", "pool"][ci % 2]
            kb.op(eng, lambda e: e.tensor_copy(out=wdst[:, k0:k0 + kstep, n0:n0 + step], in_=st[:]),
                  reads=[st], writes=[wdst])
            ci += 1


def phase_d1(nc, kb, ps, x_all, WB, cst, YT, Os, Gs, X1, load_next_w=None):
    with ExitStack() as es:
        sb = lambda name, shape, dt: _sb(nc, es, name, shape, dt)
        wg = sb("wg", [128, 4, 2 * D], BF16)
        wo = sb("wo", [128, 8, D], BF16)
        cstt = sb("d1_cstt", [128, 260], F32)
        identb = sb("d1_identb", [128, 128], BF16)
        yT = [sb("yT%d" % i, [128, 4, TT], BF16) for i in range(1)]
        xs_ = [sb("d1x%d" % i, [128, D], F32) for i in range(2)]
        oa = [sb("oa%d" % i, [128, D], BF16) for i in range(2)]
        gg = [sb("gg%d" % i, [128, 2 * D], BF16) for i in range(2)]
        sg = [sb("sg%d" % i, [128, 512], F32) for i in range(2)]
        ob = sb("ob", [128, D], F32)
        m1 = sb("m1", [128, D], F32)
        mixed = [sb("mixed%d" % i, [128, D], BF16) for i in range(2)]
        mixT = sb("mixT", [128, 8, 128], BF16)
        x1 = [sb("x1_%d" % i, [128, D], F32) for i in range(1)]

        kb.dma("sp", cstt, cstt[:], cst[:, :], writes=[cstt])
        kb.op("dve", lambda e: e.tensor_copy(out=identb[:], in_=cstt[:, 0:128]), reads=[cstt], writes=[identb])
        kb.dma("sp", wg, wg[:].rearrange("p c n -> p (c n)"), WB["glu"][:, :], writes=[wg])
        kb.dma("sp", wo, wo[:].rearrange("p c n -> p (c n)"), WB["out"][:, :], writes=[wo])

        yts = {}

        def stage_a(idx):
            it, s = idx // 4, idx % 4
            t0 = it * TT
            if s == 0:
                y_ = yT[0]
                kb.dma("sp", y_, y_[:], YT[:, t0:t0 + TT].rearrange("(m p) t -> p m t", p=128), writes=[y_])
                yts[it] = y_
            y_ = yts[it]
            j = idx % 2
            r0 = t0 + s * 128
            kb.dma("sp", xs_[j], xs_[j][:], x_all[HALF + r0:HALF + r0 + 128, :], writes=[xs_[j]])
            kb.dma("sp", oa[j], oa[j][:], Os[r0:r0 + 128, :], writes=[oa[j]])
            kb.dma("sp", gg[j], gg[j][:], Gs[r0:r0 + 128, :], writes=[gg[j]])
            for hb_ in range(2):
                pv = ps[(2 * hb_) % 8]
                pgt = ps[(2 * hb_ + 1) % 8]
                for (pbuf, cb) in ((pv, hb_), (pgt, 2 + hb_)):
                    kb.sync("pe", reads=[y_, wg], writes=[pbuf])
                    ins = None
                    for m in range(4):
                        ins = nc.tensor.matmul(pbuf[:], lhsT=y_[:, m, s * 128:(s + 1) * 128],
                                               rhs=wg[:, m, cb * 512:(cb + 1) * 512], start=(m == 0), stop=(m == 3))
                    tok = kb.bump("pe", ins)
                    kb.note(tok, reads=[y_, wg], writes=[pbuf])
                sgb = sg[hb_]
                kb.op("act", lambda e: e.activation(out=sgb[:], in_=pgt[:], func=AF.Sigmoid),
                      reads=[pgt], writes=[sgb])
                kb.op("dve", lambda e: e.tensor_tensor(out=ob[:, hb_ * 512:(hb_ + 1) * 512], in0=pv[:], in1=sgb[:],
                                                       op=ALU.mult), reads=[pv, sgb], writes=[ob])
            mx_ = mixed[j]
            kb.op("pool", lambda e: e.tensor_tensor(out=m1[:], in0=gg[j][:, 0:D], in1=oa[j][:], op=ALU.mult),
                  reads=[gg[j], oa[j]], writes=[m1])
            kb.op("dve", lambda e: e.tensor_tensor(out=ob[:], in0=ob[:], in1=gg[j][:, D:2 * D], op=ALU.mult),
                  reads=[ob, gg[j]], writes=[ob])
            kb.op("pool", lambda e: e.tensor_tensor(out=mx_[:], in0=m1[:], in1=ob[:], op=ALU.add),
                  reads=[m1, ob], writes=[mx_])

        def stage_b(idx):
            it, s = idx // 4, idx % 4
            t0 = it * TT
            j = idx % 2
            r0 = t0 + s * 128
            mx_ = mixed[j]
            for g in range(2):
                pt = ps[4 + g]
                ptb = pt[:].bitcast(BF16)
                kb.sync("pe", reads=[mx_, identb], writes=[pt])
                ins = None
                for c4 in range(4):
                    c = g * 4 + c4
                    ins = nc.tensor.transpose(out=ptb[:, c4 * 128:(c4 + 1) * 128], in_=mx_[:, c * 128:(c + 1) * 128],
                                              identity=identb[:])
                tok = kb.bump("pe", ins)
                kb.note(tok, reads=[mx_, identb], writes=[pt])
                dst = mixT[:, g * 4:(g + 1) * 4, :].rearrange("p a b -> p (a b)")
                if g == 0:
                    kb.op("act", lambda e: e.activation(out=dst, in_=ptb[:, 0:512], func=AF.Copy),
                          reads=[pt], writes=[mixT])
                else:
                    kb.op("dve", lambda e: e.tensor_copy(out=dst, in_=ptb[:, 0:512]), reads=[pt], writes=[mixT])
            xo = x1[0]
            for hf in range(2):
                pbuf = ps[6 + hf]
                kb.sync("pe", reads=[mixT, wo], writes=[pbuf])
                ins = None
                for c in range(8):
                    ins = nc.tensor.matmul(pbuf[:], lhsT=mixT[:, c, :], rhs=wo[:, c, hf * 512:(hf + 1) * 512],
                                           start=(c == 0), stop=(c == 7))
                tok = kb.bump("pe", ins)
                kb.note(tok, reads=[mixT, wo], writes=[pbuf])
                kb.op("dve", lambda e: e.tensor_tensor(out=xo[:, hf * 512:(hf + 1) * 512], in0=pbuf[:],
                                                       in1=xs_[j][:, hf * 512:(hf + 1) * 512], op=ALU.add),
                      reads=[pbuf, xs_[j]], writes=[xo])
            kb.dma("act", xo, X1[r0:r0 + 128, :], xo[:], reads=[xo])

        nidx = HALF // 128
        stage_a(0)
        for idx in range(nidx):
            if idx + 1 < nidx:
                stage_a(idx + 1)
            if load_next_w is not None and idx % 3 == 2:
                load_next_w(1)
            stage_b(idx)
        if load_next_w is not None:
            load_next_w(100)
        kb.drain_dma()


def phase_d2(nc, kb, ps, wu, wd, g_mlp, g_fin, cst, X1, out):
    with ExitStack() as es:
        sb = lambda name, shape, dt: _sb(nc, es, name, shape, dt)
        cstt = sb("d2_cstt", [128, 260], F32)
        identb = sb("d2_identb", [128, 128], BF16)
        gcol = sb("d2_gcol", [128, 8], F32)
        gfin = sb("gfin", [128, D], F32)
        xt = [sb("d2x%d" % i, [128, D], F32) for i in range(5)]
        junk = sb("d2junk", [128, D], BF16)
        ss = sb("d2ss", [128, 4], F32)
        rstd = sb("d2rstd", [128, 4], F32)
        hbt = sb("d2hb", [128, 4, D], BF16)
        hT = sb("d2hT", [128, 8, TT], BF16)
        rl = [sb("rl%d" % i, [128, TT], BF16) for i in range(2)]
        aT = sb("aT", [128, 32, TT], BF16)
        ss2 = sb("d2ss2", [128, 2], F32)
        rstd2 = sb("d2rstd2", [128, 2], F32)

        kb.dma("sp", cstt, cstt[:], cst[:, :], writes=[cstt])
        kb.dma("sp", gcol, gcol[:], g_mlp[:, :], writes=[gcol])
        kb.dma("sp", gfin, gfin[:], g_fin.partition_broadcast(128), writes=[gfin])
        kb.op("dve", lambda e: e.tensor_copy(out=identb[:], in_=cstt[:, 0:128]), reads=[cstt], writes=[identb])
        xn_i = [0]
        xr_i = [0]

        def norm_tile(it):
            t0 = it * TT
            for s in range(4):
                xs = xt[xn_i[0] % 2]
                xn_i[0] += 1
                kb.dma("sp", xs, xs[:], X1[t0 + s * 128:t0 + (s + 1) * 128, :], writes=[xs])
                kb.op("act", lambda e: e.activation(out=junk[:], in_=xs[:], func=AF.Square, accum_out=ss[:, s:s + 1]),
                      reads=[xs], writes=[junk, ss])
                kb.op("dve", lambda e: e.tensor_scalar(out=rstd[:, s:s + 1], in0=ss[:, s:s + 1], scalar1=1.0 / D,
                                                       scalar2=EPS, op0=ALU.mult, op1=ALU.add), reads=[ss], writes=[rstd])
                kb.op("act", lambda e: e.activation(out=rstd[:, s:s + 1], in_=rstd[:, s:s + 1], func=AF.Sqrt),
                      reads=[rstd], writes=[rstd])
                kb.op("dve", lambda e: e.reciprocal(out=rstd[:, s:s + 1], in_=rstd[:, s:s + 1]),
                      reads=[rstd], writes=[rstd])
                if s % 2 == 0:
                    kb.op("dve", lambda e: e.tensor_scalar(out=hbt[:, s, :], in0=xs[:], scalar1=rstd[:, s:s + 1],
                                                           scalar2=None, op0=ALU.mult), reads=[xs, rstd], writes=[hbt])
                else:
                    kb.op("act", lambda e: e.activation(out=hbt[:, s, :], in_=xs[:], func=AF.Copy,
                                                        scale=rstd[:, s:s + 1]), reads=[xs, rstd], writes=[hbt])

        def trans_tile(it):
            for c in range(8):
                pt = ps[c % 2]
                ptb = pt[:].bitcast(BF16)
                kb.sync("pe", reads=[hbt, identb], writes=[pt])
                ins = None
                for s in range(4):
                    ins = nc.tensor.transpose(out=ptb[:, s * 128:(s + 1) * 128], in_=hbt[:, s, c * 128:(c + 1) * 128],
                                              identity=identb[:])
                tok = kb.bump("pe", ins)
                kb.note(tok, reads=[hbt, identb], writes=[pt])
                if c % 2 == 0:
                    kb.op("act", lambda e: e.activation(out=hT[:, c, :], in_=ptb[:, 0:TT], func=AF.Copy,
                                                        scale=gcol[:, c:c + 1]), reads=[pt, gcol], writes=[hT])
                else:
                    kb.op("dve", lambda e: e.tensor_scalar(out=hT[:, c, :], in0=ptb[:, 0:TT], scalar1=gcol[:, c:c + 1],
                                                           scalar2=None, op0=ALU.mult), reads=[pt, gcol], writes=[hT])

        def up_tile(it):
            for f in range(32):
                pbuf = ps[2 + f % 3]
                kb.sync("pe", reads=[wu, hT], writes=[pbuf])
                ins = None
                for c in range(8):
                    ins = nc.tensor.matmul(pbuf[:], lhsT=wu[:, c, f * 128:(f + 1) * 128], rhs=hT[:, c, :],
                                           start=(c == 0), stop=(c == 7))
                tok = kb.bump("pe", ins)
                kb.note(tok, reads=[wu, hT], writes=[pbuf])
                r_ = rl[f % 2]
                kb.op("act", lambda e: e.activation(out=r_[:], in_=pbuf[:], func=AF.Relu), reads=[pbuf], writes=[r_])
                eng = ["pool", "dve", "dve"][f % 3]
                kb.op(eng, lambda e: e.tensor_tensor(out=aT[:, f, :], in0=r_[:], in1=r_[:], op=ALU.mult),
                      reads=[r_], writes=[aT])

        def down_tile(it):
            t0 = it * TT
            for s in range(4):
                xo = xt[2 + xr_i[0] % 3]
                xr_i[0] += 1
                kb.dma("sp", xo, xo[:], X1[t0 + s * 128:t0 + (s + 1) * 128, :], writes=[xo])
                for hf in range(2):
                    pbuf = ps[5 + (2 * s + hf) % 3]
                    kb.sync("pe", reads=[aT, wd], writes=[pbuf])
                    ins = None
                    for f in range(32):
                        ins = nc.tensor.matmul(pbuf[:], lhsT=aT[:, f, s * 128:(s + 1) * 128],
                                               rhs=wd[:, f, hf * 512:(hf + 1) * 512], start=(f == 0), stop=(f == 31))
                    tok = kb.bump("pe", ins)
                    kb.note(tok, reads=[aT, wd], writes=[pbuf])
                    kb.op("dve", lambda e: e.tensor_tensor(out=xo[:, hf * 512:(hf + 1) * 512], in0=pbuf[:],
                                                           in1=xo[:, hf * 512:(hf + 1) * 512], op=ALU.add),
                          reads=[pbuf, xo], writes=[xo])
                k2 = s % 2
                kb.op("act", lambda e: e.activation(out=junk[:], in_=xo[:], func=AF.Square, accum_out=ss2[:, k2:k2 + 1]),
                      reads=[xo], writes=[junk, ss2])
                kb.op("dve", lambda e: e.tensor_scalar(out=rstd2[:, k2:k2 + 1], in0=ss2[:, k2:k2 + 1], scalar1=1.0 / D,
                                                       scalar2=EPS, op0=ALU.mult, op1=ALU.add), reads=[ss2], writes=[rstd2])
                kb.op("act", lambda e: e.activation(out=rstd2[:, k2:k2 + 1], in_=rstd2[:, k2:k2 + 1], func=AF.Sqrt),
                      reads=[rstd2], writes=[rstd2])
                kb.op("dve", lambda e: e.reciprocal(out=rstd2[:, k2:k2 + 1], in_=rstd2[:, k2:k2 + 1]),
                      reads=[rstd2], writes=[rstd2])
                kb.op("dve", lambda e: e.scalar_tensor_tensor(out=xo[:], in0=xo[:], scalar=rstd2[:, k2:k2 + 1],
                                                              in1=gfin[:], op0=ALU.mult, op1=ALU.mult),
                      reads=[xo, rstd2, gfin], writes=[xo])
                kb.dma("act", xo, out[t0 + s * 128:t0 + (s + 1) * 128, :], xo[:], reads=[xo])

        ntl = HALF // TT
        norm_tile(0)
        trans_tile(0)
        for it in range(ntl):
            up_tile(it)
            if it + 1 < ntl:
                norm_tile(it + 1)
                trans_tile(it + 1)
            down_tile(it)
        kb.drain_dma()


S5C_COLS = 32 * 4 + 512 * 6


def phase_c(nc, kb, ps, UT, YT, cst, s5c_d):
    NJ = SEQ // 16
    NO = HALF // 16
    with ExitStack() as es:
        sb = lambda name, shape, dt: _sb(nc, es, name, shape, dt)
        cstt = sb("c_cstt", [128, 260], F32)
        identb = sb("c_identb", [128, 128], BF16)
        s5c = sb("s5c_sb", [128, S5C_COLS], F32)
        dt_ = sb("c_dt", [128, 32], F32)
        lrdt = sb("c_lrdt", [128, 32], F32)
        lidt = sb("c_lidt", [128, 32], F32)
        mv = sb("c_mv", [128, 32, 16], F32)
        tA = [sb("c_tA%d" % i, [128, 32, 16], F32) for i in range(6)]
        tI = sb("c_tI", [128, 32, 16], I32)
        PWr = sb("PWr", [128, 32, 17], F32)
        PWi = sb("PWi", [128, 32, 17], F32)
        PVr = sb("PVr", [128, 32, 16], F32)
        PVi = sb("PVi", [128, 32, 16], F32)
        PMr = sb("PMr", [128, 32, 16], F32)
        PMi = sb("PMi", [128, 32, 16], F32)
        Btr = sb("Btr", [128, 32, 16], F32)
        Bti = sb("Bti", [128, 32, 16], F32)
        sm = [sb("c_sm%d" % i, [128, 32], F32) for i in range(8)]
        LVr = sb("LVr", [128, 32, 9], F32)
        LVi = sb("LVi", [128, 32, 9], F32)
        LVs = sb("LVs", [128, 32, 9], F32)
        Sel = sb("Sel", [128, 64, 128], BF16)
        SelR = sb("SelR", [128, 64, 128], BF16)
        NSET = 8
        WT = [sb("WT%d" % i, [128, 16, 16], BF16) for i in range(NSET)]
        Fg = [sb("Fg%d" % i, [128, 16, 16], BF16) for i in range(NSET)]
        Gg = [sb("Gg%d" % i, [128, 16, 16], BF16) for i in range(NSET)]
        Wg = [sb("Wg%d" % i, [128, 2, 128], BF16) for i in range(NSET)]
        M0g = [sb("M0g%d" % i, [128, 2, 256], BF16) for i in range(NSET)]
        ALg = [sb("ALg%d" % i, [128, 9, 128], BF16) for i in range(NSET)]
        ct1 = [sb("ct1_%d" % i, [128, 16, 16], F32) for i in range(2)]
        ct2 = [sb("ct2_%d" % i, [128, 16, 16], F32) for i in range(2)]
        mtmp = sb("mtmp", [128, 512], F32)
        atmp = [sb("atmp%d" % i, [128, 128], F32) for i in range(2)]
        Ug2 = [sb("Ug%d" % i, [128, 2, NJ], BF16) for i in range(8)]
        Ug = list(Ug2[0:4])
        Z = [[sb("Z%d_%d" % (i, j), [128, NJ + 1], BF16) for j in range(2)] for i in range(4)]
        ysb = [sb("ysb%d" % i, [128, NO], F32) for i in range(8)]
        yt_ = [sb("yt%d" % i, [128, NO], F32) for i in range(4)]
        ysg = [sb("ysg%d" % i, [128, NO], F32) for i in range(4)]
        ygel = sb("ygel", [128, 8, 2, NO], BF16)
        YTm = [sb("YTm%d" % i, [128, HALF], BF16) for i in range(1)]

        C_LR, C_LI, C_LS, C_DS = 0, 32, 64, 96
        C_BR, C_BI, C_CR, C_CI, C_MK, C_DI = 128, 640, 1152, 1664, 2176, 2688

        kb.dma("sp", cstt, cstt[:], cst[:, :], writes=[cstt])
        kb.dma("sp", s5c, s5c[:], s5c_d[:, :], writes=[s5c])
        kb.op("dve", lambda e: e.tensor_copy(out=identb[:], in_=cstt[:, 0:128]), reads=[cstt], writes=[identb])
        identf = cstt[:, 0:128]
        swapf = cstt[:, 128:256]

        def V(e, fn, reads, writes):
            return kb.op(e, fn, reads=reads, writes=writes)

        def tt(e, out_b, out_ap, a_b, a_ap, b_b, b_ap, op):
            return kb.op(e, lambda en: en.tensor_tensor(out=out_ap, in0=a_ap, in1=b_ap, op=op),
                         reads=[a_b, b_b], writes=[out_b])

        def bc_g(ap2):
            return ap2.unsqueeze(2).to_broadcast([128, 32, 16])

        V("pool", lambda e: e.memset(Sel[:], 0.0), [], [Sel])
        V("pool", lambda e: e.memset(SelR[:], 0.0), [], [SelR])
        for gl in range(8):
            for s8 in range(8):
                eng = ["dve", "pool"][s8 % 2]
                V(eng, lambda e: e.tensor_copy(out=Sel[:, gl * 8 + s8, s8 * 16:(s8 + 1) * 16],
                                               in_=cstt[:, gl * 16:(gl + 1) * 16]), [cstt, Sel], [Sel])
                V(eng, lambda e: e.tensor_copy(out=SelR[:, gl * 8 + s8, gl * 16:(gl + 1) * 16],
                                               in_=cstt[:, s8 * 16:(s8 + 1) * 16]), [cstt, SelR], [SelR])
        for i in range(4):
            for j in range(2):
                V("pool", lambda e: e.memset(Z[i][j][:, 0:1], 0.0), [], [Z[i][j]])

        V("act", lambda e: e.activation(out=dt_[:], in_=s5c[:, C_LS:C_LS + 32], func=AF.Exp), [s5c], [dt_])
        tt("dve", lrdt, lrdt[:], s5c, s5c[:, C_LR:C_LR + 32], dt_, dt_[:], ALU.mult)
        tt("dve", lidt, lidt[:], s5c, s5c[:, C_LI:C_LI + 32], dt_, dt_[:], ALU.mult)
        for m in range(16):
            V("pool", lambda e: e.memset(mv[:, :, m:m + 1], float(m + 1)), [], [mv])
        ang, kf, rr, rc, mm, mg = tA
        tt("dve", ang, ang[:], mv, mv[:], lidt, bc_g(lidt[:]), ALU.mult)
        tt("dve", mg, mg[:], mv, mv[:], lrdt, bc_g(lrdt[:]), ALU.mult)
        V("act", lambda e: e.activation(out=mg[:], in_=mg[:], func=AF.Exp), [mg], [mg])
        V("dve", lambda e: e.tensor_scalar(out=kf[:], in0=ang[:], scalar1=1.0 / TWO_PI, scalar2=None, op0=ALU.mult),
          [ang], [kf])
        V("dve", lambda e: e.tensor_copy(out=tI[:], in_=kf[:]), [kf], [tI])
        V("dve", lambda e: e.tensor_copy(out=kf[:], in_=tI[:]), [tI], [kf])
        V("dve", lambda e: e.scalar_tensor_tensor(out=rr[:], in0=kf[:], scalar=-TWO_PI, in1=ang[:],
                                                  op0=ALU.mult, op1=ALU.add), [kf, ang], [rr])

        def wrap(dst, src):
            V("dve", lambda e: e.tensor_scalar(out=mm[:], in0=src[:], scalar1=PI, scalar2=-TWO_PI,
                                               op0=ALU.is_gt, op1=ALU.mult), [src], [mm])
            tt("dve", dst, dst[:], src, src[:], mm, mm[:], ALU.add)
            V("dve", lambda e: e.tensor_scalar(out=mm[:], in0=dst[:], scalar1=-PI, scalar2=TWO_PI,
                                               op0=ALU.is_lt, op1=ALU.mult), [dst], [mm])
            tt("dve", dst, dst[:], dst, dst[:], mm, mm[:], ALU.add)
            V("dve", lambda e: e.tensor_scalar(out=dst[:], in0=dst[:], scalar1=3.14159, scalar2=-3.14159,
                                               op0=ALU.min, op1=ALU.max), [dst], [dst])

        wrap(rr, rr)
        V("dve", lambda e: e.tensor_scalar(out=rc[:], in0=rr[:], scalar1=PI / 2, scalar2=None, op0=ALU.add), [rr], [rc])
        wrap(rc, rc)
        V("act", lambda e: e.activation(out=rr[:], in_=rr[:], func=AF.Sin), [rr], [rr])
        V("act", lambda e: e.activation(out=rc[:], in_=rc[:], func=AF.Sin), [rc], [rc])
        V("dve", lambda e: e.memset(PWr[:, :, 0:1], 1.0), [], [PWr])
        V("dve", lambda e: e.memset(PWi[:, :, 0:1], 0.0), [], [PWi])
        tt("dve", PWr, PWr[:, :, 1:17], mg, mg[:], rc, rc[:], ALU.mult)
        tt("dve", PWi, PWi[:, :, 1:17], mg, mg[:], rr, rr[:], ALU.mult)
        for s_ in range(16):
            V("dve", lambda e: e.tensor_copy(out=PVr[:, :, s_:s_ + 1], in_=PWr[:, :, 15 - s_:16 - s_]), [PWr], [PVr])
            V("pool", lambda e: e.tensor_copy(out=PVi[:, :, s_:s_ + 1], in_=PWi[:, :, 15 - s_:16 - s_]), [PWi], [PVi])
        d16, ir, ii, nr, den, cr, ci, tq = sm
        p16r = PWr[:, :, 16]
        p16i = PWi[:, :, 16]
        tt("dve", d16, d16[:], PWr, p16r, PWr, p16r, ALU.mult)
        tt("dve", tq, tq[:], PWi, p16i, PWi, p16i, ALU.mult)
        tt("dve", d16, d16[:], d16, d16[:], tq, tq[:], ALU.add)
        V("dve", lambda e: e.reciprocal(out=d16[:], in_=d16[:]), [d16], [d16])
        tt("dve", ir, ir[:], PWr, p16r, d16, d16[:], ALU.mult)
        tt("dve", ii, ii[:], PWi, p16i, d16, d16[:], ALU.mult)
        V("dve", lambda e: e.tensor_scalar(out=ii[:], in0=ii[:], scalar1=-1.0, scalar2=None, op0=ALU.mult), [ii], [ii])
        x1_, x2_ = tA[0], tA[1]
        tt("dve", x1_, x1_[:], PWr, PWr[:, :, 1:17], ir, bc_g(ir[:]), ALU.mult)
        tt("dve", x2_, x2_[:], PWi, PWi[:, :, 1:17], ii, bc_g(ii[:]), ALU.mult)
        tt("dve", PMr, PMr[:], x1_, x1_[:], x2_, x2_[:], ALU.subtract)
        tt("dve", x1_, x1_[:], PWr, PWr[:, :, 1:17], ii, bc_g(ii[:]), ALU.mult)
        tt("dve", x2_, x2_[:], PWi, PWi[:, :, 1:17], ir, bc_g(ir[:]), ALU.mult)
        tt("dve", PMi, PMi[:], x1_, x1_[:], x2_, x2_[:], ALU.add)
        lamr = s5c[:, C_LR:C_LR + 32]
        lami = s5c[:, C_LI:C_LI + 32]
        V("dve", lambda e: e.tensor_scalar(out=nr[:], in0=PWr[:, :, 1], scalar1=-1.0, scalar2=None, op0=ALU.add),
          [PWr], [nr])
        ni = PWi[:, :, 1]
        tt("dve", den, den[:], s5c, lamr, s5c, lamr, ALU.mult)
        tt("dve", tq, tq[:], s5c, lami, s5c, lami, ALU.mult)
        tt("dve", den, den[:], den, den[:], tq, tq[:], ALU.add)
        V("dve", lambda e: e.reciprocal(out=den[:], in_=den[:]), [den], [den])
        tt("dve", cr, cr[:], nr, nr[:], s5c, lamr, ALU.mult)
        tt("dve", tq, tq[:], PWi, ni, s5c, lami, ALU.mult)
        tt("dve", cr, cr[:], cr, cr[:], tq, tq[:], ALU.add)
        tt("dve", cr, cr[:], cr, cr[:], den, den[:], ALU.mult)
        tt("dve", ci, ci[:], PWi, ni, s5c, lamr, ALU.mult)
        tt("dve", tq, tq[:], nr, nr[:], s5c, lami, ALU.mult)
        tt("dve", ci, ci[:], ci, ci[:], tq, tq[:], ALU.subtract)
        tt("dve", ci, ci[:], ci, ci[:], den, den[:], ALU.mult)
        bre = s5c[:, C_BR:C_BR + 512].rearrange("p (g c) -> p g c", c=16)
        bim = s5c[:, C_BI:C_BI + 512].rearrange("p (g c) -> p g c", c=16)
        tt("dve", x1_, x1_[:], s5c, bre, cr, bc_g(cr[:]), ALU.mult)
        tt("dve", x2_, x2_[:], s5c, bim, ci, bc_g(ci[:]), ALU.mult)
        tt("dve", Btr, Btr[:], x1_, x1_[:], x2_, x2_[:], ALU.subtract)
        tt("dve", x1_, x1_[:], s5c, bim, cr, bc_g(cr[:]), ALU.mult)
        tt("dve", x2_, x2_[:], s5c, bre, ci, bc_g(ci[:]), ALU.mult)
        tt("dve", Bti, Bti[:], x1_, x1_[:], x2_, x2_[:], ALU.add)
        V("dve", lambda e: e.tensor_copy(out=LVr[:, :, 0:1], in_=PWr[:, :, 16:17]), [PWr], [LVr])
        V("dve", lambda e: e.tensor_copy(out=LVi[:, :, 0:1], in_=PWi[:, :, 16:17]), [PWi], [LVi])
        q1, q2 = sm[0], sm[1]
        for l in range(8):
            ar_, ai_ = LVr[:, :, l], LVi[:, :, l]
            tt("dve", q1, q1[:], LVr, ar_, LVr, ar_, ALU.mult)
            tt("dve", q2, q2[:], LVi, ai_, LVi, ai_, ALU.mult)
            tt("dve", LVr, LVr[:, :, l + 1], q1, q1[:], q2, q2[:], ALU.subtract)
            tt("dve", q1, q1[:], LVr, ar_, LVi, ai_, ALU.mult)
            V("dve", lambda e: e.tensor_scalar(out=LVi[:, :, l + 1], in0=q1[:], scalar1=2.0, scalar2=None,
                                               op0=ALU.mult), [q1], [LVi])
        V("dve", lambda e: e.tensor_scalar(out=LVs[:], in0=LVi[:], scalar1=cstt[:, 258:259], scalar2=None,
                                           op0=ALU.mult), [LVi, cstt], [LVs])

        cre = s5c[:, C_CR:C_CR + 512].rearrange("p (g c) -> p g c", c=16)
        cim = s5c[:, C_CI:C_CI + 512].rearrange("p (g c) -> p g c", c=16)
        ncre_b = sb("ncre", [128, 32, 16], F32)
        ncim_b = sb("ncim", [128, 32, 16], F32)
        V("dve", lambda e: e.tensor_scalar(out=ncre_b[:], in0=cre, scalar1=-1.0, scalar2=None, op0=ALU.mult), [s5c], [ncre_b])
        V("dve", lambda e: e.tensor_scalar(out=ncim_b[:], in0=cim, scalar1=-1.0, scalar2=None, op0=ALU.mult), [s5c], [ncim_b])

        def cmat(dst, k, tabr_b, tabr, tabi_b, tabi, mr_b, mr, mi_b, mi, nmr_b=None, nmr=None, nmi_b=None, nmi=None):
            def b_a(ap, lo, hi):
                return ap[lo:hi].unsqueeze(2).to_broadcast([64, 16, 16])

            def b_b(ap, lo, hi):
                return ap[lo:hi].unsqueeze(1).to_broadcast([64, 16, 16])

            t1, t2 = ct1[k], ct2[k]
            tt("dve", t1, t1[0:64], tabr_b, b_a(tabr, 0, 64), mr_b, b_b(mr, 0, 64), ALU.mult)
            tt("dve", t2, t2[0:64], tabi_b, b_a(tabi, 0, 64), mi_b, b_b(mi, 0, 64), ALU.mult)
            tt("dve", dst, dst[0:64], t1, t1[0:64], t2, t2[0:64], ALU.subtract)
            if nmr is not None:
                mr_b, mr, mi_b, mi = nmr_b, nmr, nmi_b, nmi
            tt("pool", t1, t1[64:128], tabr_b, b_a(tabr, 64, 128), mi_b, b_b(mi, 64, 128), ALU.mult)
            tt("pool", t2, t2[64:128], tabi_b, b_a(tabi, 64, 128), mr_b, b_b(mr, 64, 128), ALU.mult)
            tt("pool", dst, dst[64:128], t1, t1[64:128], t2, t2[64:128], ALU.add)

        ev = [0]

        def evac(dst_b, dst_ap, src_b, src_ap):
            ev[0] += 1
            if ev[0] % 2:
                kb.op("act", lambda e: e.activation(out=dst_ap, in_=src_ap, func=AF.Copy), reads=[src_b], writes=[dst_b])
            else:
                kb.op("dve", lambda e: e.tensor_copy(out=dst_ap, in_=src_ap), reads=[src_b], writes=[dst_b])

        def pe(reads, writes, fn):
            kb.sync("pe", reads=reads, writes=writes)
            ins = fn()
            tok = kb.bump("pe", ins)
            kb.note(tok, reads=reads, writes=writes)

        NB = 4
        batches = [(m, half) for m in range(4) for half in range(2)]

        def slots_of(m, half):
            sl = []
            for b in range(NB):
                gl = half * NB + b
                sl.append((b, gl, m * 8 + gl, (half * NB + b) % NSET))
            return sl

        def do_setup(m, half):
            slots = slots_of(m, half)
            for (b, gl, g, k) in slots:
                cmat(WT[k], b % 2, PVr, PVr[:, g, :], PVi, PVi[:, g, :], Btr, Btr[:, g, :], Bti, Bti[:, g, :])
                cmat(Fg[k], b % 2, PWr, PWr[:, g, 1:17], PWi, PWi[:, g, 1:17], s5c, cre[:, g, :], s5c, cim[:, g, :],
                     ncre_b, ncre_b[:, g, :], ncim_b, ncim_b[:, g, :])
                cmat(Gg[k], b % 2, PMr, PMr[:, g, :], PMi, PMi[:, g, :], s5c, cre[:, g, :], s5c, cim[:, g, :],
                     ncre_b, ncre_b[:, g, :], ncim_b, ncim_b[:, g, :])
                wt2 = WT[k][:].rearrange("p a b -> p (a b)")
                gg2 = Gg[k][:].rearrange("p a b -> p (a b)")
                p0 = ps[0]
                p0b = p0[:].bitcast(BF16)

                def f_tr():
                    ins = None
                    for kt in range(2):
                        ins = nc.tensor.transpose(out=p0b[:, kt * 128:(kt + 1) * 128],
                                                  in_=wt2[:, kt * 128:(kt + 1) * 128], identity=identb[:])
                    return ins
                pe([WT[k], identb], [p0], f_tr)
                evac(Wg[k], Wg[k][:].rearrange("p a b -> p (a b)"), p0, p0b[:, 0:256])
                p1 = ps[1]

                def f_m0():
                    ins = None
                    for kt in range(2):
                        ins = nc.tensor.matmul(p1[:, kt * 256:(kt + 1) * 256], lhsT=wt2[:, kt * 128:(kt + 1) * 128],
                                               rhs=gg2, start=True, stop=True)
                    return ins
                pe([WT[k], Gg[k]], [p1], f_m0)
                tt("dve", mtmp, mtmp[:], p1, p1[:], s5c, s5c[:, C_MK:C_MK + 512], ALU.mult)
                kb.op("dve", lambda e: e.scalar_tensor_tensor(out=M0g[k][:].rearrange("p a b -> p (a b)"),
                                                              in0=s5c[:, C_DI:C_DI + 512],
                                                              scalar=s5c[:, C_DS + g:C_DS + g + 1],
                                                              in1=mtmp[:], op0=ALU.mult, op1=ALU.add),
                      reads=[s5c, mtmp], writes=[M0g[k]])
                for l in range(9):
                    at = atmp[l % 2]
                    kb.op("act", lambda e: e.activation(out=at[:], in_=swapf, func=AF.Copy, scale=LVs[:, g, l:l + 1]),
                          reads=[cstt, LVs], writes=[at])
                    kb.op("dve", lambda e: e.scalar_tensor_tensor(out=ALg[k][:, l, :], in0=identf,
                                                                  scalar=LVr[:, g, l:l + 1],
                                                                  in1=at[:], op0=ALU.mult, op1=ALU.add),
                          reads=[cstt, LVr, at], writes=[ALg[k]])

        def do_relayout(m, half, um):
            slots = slots_of(m, half)
            for (b, gl, g, k) in slots:
                for s8 in range(8):
                    kb.dma("sp", Ug[b], Ug[b][s8 * 16:(s8 + 1) * 16, :, :], UT[16 * g:16 * g + 16, s8:16:8, :],
                           writes=[Ug[b]])

        curs = {}

        def do_ds(m, half):
            slots = slots_of(m, half)
            cur = [0] * NB
            for (b, gl, g, k) in slots:
                pz = ps[4 + b]

                def f_ds():
                    ins = None
                    for kt in range(2):
                        ins = nc.tensor.matmul(pz[:], lhsT=Wg[k][:, kt, :], rhs=Ug[b][:, kt, :],
                                               start=(kt == 0), stop=(kt == 1))
                    return ins
                pe([Wg[k], Ug[b]], [pz], f_ds)
                evac(Z[b][0], Z[b][0][:, 1:NJ + 1], pz, pz[:])
            return cur

        def do_ks(m, half, cur):
            slots = slots_of(m, half)
            for l in range(9):
                d = 1 << l
                for (b, gl, g, k) in slots:
                    pz = ps[4 + b]
                    src = Z[b][cur[b]]

                    def f_ks():
                        nc.tensor.matmul(pz[:], lhsT=identb[:], rhs=src[:, 1:NJ + 1], start=True, stop=False)
                        return nc.tensor.matmul(pz[:, d:NJ], lhsT=ALg[k][:, l, :], rhs=src[:, 1:NJ + 1 - d],
                                                start=False, stop=True)
                    pe([identb, ALg[k], src], [pz], f_ks)
                    cur[b] = 1 - cur[b]
                    dstz = Z[b][cur[b]]
                    evac(dstz, dstz[:, 1:NJ + 1], pz, pz[:])
            return cur

        def do_ymm(m, half, cur):
            slots = slots_of(m, half)
            for (b, gl, g, k) in slots:
                zf = Z[b][cur[b]]
                fg2 = Fg[k][:].rearrange("p a b -> p (a b)")
                for mt in range(2):
                    py = ps[4 + b]

                    def f_y():
                        for kt in range(mt + 1):
                            nc.tensor.matmul(py[:, 0:NO], lhsT=M0g[k][:, kt, mt * 128:(mt + 1) * 128],
                                             rhs=Ug[b][:, kt, NO:NJ], start=(kt == 0), stop=False)
                        return nc.tensor.matmul(py[:, 0:NO], lhsT=fg2[:, mt * 128:(mt + 1) * 128], rhs=zf[:, NO:NJ],
                                                start=False, stop=True)
                    pe([M0g[k], Ug[b], Fg[k], zf], [py], f_y)
                    ys = ysb[b * 2 + mt]
                    kb.op("act", lambda e: e.activation(out=ys[:], in_=py[:, 0:NO], func=AF.Copy),
                          reads=[py], writes=[ys])

        def do_gelu(m, half):
            slots = slots_of(m, half)
            for (b, gl, g, k) in slots:
                for mt in range(2):
                    ys = ysb[b * 2 + mt]
                    yi = (b * 2 + mt) % 4
                    yt2, yg = yt_[yi], ysg[yi]
                    tt("dve", yt2, yt2[:], ys, ys[:], ys, ys[:], ALU.mult)
                    kb.op("dve", lambda e: e.tensor_scalar(out=yt2[:], in0=yt2[:], scalar1=0.044715, scalar2=1.0,
                                                           op0=ALU.mult, op1=ALU.add), reads=[yt2], writes=[yt2])
                    tt("dve", yt2, yt2[:], yt2, yt2[:], ys, ys[:], ALU.mult)
                    kb.op("act", lambda e: e.activation(out=yg[:], in_=yt2[:], func=AF.Sigmoid, scale=1.5957691216),
                          reads=[yt2], writes=[yg])
                    tt("pool", ygel, ygel[:, gl, mt, :], ys, ys[:], yg, yg[:], ALU.mult)

        def do_reverse(m, ym):
            for mt in range(2):
                for t8 in range(8):
                    pr = ps[(mt * 8 + t8) % 2]

                    def f_r():
                        ins = None
                        for gl in range(8):
                            ins = nc.tensor.matmul(pr[:, 0:NO], lhsT=SelR[:, gl * 8 + t8, :], rhs=ygel[:, gl, mt, :],
                                                   start=(gl == 0), stop=(gl == 7))
                        return ins
                    pe([SelR, ygel], [pr], f_r)
                    off = mt * 8 + t8
                    evac(ym, ym[:, off:HALF:16], pr, pr[:, 0:NO])
            kb.dma("act", ym, YT[m * 128:(m + 1) * 128, :], ym[:], reads=[ym])

        do_setup(*batches[0])
        um = None
        for bi, (m, half) in enumerate(batches):
            for b_ in range(NB):
                Ug[b_] = Ug2[(bi % 2) * 4 + b_]
            do_relayout(m, half, um)
            cur = do_ds(m, half)
            cur = do_ks(m, half, cur)
            do_ymm(m, half, cur)
            if bi + 1 < len(batches):
                do_setup(*batches[bi + 1])
            do_gelu(m, half)
            if half == 1:
                do_reverse(m, YTm[0])
        kb.drain_dma()


def make_consts():
    c = np.zeros((128, 260), np.float32)
    c[:, 0:128] = np.eye(128, dtype=np.float32)
    for m in range(128):
        c[(m + 64) % 128, 128 + m] = 1.0
    c[:, 256] = np.arange(128) % 64
    c[:, 257] = np.where(np.arange(128) < 64, -1.0, 1.0)
    c[:, 258] = np.where(np.arange(128) < 64, 1.0, -1.0)
    return c


def make_vbq(hf):
    v = np.full((32, 32), -1e30, np.float32)
    for qt in range(32):
        i = qt // 2
        for n in range(32):
            if n < 16 + i and (n >= 16 or hf == 1):
                v[qt, n] = 0.0
    return np.ascontiguousarray(np.broadcast_to(v[None], (128, 32, 32)))


def make_oneh():
    o = np.zeros((32, 32, 128), np.float32)
    for n in range(32):
        o[n, n, :] = 1.0
    return o.reshape(32, 32 * 128)


def rot_perm():
    idx = []
    for base in (0, 1024):
        for h in range(NH):
            for d in range(HD):
                idx.append(base + h * HD + (d + 64) % HD)
    return np.array(idx)


def make_s5c(inputs):
    p = np.arange(128)
    n = p % 64
    lam_re = f32c(inputs["lam_re"])[0]
    lam_im = f32c(inputs["lam_im"])[0]
    log_step = f32c(inputs["log_step"])[0]
    b_re = f32c(inputs["b_re"])[0]
    b_im = f32c(inputs["b_im"])[0]
    c_re = f32c(inputs["c_re"])[0]
    c_im = f32c(inputs["c_im"])[0]
    d_skip = f32c(inputs["d_skip"])[0]
    out = np.zeros((128, S5C_COLS), np.float32)
    out[:, 0:32] = lam_re[:, n].T
    out[:, 32:64] = lam_im[:, n].T
    out[:, 64:96] = log_step[None, :]
    out[:, 96:128] = d_skip[:, p % 16].T
    out[:, 128:640] = b_re[:, n, :].transpose(1, 0, 2).reshape(128, 512)
    out[:, 640:1152] = b_im[:, n, :].transpose(1, 0, 2).reshape(128, 512)
    out[:, 1152:1664] = c_re[:, :, n].transpose(2, 0, 1).reshape(128, 512)
    out[:, 1664:2176] = c_im[:, :, n].transpose(2, 0, 1).reshape(128, 512)
    mk = np.zeros((128, 2, 16, 16), np.float32)
    di = np.zeros((128, 2, 256), np.float32)
    for r in range(128):
        s8 = r // 16
        for kt in range(2):
            sfull = kt * 8 + s8
            mk[r, kt, sfull:, :] = 1.0
            di[r, kt, kt * 128 + r] = 1.0
    out[:, 2176:2688] = mk.reshape(128, 512)
    out[:, 2688:3200] = di.reshape(128, 512)
    return out


def f32c(a):
    return np.ascontiguousarray(np.asarray(a, np.float32))


def make_in_maps(inputs):
    x = np.asarray(inputs["x"], np.float32)
    w_in = np.ascontiguousarray(np.asarray(inputs["w_in"], np.float32)[0])
    s5c = make_s5c(inputs)
    maps = []
    for core in range(8):
        b, hf = core // 2, core % 2
        xa = np.zeros((SEQ, D), np.float32)
        if hf == 1:
            xa[:] = x[b]
            pos = np.arange(SEQ, dtype=np.float32)
        else:
            xa[HALF:] = x[b, :HALF]
            pos = np.concatenate([np.zeros(HALF, np.float32), np.arange(HALF, dtype=np.float32)])
        maps.append({
            "x_all": xa, "w_in": w_in,
            "g_mix": np.ascontiguousarray(np.asarray(inputs["norm_mix_g"], np.float32)[0].reshape(8, 128).T),
            "pos_all": pos, "cst": make_consts(),
            "vbq": make_vbq(hf), "tri": np.triu(np.ones((128, 128), np.float32)),
            "w_glu": f32c(inputs["w_glu"][0]), "w_out": f32c(inputs["w_out"][0]),
            "w_up": f32c(inputs["w_up"][0]), "w_down": f32c(inputs["w_down"][0]),
            "g_mlp": np.ascontiguousarray(np.asarray(inputs["norm_mlp_g"], np.float32)[0].reshape(8, 128).T),
            "g_fin": f32c(inputs["norm_final_g"]), "s5c": s5c,
        })
    return maps


def kernel(**inputs):
    nc = build(debug=False)
    maps = make_in_maps(inputs)
    res = run_bass_kernel_spmd(nc, maps, core_ids=list(range(8)))
    out = np.zeros((4, SEQ, D), np.float32)
    for core in range(8):
        b, hf = core // 2, core % 2
        out[b, hf * HALF:(hf + 1) * HALF] = np.asarray(res.results[core]["out"], np.float32)
    return out
```

```python
import math
from contextlib import ExitStack

import numpy as np
import concourse.bass as bass
import concourse.mybir as mybir
from concourse.bass_utils import run_bass_kernel_spmd

F32 = mybir.dt.float32
BF16 = mybir.dt.bfloat16
I32 = mybir.dt.int32
ALU = mybir.AluOpType
AF = mybir.ActivationFunctionType
AX = mybir.AxisListType

D = 1024
SEQ = 8192
HALF = 4096
NH = 8
HD = 128
INW = 5632
SSMW = 512
DFF = 4096
EPS = 1e-6
TT = 512
PI = math.pi
TWO_PI = 2.0 * math.pi
MASKV = 32768.0


class Buf:
    __slots__ = ("t", "w", "r", "name", "psum")

    def __init__(self, t, name="", psum=False):
        self.t = t
        self.w = {}
        self.r = {}
        self.name = name
        self.psum = psum

    def __getitem__(self, idx):
        return self.t[idx]


class KB:
    def __init__(self, nc, es):
        self.nc = nc
        self.es = es
        self.E = {"pe": nc.tensor, "act": nc.scalar, "dve": nc.vector,
                  "pool": nc.gpsimd, "sp": nc.sync}
        self.sem = {}
        self.cnt = {}
        self.seen = {}
        for n in ["pe", "act", "dve", "pool"]:
            self.sem[n] = es.enter_context(nc.semaphore("s_" + n))
            self.cnt[n] = 0
        self.ndq = 0

    def new_dma_sem(self, name):
        s = "dq_" + name
        self.sem[s] = self.es.enter_context(self.nc.semaphore(s))
        self.cnt[s] = 0
        return s

    def mult(self, s):
        return 16 if s.startswith("dq_") else 1

    def wait(self, e, tok):
        s, v = tok
        if s == e and e == "pe":
            return
        key = (e, s)
        if self.seen.get(key, 0) >= v:
            return
        self.seen[key] = v
        self.E[e].wait_ge(self.sem[s], v * self.mult(s))

    def sync(self, e, reads=(), writes=(), deps=(), issuer=None):
        we = issuer or e
        for b in reads:
            for s, v in b.w.items():
                self.wait(we, (s, v))
            if b.psum:
                for s, v in b.r.items():
                    if s != e:
                        self.wait(we, (s, v))
        for b in writes:
            for s, v in b.w.items():
                if s != e or s.startswith("dq_"):
                    self.wait(we, (s, v))
            for s, v in b.r.items():
                if s != e or s.startswith("dq_"):
                    self.wait(we, (s, v))
        for t in deps:
            if t is not None:
                self.wait(we, t)

    def note(self, tok, reads=(), writes=()):
        s, v = tok
        for b in reads:
            b.r[s] = v
        for b in writes:
            if b.r:
                b.w = {s: v}
                b.r = {}
            else:
                b.w[s] = v

    def bump(self, e, ins):
        self.cnt[e] += 1
        ins.then_inc(self.sem[e], self.mult(e))
        return (e, self.cnt[e])

    def op(self, e, fn, reads=(), writes=(), deps=()):
        self.sync(e, reads, writes, deps)
        ins = fn(self.E[e])
        tok = self.bump(e, ins)
        self.note(tok, reads, writes)
        return tok

    def dma(self, q, sbuf_buf, out, in_, reads=(), writes=(), deps=()):
        dsem = "dq_" + sbuf_buf.name
        if dsem not in self.sem:
            self.new_dma_sem(sbuf_buf.name)
        self.sync(dsem, reads, writes, deps, issuer=q)
        ins = self.E[q].dma_start(out=out, in_=in_)
        tok = self.bump(dsem, ins)
        self.note(tok, reads, writes)
        return tok

    def drain_dma(self, engines=("sp", "act")):
        for s, v in self.cnt.items():
            if s.startswith("dq_") and v > 0:
                for e in engines:
                    self.wait(e, (s, v))

    def barrier(self):
        toks = [(s, v) for s, v in self.cnt.items() if v > 0]
        for e in ["pe", "act", "dve", "pool", "sp"]:
            for t in toks:
                if t[0] != e:
                    self.wait(e, t)


def _sb(nc, es, name, shape, dt):
    return Buf(es.enter_context(nc.sbuf_tensor(name, shape, dt)), name)


STOP = [None]
PHASES = set("ABCDE")


class _Stop(Exception):
    pass


def _chk(tag):
    if STOP[0] == tag:
        raise _Stop()


def build(debug=False):
    nc = bass.Bass("TRN2", target_bir_lowering=False)
    dk = "ExternalOutput" if debug else "Internal"

    x_all = nc.dram_tensor("x_all", [SEQ, D], F32, kind="ExternalInput").ap()
    w_in = nc.dram_tensor("w_in", [D, INW], F32, kind="ExternalInput").ap()
    g_mix = nc.dram_tensor("g_mix", [128, 8], F32, kind="ExternalInput").ap()
    pos_all = nc.dram_tensor("pos_all", [SEQ], F32, kind="ExternalInput").ap()
    cst = nc.dram_tensor("cst", [128, 260], F32, kind="ExternalInput").ap()

    KT = nc.dram_tensor("KT", [NH, HD, SEQ], BF16, kind=dk).ap()
    QT = nc.dram_tensor("QT", [NH, HD, HALF], BF16, kind=dk).ap()
    Vs = nc.dram_tensor("Vs", [SEQ, D], BF16, kind=dk).ap()
    UT = nc.dram_tensor("UT", [SSMW, 16, SEQ // 16], BF16, kind=dk).ap()
    Gs = nc.dram_tensor("Gs", [HALF, 2 * D], BF16, kind=dk).ap()
    Os = nc.dram_tensor("Os", [HALF, D], BF16, kind=dk).ap()
    vbq_d = nc.dram_tensor("vbq", [128, 32, 32], F32, kind="ExternalInput").ap()
    SEL = nc.dram_tensor("SEL", [NH, 16, 32 * 256], BF16, kind="Internal").ap()
    tri_d = nc.dram_tensor("tri", [128, 128], F32, kind="ExternalInput").ap()
    w_glu = nc.dram_tensor("w_glu", [SSMW, 2 * D], F32, kind="ExternalInput").ap()
    w_out = nc.dram_tensor("w_out", [D, D], F32, kind="ExternalInput").ap()
    w_up = nc.dram_tensor("w_up", [D, DFF], F32, kind="ExternalInput").ap()
    w_down = nc.dram_tensor("w_down", [DFF, D], F32, kind="ExternalInput").ap()
    g_mlp = nc.dram_tensor("g_mlp", [128, 8], F32, kind="ExternalInput").ap()
    g_fin = nc.dram_tensor("g_fin", [D], F32, kind="ExternalInput").ap()
    YT = nc.dram_tensor("YT", [SSMW, HALF], BF16, kind=dk).ap()
    WB = {
        "glu": nc.dram_tensor("WB_glu", [128, 4 * 2 * D], BF16, kind="Internal").ap(),
        "out": nc.dram_tensor("WB_out", [128, 8 * D], BF16, kind="Internal").ap(),
        "up": nc.dram_tensor("WB_up", [128, 8 * DFF], BF16, kind="Internal").ap(),
        "down": nc.dram_tensor("WB_down", [128, 32 * D], BF16, kind="Internal").ap(),
    }
    WSRC = {"glu": (w_glu, 4, 2 * D), "out": (w_out, 8, D), "up": (w_up, 8, DFF), "down": (w_down, 32, D)}
    X1 = nc.dram_tensor("X1", [HALF, D], F32, kind=dk).ap()
    out = nc.dram_tensor("out", [HALF, D], F32, kind="ExternalOutput").ap()
    s5c_d = nc.dram_tensor("s5c", [128, S5C_COLS], F32, kind="ExternalInput").ap()

    with ExitStack() as es:
        kb = KB(nc, es)
        ps = [Buf(es.enter_context(nc.psum_tensor("ps%d" % i, [128, 512], F32)), "ps%d" % i, psum=True)
              for i in range(8)]
        if "A" in PHASES:
            try:
                phase_a(nc, kb, ps, x_all, w_in, g_mix, pos_all, cst, KT, QT, Vs, UT, Gs)
            except _Stop:
                pass
            kb.barrier()
        if "B" in PHASES:
            phase_b(nc, kb, ps, KT, QT, Vs, Os, cst, vbq_d, SEL, tri_d, WB, WSRC)
            kb.barrier()
        if "C" in PHASES:
            phase_c(nc, kb, ps, UT, YT, cst, s5c_d)
            kb.barrier()
        with ExitStack() as es2:
            wu = _sb(nc, es2, "wu", [128, 8, DFF], BF16)
            wd = _sb(nc, es2, "wd", [128, 32, D], BF16)

            def mlp_w_gen():
                for c0 in range(0, 8, 2):
                    kb.dma("sp", wu, wu[:, c0:c0 + 2, :].rearrange("p c n -> p (c n)"),
                           WB["up"][:, c0 * DFF:(c0 + 2) * DFF], writes=[wu])
                    yield
                for c0 in range(0, 32, 8):
                    kb.dma("sp", wd, wd[:, c0:c0 + 8, :].rearrange("p c n -> p (c n)"),
                           WB["down"][:, c0 * D:(c0 + 8) * D], writes=[wd])
                    yield
            wgen = mlp_w_gen()

            def load_mlp_w(n=100):
                for _ in range(n):
                    try:
                        next(wgen)
                    except StopIteration:
                        return
            loaded = False
            if "D" in PHASES:
                phase_d1(nc, kb, ps, x_all, WB, cst, YT, Os, Gs, X1, load_mlp_w)
                loaded = True
                kb.barrier()
            if "E" in PHASES:
                if not loaded:
                    load_mlp_w()
                phase_d2(nc, kb, ps, wu, wd, g_mlp, g_fin, cst, X1, out)
                kb.barrier()
    return nc


def phase_a(nc, kb, ps, x_all, w_in, g_mix, pos_all, cst, KT, QT, Vs, UT, Gs):
    with ExitStack() as es:
        sb = lambda name, shape, dt: _sb(nc, es, name, shape, dt)
        wb = sb("wb", [128, 8, INW], BF16)
        stg = [sb("stg%d" % i, [128, 8, 128], F32) for i in range(2)]
        cstt = sb("cstt", [128, 260], F32)
        identb = sb("identb", [128, 128], BF16)
        swapb = sb("swapb", [128, 128], BF16)
        gcol = sb("gcol", [128, 8], F32)
        invf = sb("invf", [128, 1], F32)
        xt = [sb("xt%d" % i, [128, D], F32) for i in range(4)]
        junk = sb("junk", [128, D], BF16)
        ss = [sb("ss%d" % i, [128, 4], F32) for i in range(2)]
        rstd = [sb("rstd%d" % i, [128, 4], F32) for i in range(2)]
        hb = [sb("hb%d" % i, [128, 4, D], BF16) for i in range(1)]
        hT = [sb("hT%d" % i, [128, 8, TT], BF16) for i in range(2)]
        posb = [sb("posb%d" % i, [128, TT], F32) for i in range(1)]
        ang = sb("ang", [128, TT], F32)
        kf = sb("kf", [128, TT], F32)
        ki = sb("ki", [128, TT], I32)
        rr = sb("rr", [128, TT], F32)
        rc = sb("rc", [128, TT], F32)
        mm = sb("mm", [128, TT], F32)
        cosT = [sb("cosT%d" % i, [128, TT], F32) for i in range(1)]
        sinT = [sb("sinT%d" % i, [128, TT], F32) for i in range(1)]
        t1 = [sb("t1_%d" % i, [128, TT], F32) for i in range(3)]
        t2 = [sb("t2_%d" % i, [128, TT], F32) for i in range(3)]
        qraw = [sb("qraw%d" % i, [128, TT], BF16) for i in range(3)]
        kqst = [sb("kqst%d" % i, [128, NH, TT], BF16) for i in range(2)]
        vst = [sb("vst%d" % i, [128, D], BF16) for i in range(2)]
        ust = [sb("ust%d" % i, [128, 4, TT], BF16) for i in range(1)]
        gst = [sb("gst%d" % i, [128, 2 * D], BF16) for i in range(2)]

        kb.dma("sp", cstt, cstt[:], cst[:, :], writes=[cstt])
        kb.dma("sp", gcol, gcol[:], g_mix[:, :], writes=[gcol])
        kb.op("dve", lambda e: e.tensor_copy(out=identb[:], in_=cstt[:, 0:128]), reads=[cstt], writes=[identb])
        kb.op("dve", lambda e: e.tensor_copy(out=swapb[:], in_=cstt[:, 128:256]), reads=[cstt], writes=[swapb])
        kb.op("act", lambda e: e.activation(out=invf[:], in_=cstt[:, 256:257], func=AF.Exp,
                                            scale=-math.log(10000.0) / 64.0), reads=[cstt], writes=[invf])

        wv = w_in.rearrange("(c p) n -> p c n", p=128)
        wbq = Buf(wb.t, "wbq")

        def wsel(col0):
            return wb if 1024 <= col0 < 3584 else wbq
        wci = [0]

        def conv_w(n0):
            ci = wci[0]
            wci[0] += 1
            st = stg[ci % 2]
            kb.dma("sp", st, st[:], wv[:, :, n0:n0 + 128], writes=[st])
            eng = ["dve", "pool"][ci % 2]
            kb.op(eng, lambda e: e.tensor_copy(out=wb[:, :, n0:n0 + 128], in_=st[:]), reads=[st], writes=[wsel(n0)])
        for n0 in range(1024, 3584, 128):
            conv_w(n0)
        late_cols = list(range(0, 1024, 128)) + list(range(3584, INW, 128))

        _chk('w')
        QC, KC, VC, UC, GC = 0, 1024, 2048, 3072, 3584
        evac_rr = [0]

        def evac_engine():
            evac_rr[0] += 1
            return ["act", "dve"][evac_rr[0] % 2]

        psrot = [0]

        def next_ps():
            psrot[0] = (psrot[0] + 1) % 6
            return ps[2 + psrot[0]]

        def evac_copy(dst_buf, dst_ap, pbuf):
            eng = evac_engine()
            if eng == "act":
                kb.op("act", lambda e: e.activation(out=dst_ap, in_=pbuf[:], func=AF.Copy),
                      reads=[pbuf], writes=[dst_buf])
            else:
                kb.op("dve", lambda e: e.tensor_copy(out=dst_ap, in_=pbuf[:]), reads=[pbuf], writes=[dst_buf])

        ntiles = SEQ // TT
        xcount = [0]
        kq_i = [0]
        rot_i = [0]
        v_i = [0]
        g_i = [0]
        hbt = hb[0]
        pb_ = posb[0]

        def norm_tile(it):
            p2 = it % 2
            tok0 = it * TT
            for s in range(4):
                xs = xt[xcount[0] % 4]
                xcount[0] += 1
                kb.dma("sp", xs, xs[:], x_all[tok0 + s * 128:tok0 + (s + 1) * 128, :], writes=[xs])
                kb.op("act", lambda e: e.activation(out=junk[:], in_=xs[:], func=AF.Square,
                                                    accum_out=ss[p2][:, s:s + 1]),
                      reads=[xs], writes=[junk, ss[p2]])
                kb.op("dve", lambda e: e.tensor_scalar(out=rstd[p2][:, s:s + 1], in0=ss[p2][:, s:s + 1],
                                                       scalar1=1.0 / D, scalar2=EPS,
                                                       op0=ALU.mult, op1=ALU.add), reads=[ss[p2]], writes=[rstd[p2]])
                kb.op("act", lambda e: e.activation(out=rstd[p2][:, s:s + 1], in_=rstd[p2][:, s:s + 1], func=AF.Sqrt),
                      reads=[rstd[p2]], writes=[rstd[p2]])
                kb.op("dve", lambda e: e.reciprocal(out=rstd[p2][:, s:s + 1], in_=rstd[p2][:, s:s + 1]),
                      reads=[rstd[p2]], writes=[rstd[p2]])
                if s % 2 == 0:
                    kb.op("dve", lambda e: e.tensor_scalar(out=hbt[:, s, :], in0=xs[:],
                                                           scalar1=rstd[p2][:, s:s + 1], scalar2=None, op0=ALU.mult),
                          reads=[xs, rstd[p2]], writes=[hbt])
                else:
                    kb.op("act", lambda e: e.activation(out=hbt[:, s, :], in_=xs[:], func=AF.Copy,
                                                        scale=rstd[p2][:, s:s + 1]),
                          reads=[xs, rstd[p2]], writes=[hbt])

        def rope_tables(it):
            tok0 = it * TT
            kb.dma("sp", pb_, pb_[:], pos_all[tok0:tok0 + TT].partition_broadcast(128), writes=[pb_])
            kb.op("dve", lambda e: e.tensor_scalar(out=ang[:], in0=pb_[:], scalar1=invf[:, 0:1], scalar2=None,
                                                   op0=ALU.mult), reads=[pb_, invf], writes=[ang])
            kb.op("dve", lambda e: e.tensor_scalar(out=kf[:], in0=ang[:], scalar1=1.0 / TWO_PI, scalar2=None,
                                                   op0=ALU.mult), reads=[ang], writes=[kf])
            kb.op("dve", lambda e: e.tensor_copy(out=ki[:], in_=kf[:]), reads=[kf], writes=[ki])
            kb.op("dve", lambda e: e.tensor_copy(out=kf[:], in_=ki[:]), reads=[ki], writes=[kf])
            kb.op("dve", lambda e: e.scalar_tensor_tensor(out=rr[:], in0=kf[:], scalar=-TWO_PI, in1=ang[:],
                                                          op0=ALU.mult, op1=ALU.add), reads=[kf, ang], writes=[rr])

            def wrap(dst, src):
                kb.op("dve", lambda e: e.tensor_scalar(out=mm[:], in0=src[:], scalar1=PI, scalar2=-TWO_PI,
                                                       op0=ALU.is_gt, op1=ALU.mult), reads=[src], writes=[mm])
                kb.op("dve", lambda e: e.tensor_tensor(out=dst[:], in0=src[:], in1=mm[:], op=ALU.add),
                      reads=[src, mm], writes=[dst])
                kb.op("dve", lambda e: e.tensor_scalar(out=mm[:], in0=dst[:], scalar1=-PI, scalar2=TWO_PI,
                                                       op0=ALU.is_lt, op1=ALU.mult), reads=[dst], writes=[mm])
                kb.op("dve", lambda e: e.tensor_tensor(out=dst[:], in0=dst[:], in1=mm[:], op=ALU.add),
                      reads=[dst, mm], writes=[dst])
                kb.op("dve", lambda e: e.tensor_scalar(out=dst[:], in0=dst[:], scalar1=3.14159, scalar2=-3.14159,
                                                       op0=ALU.min, op1=ALU.max), reads=[dst], writes=[dst])

            wrap(rr, rr)
            kb.op("dve", lambda e: e.tensor_scalar(out=rc[:], in0=rr[:], scalar1=PI / 2, scalar2=None, op0=ALU.add),
                  reads=[rr], writes=[rc])
            wrap(rc, rc)
            kb.op("act", lambda e: e.activation(out=sinT[0][:], in_=rr[:], func=AF.Sin, scale=cstt[:, 257:258]),
                  reads=[rr, cstt], writes=[sinT[0]])
            kb.op("act", lambda e: e.activation(out=cosT[0][:], in_=rc[:], func=AF.Sin),
                  reads=[rc], writes=[cosT[0]])

        def transposes(it):
            p2 = it % 2
            for c in range(8):
                pt = ps[c % 2]
                ptb = pt[:].bitcast(BF16)
                kb.sync("pe", reads=[hbt, identb], writes=[pt])
                ins = None
                for s in range(4):
                    ins = nc.tensor.transpose(out=ptb[:, s * 128:(s + 1) * 128],
                                              in_=hbt[:, s, c * 128:(c + 1) * 128], identity=identb[:])
                tok = kb.bump("pe", ins)
                kb.note(tok, reads=[hbt, identb], writes=[pt])
                eng = evac_engine()
                if eng == "act":
                    kb.op("act", lambda e: e.activation(out=hT[p2][:, c, :], in_=ptb[:, 0:TT], func=AF.Copy,
                                                        scale=gcol[:, c:c + 1]),
                          reads=[pt, gcol], writes=[hT[p2]])
                else:
                    kb.op("dve", lambda e: e.tensor_scalar(out=hT[p2][:, c, :], in0=ptb[:, 0:TT],
                                                           scalar1=gcol[:, c:c + 1], scalar2=None, op0=ALU.mult),
                          reads=[pt, gcol], writes=[hT[p2]])

        def fm_group(p2, col0, pbuf):
            wb_ = wsel(col0)
            kb.sync("pe", reads=[wb_, hT[p2]], writes=[pbuf])
            ins = None
            for c in range(8):
                ins = nc.tensor.matmul(pbuf[:], lhsT=wb[:, c, col0:col0 + 128], rhs=hT[p2][:, c, :],
                                       start=(c == 0), stop=(c == 7))
            tok = kb.bump("pe", ins)
            kb.note(tok, reads=[wb_, hT[p2]], writes=[pbuf])

        def tm_group(p2, s, col0, pbuf):
            wb_ = wsel(col0)
            kb.sync("pe", reads=[wb_, hT[p2]], writes=[pbuf])
            ins = None
            for c in range(8):
                ins = nc.tensor.matmul(pbuf[:], lhsT=hT[p2][:, c, s * 128:(s + 1) * 128],
                                       rhs=wb[:, c, col0:col0 + 512], start=(c == 0), stop=(c == 7))
            tok = kb.bump("pe", ins)
            kb.note(tok, reads=[wb_, hT[p2]], writes=[pbuf])

        def rope_heads(p2, col_base, dst):
            pend = None

            def finish(pa, j, h):
                pb = next_ps()
                kb.sync("pe", reads=[qraw[j], swapb], writes=[pb])
                ins = nc.tensor.matmul(pb[:], lhsT=swapb[:], rhs=qraw[j][:], start=True, stop=True)
                tok = kb.bump("pe", ins)
                kb.note(tok, reads=[qraw[j], swapb], writes=[pb])
                kb.op("dve", lambda e: e.tensor_tensor(out=t1[j][:], in0=pa[:], in1=cosT[0][:], op=ALU.mult),
                      reads=[pa, cosT[0]], writes=[t1[j]])
                kb.op("dve", lambda e: e.tensor_tensor(out=t2[j][:], in0=pb[:], in1=sinT[0][:], op=ALU.mult),
                      reads=[pb, sinT[0]], writes=[t2[j]])
                kb.op("pool", lambda e: e.tensor_tensor(out=dst[:, h, :], in0=t1[j][:], in1=t2[j][:], op=ALU.add),
                      reads=[t1[j], t2[j]], writes=[dst])

            pend = []
            for h in range(NH):
                pa = next_ps()
                fm_group(p2, col_base + h * 128, pa)
                j = rot_i[0] % 3
                rot_i[0] += 1
                kb.op("act", lambda e: e.activation(out=qraw[j][:], in_=pa[:], func=AF.Copy),
                      reads=[pa], writes=[qraw[j]])
                pend.append((pa, j, h))
                if len(pend) > 2:
                    finish(*pend.pop(0))
            while pend:
                finish(*pend.pop(0))

        def proj_k(it):
            p2 = it % 2
            tok0 = it * TT
            kst = kqst[kq_i[0] % 2]
            kq_i[0] += 1
            rope_heads(p2, KC, kst)
            kb.dma("act", kst, KT[:, :, tok0:tok0 + TT].rearrange("h d t -> d h t"), kst[:], reads=[kst])

        def proj_rest(it):
            own = it >= ntiles // 2
            p2 = it % 2
            tok0 = it * TT
            for s in range(4):
                vb = vst[v_i[0] % 2]
                v_i[0] += 1
                for hf in range(2):
                    pbuf = next_ps()
                    tm_group(p2, s, VC + hf * 512, pbuf)
                    evac_copy(vb, vb[:, hf * 512:(hf + 1) * 512], pbuf)
                kb.dma("act", vb, Vs[tok0 + s * 128:tok0 + (s + 1) * 128, :], vb[:], reads=[vb])
            j0 = tok0 // 16
            for m in range(4):
                pbuf = next_ps()
                fm_group(p2, UC + m * 128, pbuf)
                src = pbuf[:].rearrange("p (j s) -> p s j", s=16)
                dstv = ust[0][:, m, :].rearrange("p (s j) -> p s j", s=16)
                if evac_engine() == "act":
                    kb.op("act", lambda e: e.activation(out=dstv, in_=src, func=AF.Copy), reads=[pbuf], writes=[ust[0]])
                else:
                    kb.op("dve", lambda e: e.tensor_copy(out=dstv, in_=src), reads=[pbuf], writes=[ust[0]])
            for m in range(4):
                usrc = ust[0][:, m, :].rearrange("p (s j) -> p s j", s=16)
                for s0 in (0, 8):
                    kb.dma("act", ust[0], UT[m * 128:(m + 1) * 128, s0:s0 + 8, j0:j0 + TT // 16],
                           usrc[:, s0:s0 + 8, :], reads=[ust[0]])
            if own:
                o0 = tok0 - HALF
                qst = kqst[kq_i[0] % 2]
                kq_i[0] += 1
                rope_heads(p2, QC, qst)
                kb.dma("act", qst, QT[:, :, o0:o0 + TT].rearrange("h d t -> d h t"), qst[:], reads=[qst])
                for s in range(4):
                    gb = gst[g_i[0] % 2]
                    g_i[0] += 1
                    for cb in range(4):
                        pbuf = next_ps()
                        tm_group(p2, s, GC + cb * 512, pbuf)
                        kb.op("act", lambda e: e.activation(out=gb[:, cb * 512:(cb + 1) * 512], in_=pbuf[:],
                                                            func=AF.Sigmoid), reads=[pbuf], writes=[gb])
                    kb.dma("act", gb, Gs[o0 + s * 128:o0 + (s + 1) * 128, :], gb[:], reads=[gb])

        norm_tile(0)
        transposes(0)
        for it in range(ntiles):
            rope_tables(it)
            if it + 1 < ntiles:
                norm_tile(it + 1)
            proj_k(it)
            if it + 1 < ntiles:
                transposes(it + 1)
            for _ in range(3):
                if late_cols:
                    conv_w(late_cols.pop(0))
            proj_rest(it)
        kb.drain_dma()


def phase_b(nc, kb, ps, KT, QT, Vs, Os, cst, vbq_d, SEL, tri_d, WB, WSRC):
    SCALE = 1.0 / math.sqrt(HD)
    with ExitStack() as es:
        sb = lambda name, shape, dt: _sb(nc, es, name, shape, dt)
        kth = [sb("kth%d" % i, [128, SEQ], BF16) for i in range(2)]
        vh = [sb("vh%d" % i, [128, 64, 129], BF16) for i in range(2)]
        qth = [sb("qth%d" % i, [128, HALF], BF16) for i in range(2)]
        cstt = sb("b_cstt", [128, 260], F32)
        identb = sb("b_identb", [128, 128], BF16)
        vbq = sb("vbq_sb", [128, 32, 32], F32)
        selr = [sb("selr%d" % i, [128, 32, 256], BF16) for i in range(2)]
        seld = [Buf(None, "seld%d" % i) for i in range(2)]
        trif = sb("trif", [128, 128], F32)
        trib = sb("trib", [128, 128], BF16)
        kmf = sb("kmf", [128, 32], F32)
        kmb = sb("kmb", [128, 32], BF16)
        gv = sb("gv", [128, 32, 32], F32)
        mx = sb("mx", [128, 32, 8], F32)
        thr = sb("thr", [128, 32], F32)
        biasb = sb("biasb", [128, 32, 32], BF16)
        biasT = sb("biasT", [32, HALF], BF16)
        ptsb = [sb("ptsb%d" % i, [128, 512], BF16) for i in range(20)]
        pst_bufs = [ps[0], ps[1], ps[7]]
        rec = sb("rec", [128, 2], F32)
        ost = [sb("ost%d" % i, [128, 32, 128], BF16) for i in range(2)]

        kb.dma("sp", cstt, cstt[:], cst[:, :], writes=[cstt])
        kb.dma("sp", vbq, vbq[:], vbq_d[:, :, :], writes=[vbq])
        kb.dma("sp", trif, trif[:], tri_d[:, :], writes=[trif])
        kb.op("dve", lambda e: e.tensor_copy(out=identb[:], in_=cstt[:, 0:128]), reads=[cstt], writes=[identb])
        kb.op("dve", lambda e: e.tensor_copy(out=trib[:], in_=trif[:]), reads=[trif], writes=[trib])
        for i in range(2):
            kb.op("pool", lambda e: e.memset(vh[i][:, :, 128:129], 1.0), writes=[vh[i]])

        def load_head(h):
            p = h % 2
            kb.dma("sp", kth[p], kth[p][:], KT[h, :, :], writes=[kth[p]])
            kb.dma("sp", qth[p], qth[p][:], QT[h, :, :], writes=[qth[p]])
            vsrc = Vs[:, h * 128:(h + 1) * 128].rearrange("(t p) c -> p t c", p=128)
            for t0_ in range(0, 64, 8):
                kb.dma("sp", vh[p], vh[p][:, t0_:t0_ + 8, 0:128], vsrc[:, t0_:t0_ + 8, :], writes=[vh[p]])

        cstg = [sb("cstg%d" % i, [128, 2, 512], F32) for i in range(2)]
        cbf = [sb("cbf%d" % i, [128, 2, 512], BF16) for i in range(2)]

        def conv_gen():
            ci = 0
            for name in ("glu", "out", "up", "down"):
                wsrc, kc, ncols = WSRC[name]
                wv = wsrc.rearrange("(c p) n -> p c n", p=128)
                dst = WB[name].rearrange("p (c n) -> p c n", n=ncols)
                for k0 in range(0, kc, 2):
                    for n0 in range(0, ncols, 512):
                        st, bf = cstg[ci % 2], cbf[ci % 2]
                        kb.dma("sp", st, st[:], wv[:, k0:k0 + 2, n0:n0 + 512], writes=[st])
                        kb.op("dve", lambda e: e.tensor_copy(out=bf[:], in_=st[:]), reads=[st], writes=[bf])
                        kb.dma("sp", bf, dst[:, k0:k0 + 2, n0:n0 + 512], bf[:], reads=[bf])
                        ci += 1
                        yield
        conv = conv_gen()

        def conv_step(n):
            for _ in range(n):
                try:
                    next(conv)
                except StopIteration:
                    return

        load_head(0)
        unit = [0]
        pcount = [0]
        for h in range(NH):
            p = h % 2
            if h + 1 < NH:
                load_head(h + 1)
            K_, V_, Q_ = kth[p], vh[p], qth[p]
            kb.op("dve", lambda e: e.tensor_reduce(out=kmf[:], in_=K_[:].rearrange("p (n k) -> p n k", k=256),
                                                   axis=AX.X, op=ALU.add), reads=[K_], writes=[kmf])
            kb.op("dve", lambda e: e.tensor_scalar(out=kmb[:], in0=kmf[:], scalar1=1.0 / 256.0, scalar2=None,
                                                   op0=ALU.mult), reads=[kmf], writes=[kmb])
            for bnk in range(2):
                pg = ps[6]
                kb.sync("pe", reads=[Q_, kmb], writes=[pg])
                ins = None
                for j in range(16):
                    qt = bnk * 16 + j
                    ins = nc.tensor.matmul(pg[:, j * 32:(j + 1) * 32], lhsT=Q_[:, qt * 128:(qt + 1) * 128],
                                           rhs=kmb[:, :], start=True, stop=True)
                tok = kb.bump("pe", ins)
                kb.note(tok, reads=[Q_, kmb], writes=[pg])
                kb.op("dve", lambda e: e.tensor_tensor(out=gv[:, bnk * 16:(bnk + 1) * 16, :].rearrange("p a b -> p (a b)"),
                                                       in0=pg[:], in1=vbq[:, bnk * 16:(bnk + 1) * 16, :].rearrange("p a b -> p (a b)"),
                                                       op=ALU.add), reads=[pg, vbq], writes=[gv])
            for qt in range(32):
                kb.op("dve", lambda e: e.max(out=mx[:, qt, :], in_=gv[:, qt, :]), reads=[gv], writes=[mx])
            kb.op("dve", lambda e: e.tensor_scalar(out=thr[:], in0=mx[:, :, 2], scalar1=-1e29, scalar2=None,
                                                   op0=ALU.max), reads=[mx], writes=[thr])
            for qt in range(32):
                kb.op("dve", lambda e: e.tensor_scalar(out=biasb[:, qt, :], in0=gv[:, qt, :], scalar1=thr[:, qt:qt + 1],
                                                       scalar2=1.0, op0=ALU.is_ge, op1=ALU.mult),
                      reads=[gv, thr], writes=[biasb])
            for g in range(4):
                pt = ps[6]
                ptb = pt[:].bitcast(BF16)
                kb.sync("pe", reads=[biasb, identb], writes=[pt])
                ins = None
                for j in range(8):
                    qt = g * 8 + j
                    ins = nc.tensor.transpose(out=ptb[0:32, j * 128:(j + 1) * 128], in_=biasb[:, qt, :],
                                              identity=identb[:])
                tok = kb.bump("pe", ins)
                kb.note(tok, reads=[biasb, identb], writes=[pt])
                kb.op("dve", lambda e: e.tensor_copy(out=biasT[0:32, g * 1024:(g + 1) * 1024], in_=ptb[0:32, 0:1024]),
                      reads=[pt], writes=[biasT])

            sd = seld[h % 2]
            kb.dma("sp", biasT, SEL[h, :, :].rearrange("i (n q) -> n i q", q=256),
                   biasT[:].rearrange("n (i q) -> n i q", q=256), reads=[biasT], writes=[sd])

            def load_sel(i):
                sr = selr[i % 2]
                kb.dma("sp", sr, sr[:].rearrange("p n q -> p (n q)"), SEL[h, i, :].partition_broadcast(128),
                       reads=[sd], writes=[sr])
            load_sel(0)

            O_ = ost[p]
            units = []
            for i in range(16):
                for n in range(16 + i + 1):
                    units.append((i, n, n == 16 + i))
            state = {}

            def stage1(u):
                i, n, diag = units[u]
                q0 = i * 256
                pst = pst_bufs[unit[0] % 3]
                P_ = ptsb[unit[0] % 20]
                unit[0] += 1
                state[u] = P_
                ka = n * 256
                if not diag:
                    if n == 0 and i + 1 < 16:
                        load_sel(i + 1)
                    kb.sync("pe", reads=[K_, Q_], writes=[pst])
                    ins = None
                    for kt in range(2):
                        ins = nc.tensor.matmul(pst[:, kt * 256:(kt + 1) * 256], lhsT=K_[:, ka + kt * 128:ka + (kt + 1) * 128],
                                               rhs=Q_[:, q0:q0 + 256], start=True, stop=True)
                    tok = kb.bump("pe", ins)
                    kb.note(tok, reads=[K_, Q_], writes=[pst])
                    kb.op("act", lambda e: e.activation(out=P_[:], in_=pst[:], func=AF.Exp, scale=SCALE),
                          reads=[pst], writes=[P_])
                    sr = selr[i % 2]
                    kb.op("dve", lambda e: e.tensor_tensor(out=P_[:].rearrange("p (a b) -> p a b", a=2),
                                                           in0=P_[:].rearrange("p (a b) -> p a b", a=2),
                                                           in1=sr[:, n, :].unsqueeze(1).to_broadcast([128, 2, 256]),
                                                           op=ALU.mult), reads=[P_, sr], writes=[P_])
                else:
                    kb.sync("pe", reads=[K_, Q_], writes=[pst])
                    nc.tensor.matmul(pst[:, 0:256], lhsT=K_[:, ka:ka + 128], rhs=Q_[:, q0:q0 + 256],
                                     start=True, stop=True)
                    ins = nc.tensor.matmul(pst[:, 256:384], lhsT=K_[:, ka + 128:ka + 256],
                                           rhs=Q_[:, q0 + 128:q0 + 256], start=True, stop=True)
                    tok = kb.bump("pe", ins)
                    kb.note(tok, reads=[K_, Q_], writes=[pst])
                    kb.op("act", lambda e: e.activation(out=P_[:, 0:384], in_=pst[:, 0:384], func=AF.Exp, scale=SCALE),
                          reads=[pst], writes=[P_])
                    kb.op("pool", lambda e: e.tensor_tensor(out=P_[:, 0:128], in0=P_[:, 0:128], in1=trib[:],
                                                            op=ALU.mult), reads=[P_, trib], writes=[P_])
                    kb.op("pool", lambda e: e.tensor_tensor(out=P_[:, 256:384], in0=P_[:, 256:384], in1=trib[:],
                                                            op=ALU.mult), reads=[P_, trib], writes=[P_])

            def stage2(u):
                i, n, diag = units[u]
                P_ = state.pop(u)
                po = [ps[2 + 2 * (i % 2)], ps[3 + 2 * (i % 2)]]
                if not diag:
                    pv = [(0, 0, 0), (0, 1, 128), (1, 0, 256), (1, 1, 384)]
                else:
                    pv = [(0, 0, 0), (0, 1, 128), (1, 1, 256)]
                kb.sync("pe", reads=[P_, V_], writes=po)
                ins = None
                for idx, (kt, qs, c0) in enumerate(pv):
                    last = diag and ((qs == 0 and idx == 0) or (qs == 1 and idx == 2))
                    first = (n == 0 and kt == 0)
                    ins = nc.tensor.matmul(po[qs][:, 0:129], lhsT=P_[:, c0:c0 + 128], rhs=V_[:, 2 * n + kt, :],
                                           start=first, stop=last)
                tok = kb.bump("pe", ins)
                kb.note(tok, reads=[P_, V_], writes=po)
                if diag:
                    for qs in range(2):
                        kb.op("dve", lambda e: e.reciprocal(out=rec[:, qs:qs + 1], in_=po[qs][:, 128:129]),
                              reads=[po[qs]], writes=[rec])
                        kb.op("dve", lambda e: e.tensor_scalar(out=O_[:, 2 * i + qs, :], in0=po[qs][:, 0:128],
                                                               scalar1=rec[:, qs:qs + 1], scalar2=None, op0=ALU.mult),
                              reads=[po[qs], rec], writes=[O_])

            DEPTH = 19
            nu = len(units)
            for u in range(nu + DEPTH):
                if u < nu:
                    stage1(u)
                if u - DEPTH >= 0:
                    stage2(u - DEPTH)
                if u % 32 == 16:
                    conv_step(1)
            odst = Os[:, h * 128:(h + 1) * 128].rearrange("(t p) c -> p t c", p=128)
            for t0_ in range(0, 32, 8):
                kb.dma("act", O_, odst[:, t0_:t0_ + 8, :], O_[:, t0_:t0_ + 8, :], reads=[O_])
        conv_step(1000)
        kb.drain_dma()


def load_weight_bf16(nc, kb, wsrc, wdst, stg, ncols, kc):
    wv = wsrc.rearrange("(c p) n -> p c n", p=128)
    step = stg[0].t.shape[2]
    kstep = stg[0].t.shape[1]
    ci = 0
    for k0 in range(0, kc, kstep):
        for n0 in range(0, ncols, step):
            st = stg[ci % len(stg)]
            kb.dma("sp", st, st[:], wv[:, k0:k0 + kstep, n0:n0 + step], writes=[st])
            eng = ["dve", "pool"][ci % 2]
            kb.op(eng, lambda e: e.tensor_copy(out=wdst[:, k0:k0 + kstep, n0:n0 + step], in_=st[:]),
                  reads=[st], writes=[wdst])
            ci += 1


def phase_d1(nc, kb, ps, x_all, WB, cst, YT, Os, Gs, X1, load_next_w=None):
    with ExitStack() as es:
        sb = lambda name, shape, dt: _sb(nc, es, name, shape, dt)
        wg = sb("wg", [128, 4, 2 * D], BF16)
        wo = sb("wo", [128, 8, D], BF16)
        cstt = sb("d1_cstt", [128, 260], F32)
        identb = sb("d1_identb", [128, 128], BF16)
        yT = [sb("yT%d" % i, [128, 4, TT], BF16) for i in range(1)]
        xs_ = [sb("d1x%d" % i, [128, D], F32) for i in range(2)]
        oa = [sb("oa%d" % i, [128, D], BF16) for i in range(2)]
        gg = [sb("gg%d" % i, [128, 2 * D], BF16) for i in range(2)]
        sg = [sb("sg%d" % i, [128, 512], F32) for i in range(2)]
        ob = sb("ob", [128, D], F32)
        m1 = sb("m1", [128, D], F32)
        mixed = [sb("mixed%d" % i, [128, D], BF16) for i in range(2)]
        mixT = sb("mixT", [128, 8, 128], BF16)
        x1 = [sb("x1_%d" % i, [128, D], F32) for i in range(1)]

        kb.dma("sp", cstt, cstt[:], cst[:, :], writes=[cstt])
        kb.op("dve", lambda e: e.tensor_copy(out=identb[:], in_=cstt[:, 0:128]), reads=[cstt], writes=[identb])
        kb.dma("sp", wg, wg[:].rearrange("p c n -> p (c n)"), WB["glu"][:, :], writes=[wg])
        kb.dma("sp", wo, wo[:].rearrange("p c n -> p (c n)"), WB["out"][:, :], writes=[wo])

        yts = {}

        def stage_a(idx):
            it, s = idx // 4, idx % 4
            t0 = it * TT
            if s == 0:
                y_ = yT[0]
                kb.dma("sp", y_, y_[:], YT[:, t0:t0 + TT].rearrange("(m p) t -> p m t", p=128), writes=[y_])
                yts[it] = y_
            y_ = yts[it]
            j = idx % 2
            r0 = t0 + s * 128
            kb.dma("sp", xs_[j], xs_[j][:], x_all[HALF + r0:HALF + r0 + 128, :], writes=[xs_[j]])
            kb.dma("sp", oa[j], oa[j][:], Os[r0:r0 + 128, :], writes=[oa[j]])
            kb.dma("sp", gg[j], gg[j][:], Gs[r0:r0 + 128, :], writes=[gg[j]])
            for hb_ in range(2):
                pv = ps[(2 * hb_) % 8]
                pgt = ps[(2 * hb_ + 1) % 8]
                for (pbuf, cb) in ((pv, hb_), (pgt, 2 + hb_)):
                    kb.sync("pe", reads=[y_, wg], writes=[pbuf])
                    ins = None
                    for m in range(4):
                        ins = nc.tensor.matmul(pbuf[:], lhsT=y_[:, m, s * 128:(s + 1) * 128],
                                               rhs=wg[:, m, cb * 512:(cb + 1) * 512], start=(m == 0), stop=(m == 3))
                    tok = kb.bump("pe", ins)
                    kb.note(tok, reads=[y_, wg], writes=[pbuf])
                sgb = sg[hb_]
                kb.op("act", lambda e: e.activation(out=sgb[:], in_=pgt[:], func=AF.Sigmoid),
                      reads=[pgt], writes=[sgb])
                kb.op("dve", lambda e: e.tensor_tensor(out=ob[:, hb_ * 512:(hb_ + 1) * 512], in0=pv[:], in1=sgb[:],
                                                       op=ALU.mult), reads=[pv, sgb], writes=[ob])
            mx_ = mixed[j]
            kb.op("pool", lambda e: e.tensor_tensor(out=m1[:], in0=gg[j][:, 0:D], in1=oa[j][:], op=ALU.mult),
                  reads=[gg[j], oa[j]], writes=[m1])
            kb.op("dve", lambda e: e.tensor_tensor(out=ob[:], in0=ob[:], in1=gg[j][:, D:2 * D], op=ALU.mult),
                  reads=[ob, gg[j]], writes=[ob])
            kb.op("pool", lambda e: e.tensor_tensor(out=mx_[:], in0=m1[:], in1=ob[:], op=ALU.add),
                  reads=[m1, ob], writes=[mx_])

        def stage_b(idx):
            it, s = idx // 4, idx % 4
            t0 = it * TT
            j = idx % 2
            r0 = t0 + s * 128
            mx_ = mixed[j]
            for g in range(2):
                pt = ps[4 + g]
                ptb = pt[:].bitcast(BF16)
                kb.sync("pe", reads=[mx_, identb], writes=[pt])
                ins = None
                for c4 in range(4):
                    c = g * 4 + c4
                    ins = nc.tensor.transpose(out=ptb[:, c4 * 128:(c4 + 1) * 128], in_=mx_[:, c * 128:(c + 1) * 128],
                                              identity=identb[:])
                tok = kb.bump("pe", ins)
                kb.note(tok, reads=[mx_, identb], writes=[pt])
                dst = mixT[:, g * 4:(g + 1) * 4, :].rearrange("p a b -> p (a b)")
                if g == 0:
                    kb.op("act", lambda e: e.activation(out=dst, in_=ptb[:, 0:512], func=AF.Copy),
                          reads=[pt], writes=[mixT])
                else:
                    kb.op("dve", lambda e: e.tensor_copy(out=dst, in_=ptb[:, 0:512]), reads=[pt], writes=[mixT])
            xo = x1[0]
            for hf in range(2):
                pbuf = ps[6 + hf]
                kb.sync("pe", reads=[mixT, wo], writes=[pbuf])
                ins = None
                for c in range(8):
                    ins = nc.tensor.matmul(pbuf[:], lhsT=mixT[:, c, :], rhs=wo[:, c, hf * 512:(hf + 1) * 512],
                                           start=(c == 0), stop=(c == 7))
                tok = kb.bump("pe", ins)
                kb.note(tok, reads=[mixT, wo], writes=[pbuf])
                kb.op("dve", lambda e: e.tensor_tensor(out=xo[:, hf * 512:(hf + 1) * 512], in0=pbuf[:],
                                                       in1=xs_[j][:, hf * 512:(hf + 1) * 512], op=ALU.add),
                      reads=[pbuf, xs_[j]], writes=[xo])
            kb.dma("act", xo, X1[r0:r0 + 128, :], xo[:], reads=[xo])

        nidx = HALF // 128
        stage_a(0)
        for idx in range(nidx):
            if idx + 1 < nidx:
                stage_a(idx + 1)
            if load_next_w is not None and idx % 3 == 2:
                load_next_w(1)
            stage_b(idx)
        if load_next_w is not None:
            load_next_w(100)
        kb.drain_dma()


def phase_d2(nc, kb, ps, wu, wd, g_mlp, g_fin, cst, X1, out):
    with ExitStack() as es:
        sb = lambda name, shape, dt: _sb(nc, es, name, shape, dt)
        cstt = sb("d2_cstt", [128, 260], F32)
        identb = sb("d2_identb", [128, 128], BF16)
        gcol = sb("d2_gcol", [128, 8], F32)
        gfin = sb("gfin", [128, D], F32)
        xt = [sb("d2x%d" % i, [128, D], F32) for i in range(5)]
        junk = sb("d2junk", [128, D], BF16)
        ss = sb("d2ss", [128, 4], F32)
        rstd = sb("d2rstd", [128, 4], F32)
        hbt = sb("d2hb", [128, 4, D], BF16)
        hT = sb("d2hT", [128, 8, TT], BF16)
        rl = [sb("rl%d" % i, [128, TT], BF16) for i in range(2)]
        aT = sb("aT", [128, 32, TT], BF16)
        ss2 = sb("d2ss2", [128, 2], F32)
        rstd2 = sb("d2rstd2", [128, 2], F32)

        kb.dma("sp", cstt, cstt[:], cst[:, :], writes=[cstt])
        kb.dma("sp", gcol, gcol[:], g_mlp[:, :], writes=[gcol])
        kb.dma("sp", gfin, gfin[:], g_fin.partition_broadcast(128), writes=[gfin])
        kb.op("dve", lambda e: e.tensor_copy(out=identb[:], in_=cstt[:, 0:128]), reads=[cstt], writes=[identb])
        xn_i = [0]
        xr_i = [0]

        def norm_tile(it):
            t0 = it * TT
            for s in range(4):
                xs = xt[xn_i[0] % 2]
                xn_i[0] += 1
                kb.dma("sp", xs, xs[:], X1[t0 + s * 128:t0 + (s + 1) * 128, :], writes=[xs])
                kb.op("act", lambda e: e.activation(out=junk[:], in_=xs[:], func=AF.Square, accum_out=ss[:, s:s + 1]),
                      reads=[xs], writes=[junk, ss])
                kb.op("dve", lambda e: e.tensor_scalar(out=rstd[:, s:s + 1], in0=ss[:, s:s + 1], scalar1=1.0 / D,
                                                       scalar2=EPS, op0=ALU.mult, op1=ALU.add), reads=[ss], writes=[rstd])
                kb.op("act", lambda e: e.activation(out=rstd[:, s:s + 1], in_=rstd[:, s:s + 1], func=AF.Sqrt),
                      reads=[rstd], writes=[rstd])
                kb.op("dve", lambda e: e.reciprocal(out=rstd[:, s:s + 1], in_=rstd[:, s:s + 1]),
                      reads=[rstd], writes=[rstd])
                if s % 2 == 0:
                    kb.op("dve", lambda e: e.tensor_scalar(out=hbt[:, s, :], in0=xs[:], scalar1=rstd[:, s:s + 1],
                                                           scalar2=None, op0=ALU.mult), reads=[xs, rstd], writes=[hbt])
                else:
                    kb.op("act", lambda e: e.activation(out=hbt[:, s, :], in_=xs[:], func=AF.Copy,
                                                        scale=rstd[:, s:s + 1]), reads=[xs, rstd], writes=[hbt])

        def trans_tile(it):
            for c in range(8):
                pt = ps[c % 2]
                ptb = pt[:].bitcast(BF16)
                kb.sync("pe", reads=[hbt, identb], writes=[pt])
                ins = None
                for s in range(4):
                    ins = nc.tensor.transpose(out=ptb[:, s * 128:(s + 1) * 128], in_=hbt[:, s, c * 128:(c + 1) * 128],
                                              identity=identb[:])
                tok = kb.bump("pe", ins)
                kb.note(tok, reads=[hbt, identb], writes=[pt])
                if c % 2 == 0:
                    kb.op("act", lambda e: e.activation(out=hT[:, c, :], in_=ptb[:, 0:TT], func=AF.Copy,
                                                        scale=gcol[:, c:c + 1]), reads=[pt, gcol], writes=[hT])
                else:
                    kb.op("dve", lambda e: e.tensor_scalar(out=hT[:, c, :], in0=ptb[:, 0:TT], scalar1=gcol[:, c:c + 1],
                                                           scalar2=None, op0=ALU.mult), reads=[pt, gcol], writes=[hT])

        def up_tile(it):
            for f in range(32):
                pbuf = ps[2 + f % 3]
                kb.sync("pe", reads=[wu, hT], writes=[pbuf])
                ins = None
                for c in range(8):
                    ins = nc.tensor.matmul(pbuf[:], lhsT=wu[:, c, f * 128:(f + 1) * 128], rhs=hT[:, c, :],
                                           start=(c == 0), stop=(c == 7))
                tok = kb.bump("pe", ins)
                kb.note(tok, reads=[wu, hT], writes=[pbuf])
                r_ = rl[f % 2]
                kb.op("act", lambda e: e.activation(out=r_[:], in_=pbuf[:], func=AF.Relu), reads=[pbuf], writes=[r_])
                eng = ["pool", "dve", "dve"][f % 3]
                kb.op(eng, lambda e: e.tensor_tensor(out=aT[:, f, :], in0=r_[:], in1=r_[:], op=ALU.mult),
                      reads=[r_], writes=[aT])

        def down_tile(it):
            t0 = it * TT
            for s in range(4):
                xo = xt[2 + xr_i[0] % 3]
                xr_i[0] += 1
                kb.dma("sp", xo, xo[:], X1[t0 + s * 128:t0 + (s + 1) * 128, :], writes=[xo])
                for hf in range(2):
                    pbuf = ps[5 + (2 * s + hf) % 3]
                    kb.sync("pe", reads=[aT, wd], writes=[pbuf])
                    ins = None
                    for f in range(32):
                        ins = nc.tensor.matmul(pbuf[:], lhsT=aT[:, f, s * 128:(s + 1) * 128],
                                               rhs=wd[:, f, hf * 512:(hf + 1) * 512], start=(f == 0), stop=(f == 31))
                    tok = kb.bump("pe", ins)
                    kb.note(tok, reads=[aT, wd], writes=[pbuf])
                    kb.op("dve", lambda e: e.tensor_tensor(out=xo[:, hf * 512:(hf + 1) * 512], in0=pbuf[:],
                                                           in1=xo[:, hf * 512:(hf + 1) * 512], op=ALU.add),
                          reads=[pbuf, xo], writes=[xo])
                k2 = s % 2
                kb.op("act", lambda e: e.activation(out=junk[:], in_=xo[:], func=AF.Square, accum_out=ss2[:, k2:k2 + 1]),
                      reads=[xo], writes=[junk, ss2])
                kb.op("dve", lambda e: e.tensor_scalar(out=rstd2[:, k2:k2 + 1], in0=ss2[:, k2:k2 + 1], scalar1=1.0 / D,
                                                       scalar2=EPS, op0=ALU.mult, op1=ALU.add), reads=[ss2], writes=[rstd2])
                kb.op("act", lambda e: e.activation(out=rstd2[:, k2:k2 + 1], in_=rstd2[:, k2:k2 + 1], func=AF.Sqrt),
                      reads=[rstd2], writes=[rstd2])
                kb.op("dve", lambda e: e.reciprocal(out=rstd2[:, k2:k2 + 1], in_=rstd2[:, k2:k2 + 1]),
                      reads=[rstd2], writes=[rstd2])
                kb.op("dve", lambda e: e.scalar_tensor_tensor(out=xo[:], in0=xo[:], scalar=rstd2[:, k2:k2 + 1],
                                                              in1=gfin[:], op0=ALU.mult, op1=ALU.mult),
                      reads=[xo, rstd2, gfin], writes=[xo])
                kb.dma("act", xo, out[t0 + s * 128:t0 + (s + 1) * 128, :], xo[:], reads=[xo])

        ntl = HALF // TT
        norm_tile(0)
        trans_tile(0)
        for it in range(ntl):
            up_tile(it)
            if it + 1 < ntl:
                norm_tile(it + 1)
                trans_tile(it + 1)
            down_tile(it)
        kb.drain_dma()


S5C_COLS = 32 * 4 + 512 * 6


def phase_c(nc, kb, ps, UT, YT, cst, s5c_d):
    NJ = SEQ // 16
    NO = HALF // 16
    with ExitStack() as es:
        sb = lambda name, shape, dt: _sb(nc, es, name, shape, dt)
        cstt = sb("c_cstt", [128, 260], F32)
        identb = sb("c_identb", [128, 128], BF16)
        s5c = sb("s5c_sb", [128, S5C_COLS], F32)
        dt_ = sb("c_dt", [128, 32], F32)
        lrdt = sb("c_lrdt", [128, 32], F32)
        lidt = sb("c_lidt", [128, 32], F32)
        mv = sb("c_mv", [128, 32, 16], F32)
        tA = [sb("c_tA%d" % i, [128, 32, 16], F32) for i in range(6)]
        tI = sb("c_tI", [128, 32, 16], I32)
        PWr = sb("PWr", [128, 32, 17], F32)
        PWi = sb("PWi", [128, 32, 17], F32)
        PVr = sb("PVr", [128, 32, 16], F32)
        PVi = sb("PVi", [128, 32, 16], F32)
        PMr = sb("PMr", [128, 32, 16], F32)
        PMi = sb("PMi", [128, 32, 16], F32)
        Btr = sb("Btr", [128, 32, 16], F32)
        Bti = sb("Bti", [128, 32, 16], F32)
        sm = [sb("c_sm%d" % i, [128, 32], F32) for i in range(8)]
        LVr = sb("LVr", [128, 32, 9], F32)
        LVi = sb("LVi", [128, 32, 9], F32)
        LVs = sb("LVs", [128, 32, 9], F32)
        Sel = sb("Sel", [128, 64, 128], BF16)
        SelR = sb("SelR", [128, 64, 128], BF16)
        NSET = 8
        WT = [sb("WT%d" % i, [128, 16, 16], BF16) for i in range(NSET)]
        Fg = [sb("Fg%d" % i, [128, 16, 16], BF16) for i in range(NSET)]
        Gg = [sb("Gg%d" % i, [128, 16, 16], BF16) for i in range(NSET)]
        Wg = [sb("Wg%d" % i, [128, 2, 128], BF16) for i in range(NSET)]
        M0g = [sb("M0g%d" % i, [128, 2, 256], BF16) for i in range(NSET)]
        ALg = [sb("ALg%d" % i, [128, 9, 128], BF16) for i in range(NSET)]
        ct1 = [sb("ct1_%d" % i, [128, 16, 16], F32) for i in range(2)]
        ct2 = [sb("ct2_%d" % i, [128, 16, 16], F32) for i in range(2)]
        mtmp = sb("mtmp", [128, 512], F32)
        atmp = [sb("atmp%d" % i, [128, 128], F32) for i in range(2)]
        Ug2 = [sb("Ug%d" % i, [128, 2, NJ], BF16) for i in range(8)]
        Ug = list(Ug2[0:4])
        Z = [[sb("Z%d_%d" % (i, j), [128, NJ + 1], BF16) for j in range(2)] for i in range(4)]
        ysb = [sb("ysb%d" % i, [128, NO], F32) for i in range(8)]
        yt_ = [sb("yt%d" % i, [128, NO], F32) for i in range(4)]
        ysg = [sb("ysg%d" % i, [128, NO], F32) for i in range(4)]
        ygel = sb("ygel", [128, 8, 2, NO], BF16)
        YTm = [sb("YTm%d" % i, [128, HALF], BF16) for i in range(1)]

        C_LR, C_LI, C_LS, C_DS = 0, 32, 64, 96
        C_BR, C_BI, C_CR, C_CI, C_MK, C_DI = 128, 640, 1152, 1664, 2176, 2688

        kb.dma("sp", cstt, cstt[:], cst[:, :], writes=[cstt])
        kb.dma("sp", s5c, s5c[:], s5c_d[:, :], writes=[s5c])
        kb.op("dve", lambda e: e.tensor_copy(out=identb[:], in_=cstt[:, 0:128]), reads=[cstt], writes=[identb])
        identf = cstt[:, 0:128]
        swapf = cstt[:, 128:256]

        def V(e, fn, reads, writes):
            return kb.op(e, fn, reads=reads, writes=writes)

        def tt(e, out_b, out_ap, a_b, a_ap, b_b, b_ap, op):
            return kb.op(e, lambda en: en.tensor_tensor(out=out_ap, in0=a_ap, in1=b_ap, op=op),
                         reads=[a_b, b_b], writes=[out_b])

        def bc_g(ap2):
            return ap2.unsqueeze(2).to_broadcast([128, 32, 16])

        V("pool", lambda e: e.memset(Sel[:], 0.0), [], [Sel])
        V("pool", lambda e: e.memset(SelR[:], 0.0), [], [SelR])
        for gl in range(8):
            for s8 in range(8):
                eng = ["dve", "pool"][s8 % 2]
                V(eng, lambda e: e.tensor_copy(out=Sel[:, gl * 8 + s8, s8 * 16:(s8 + 1) * 16],
                                               in_=cstt[:, gl * 16:(gl + 1) * 16]), [cstt, Sel], [Sel])
                V(eng, lambda e: e.tensor_copy(out=SelR[:, gl * 8 + s8, gl * 16:(gl + 1) * 16],
                                               in_=cstt[:, s8 * 16:(s8 + 1) * 16]), [cstt, SelR], [SelR])
        for i in range(4):
            for j in range(2):
                V("pool", lambda e: e.memset(Z[i][j][:, 0:1], 0.0), [], [Z[i][j]])

        V("act", lambda e: e.activation(out=dt_[:], in_=s5c[:, C_LS:C_LS + 32], func=AF.Exp), [s5c], [dt_])
        tt("dve", lrdt, lrdt[:], s5c, s5c[:, C_LR:C_LR + 32], dt_, dt_[:], ALU.mult)
        tt("dve", lidt, lidt[:], s5c, s5c[:, C_LI:C_LI + 32], dt_, dt_[:], ALU.mult)
        for m in range(16):
            V("pool", lambda e: e.memset(mv[:, :, m:m + 1], float(m + 1)), [], [mv])
        ang, kf, rr, rc, mm, mg = tA
        tt("dve", ang, ang[:], mv, mv[:], lidt, bc_g(lidt[:]), ALU.mult)
        tt("dve", mg, mg[:], mv, mv[:], lrdt, bc_g(lrdt[:]), ALU.mult)
        V("act", lambda e: e.activation(out=mg[:], in_=mg[:], func=AF.Exp), [mg], [mg])
        V("dve", lambda e: e.tensor_scalar(out=kf[:], in0=ang[:], scalar1=1.0 / TWO_PI, scalar2=None, op0=ALU.mult),
          [ang], [kf])
        V("dve", lambda e: e.tensor_copy(out=tI[:], in_=kf[:]), [kf], [tI])
        V("dve", lambda e: e.tensor_copy(out=kf[:], in_=tI[:]), [tI], [kf])
        V("dve", lambda e: e.scalar_tensor_tensor(out=rr[:], in0=kf[:], scalar=-TWO_PI, in1=ang[:],
                                                  op0=ALU.mult, op1=ALU.add), [kf, ang], [rr])

        def wrap(dst, src):
            V("dve", lambda e: e.tensor_scalar(out=mm[:], in0=src[:], scalar1=PI, scalar2=-TWO_PI,
                                               op0=ALU.is_gt, op1=ALU.mult), [src], [mm])
            tt("dve", dst, dst[:], src, src[:], mm, mm[:], ALU.add)
            V("dve", lambda e: e.tensor_scalar(out=mm[:], in0=dst[:], scalar1=-PI, scalar2=TWO_PI,
                                               op0=ALU.is_lt, op1=ALU.mult), [dst], [mm])
            tt("dve", dst, dst[:], dst, dst[:], mm, mm[:], ALU.add)
            V("dve", lambda e: e.tensor_scalar(out=dst[:], in0=dst[:], scalar1=3.14159, scalar2=-3.14159,
                                               op0=ALU.min, op1=ALU.max), [dst], [dst])

        wrap(rr, rr)
        V("dve", lambda e: e.tensor_scalar(out=rc[:], in0=rr[:], scalar1=PI / 2, scalar2=None, op0=ALU.add), [rr], [rc])
        wrap(rc, rc)
        V("act", lambda e: e.activation(out=rr[:], in_=rr[:], func=AF.Sin), [rr], [rr])
        V("act", lambda e: e.activation(out=rc[:], in_=rc[:], func=AF.Sin), [rc], [rc])
        V("dve", lambda e: e.memset(PWr[:, :, 0:1], 1.0), [], [PWr])
        V("dve", lambda e: e.memset(PWi[:, :, 0:1], 0.0), [], [PWi])
        tt("dve", PWr, PWr[:, :, 1:17], mg, mg[:], rc, rc[:], ALU.mult)
        tt("dve", PWi, PWi[:, :, 1:17], mg, mg[:], rr, rr[:], ALU.mult)
        for s_ in range(16):
            V("dve", lambda e: e.tensor_copy(out=PVr[:, :, s_:s_ + 1], in_=PWr[:, :, 15 - s_:16 - s_]), [PWr], [PVr])
            V("pool", lambda e: e.tensor_copy(out=PVi[:, :, s_:s_ + 1], in_=PWi[:, :, 15 - s_:16 - s_]), [PWi], [PVi])
        d16, ir, ii, nr, den, cr, ci, tq = sm
        p16r = PWr[:, :, 16]
        p16i = PWi[:, :, 16]
        tt("dve", d16, d16[:], PWr, p16r, PWr, p16r, ALU.mult)
        tt("dve", tq, tq[:], PWi, p16i, PWi, p16i, ALU.mult)
        tt("dve", d16, d16[:], d16, d16[:], tq, tq[:], ALU.add)
        V("dve", lambda e: e.reciprocal(out=d16[:], in_=d16[:]), [d16], [d16])
        tt("dve", ir, ir[:], PWr, p16r, d16, d16[:], ALU.mult)
        tt("dve", ii, ii[:], PWi, p16i, d16, d16[:], ALU.mult)
        V("dve", lambda e: e.tensor_scalar(out=ii[:], in0=ii[:], scalar1=-1.0, scalar2=None, op0=ALU.mult), [ii], [ii])
        x1_, x2_ = tA[0], tA[1]
        tt("dve", x1_, x1_[:], PWr, PWr[:, :, 1:17], ir, bc_g(ir[:]), ALU.mult)
        tt("dve", x2_, x2_[:], PWi, PWi[:, :, 1:17], ii, bc_g(ii[:]), ALU.mult)
        tt("dve", PMr, PMr[:], x1_, x1_[:], x2_, x2_[:], ALU.subtract)
        tt("dve", x1_, x1_[:], PWr, PWr[:, :, 1:17], ii, bc_g(ii[:]), ALU.mult)
        tt("dve", x2_, x2_[:], PWi, PWi[:, :, 1:17], ir, bc_g(ir[:]), ALU.mult)
        tt("dve", PMi, PMi[:], x1_, x1_[:], x2_, x2_[:], ALU.add)
        lamr = s5c[:, C_LR:C_LR + 32]
        lami = s5c[:, C_LI:C_LI + 32]
        V("dve", lambda e: e.tensor_scalar(out=nr[:], in0=PWr[:, :, 1], scalar1=-1.0, scalar2=None, op0=ALU.add),
          [PWr], [nr])
        ni = PWi[:, :, 1]
        tt("dve", den, den[:], s5c, lamr, s5c, lamr, ALU.mult)
        tt("dve", tq, tq[:], s5c, lami, s5c, lami, ALU.mult)
        tt("dve", den, den[:], den, den[:], tq, tq[:], ALU.add)
        V("dve", lambda e: e.reciprocal(out=den[:], in_=den[:]), [den], [den])
        tt("dve", cr, cr[:], nr, nr[:], s5c, lamr, ALU.mult)
        tt("dve", tq, tq[:], PWi, ni, s5c, lami, ALU.mult)
        tt("dve", cr, cr[:], cr, cr[:], tq, tq[:], ALU.add)
        tt("dve", cr, cr[:], cr, cr[:], den, den[:], ALU.mult)
        tt("dve", ci, ci[:], PWi, ni, s5c, lamr, ALU.mult)
        tt("dve", tq, tq[:], nr, nr[:], s5c, lami, ALU.mult)
        tt("dve", ci, ci[:], ci, ci[:], tq, tq[:], ALU.subtract)
        tt("dve", ci, ci[:], ci, ci[:], den, den[:], ALU.mult)
        bre = s5c[:, C_BR:C_BR + 512].rearrange("p (g c) -> p g c", c=16)
        bim = s5c[:, C_BI:C_BI + 512].rearrange("p (g c) -> p g c", c=16)
        tt("dve", x1_, x1_[:], s5c, bre, cr, bc_g(cr[:]), ALU.mult)
        tt("dve", x2_, x2_[:], s5c, bim, ci, bc_g(ci[:]), ALU.mult)
        tt("dve", Btr, Btr[:], x1_, x1_[:], x2_, x2_[:], ALU.subtract)
        tt("dve", x1_, x1_[:], s5c, bim, cr, bc_g(cr[:]), ALU.mult)
        tt("dve", x2_, x2_[:], s5c, bre, ci, bc_g(ci[:]), ALU.mult)
        tt("dve", Bti, Bti[:], x1_, x1_[:], x2_, x2_[:], ALU.add)
        V("dve", lambda e: e.tensor_copy(out=LVr[:, :, 0:1], in_=PWr[:, :, 16:17]), [PWr], [LVr])
        V("dve", lambda e: e.tensor_copy(out=LVi[:, :, 0:1], in_=PWi[:, :, 16:17]), [PWi], [LVi])
        q1, q2 = sm[0], sm[1]
        for l in range(8):
            ar_, ai_ = LVr[:, :, l], LVi[:, :, l]
            tt("dve", q1, q1[:], LVr, ar_, LVr, ar_, ALU.mult)
            tt("dve", q2, q2[:], LVi, ai_, LVi, ai_, ALU.mult)
            tt("dve", LVr, LVr[:, :, l + 1], q1, q1[:], q2, q2[:], ALU.subtract)
            tt("dve", q1, q1[:], LVr, ar_, LVi, ai_, ALU.mult)
            V("dve", lambda e: e.tensor_scalar(out=LVi[:, :, l + 1], in0=q1[:], scalar1=2.0, scalar2=None,
                                               op0=ALU.mult), [q1], [LVi])
        V("dve", lambda e: e.tensor_scalar(out=LVs[:], in0=LVi[:], scalar1=cstt[:, 258:259], scalar2=None,
                                           op0=ALU.mult), [LVi, cstt], [LVs])

        cre = s5c[:, C_CR:C_CR + 512].rearrange("p (g c) -> p g c", c=16)
        cim = s5c[:, C_CI:C_CI + 512].rearrange("p (g c) -> p g c", c=16)
        ncre_b = sb("ncre", [128, 32, 16], F32)
        ncim_b = sb("ncim", [128, 32, 16], F32)
        V("dve", lambda e: e.tensor_scalar(out=ncre_b[:], in0=cre, scalar1=-1.0, scalar2=None, op0=ALU.mult), [s5c], [ncre_b])
        V("dve", lambda e: e.tensor_scalar(out=ncim_b[:], in0=cim, scalar1=-1.0, scalar2=None, op0=ALU.mult), [s5c], [ncim_b])

        def cmat(dst, k, tabr_b, tabr, tabi_b, tabi, mr_b, mr, mi_b, mi, nmr_b=None, nmr=None, nmi_b=None, nmi=None):
            def b_a(ap, lo, hi):
                return ap[lo:hi].unsqueeze(2).to_broadcast([64, 16, 16])

            def b_b(ap, lo, hi):
                return ap[lo:hi].unsqueeze(1).to_broadcast([64, 16, 16])

            t1, t2 = ct1[k], ct2[k]
            tt("dve", t1, t1[0:64], tabr_b, b_a(tabr, 0, 64), mr_b, b_b(mr, 0, 64), ALU.mult)
            tt("dve", t2, t2[0:64], tabi_b, b_a(tabi, 0, 64), mi_b, b_b(mi, 0, 64), ALU.mult)
            tt("dve", dst, dst[0:64], t1, t1[0:64], t2, t2[0:64], ALU.subtract)
            if nmr is not None:
                mr_b, mr, mi_b, mi = nmr_b, nmr, nmi_b, nmi
            tt("pool", t1, t1[64:128], tabr_b, b_a(tabr, 64, 128), mi_b, b_b(mi, 64, 128), ALU.mult)
            tt("pool", t2, t2[64:128], tabi_b, b_a(tabi, 64, 128), mr_b, b_b(mr, 64, 128), ALU.mult)
            tt("pool", dst, dst[64:128], t1, t1[64:128], t2, t2[64:128], ALU.add)

        ev = [0]

        def evac(dst_b, dst_ap, src_b, src_ap):
            ev[0] += 1
            if ev[0] % 3:
                kb.op("act", lambda e: e.activation(out=dst_ap, in_=src_ap, func=AF.Copy), reads=[src_b], writes=[dst_b])
            else:
                kb.op("dve", lambda e: e.tensor_copy(out=dst_ap, in_=src_ap), reads=[src_b], writes=[dst_b])

        def pe(reads, writes, fn):
            kb.sync("pe", reads=reads, writes=writes)
            ins = fn()
            tok = kb.bump("pe", ins)
            kb.note(tok, reads=reads, writes=writes)

        NB = 4
        batches = [(m, half) for m in range(4) for half in range(2)]

        def slots_of(m, half):
            sl = []
            for b in range(NB):
                gl = half * NB + b
                sl.append((b, gl, m * 8 + gl, (half * NB + b) % NSET))
            return sl

        def do_setup(m, half):
            slots = slots_of(m, half)
            for (b, gl, g, k) in slots:
                cmat(WT[k], b % 2, PVr, PVr[:, g, :], PVi, PVi[:, g, :], Btr, Btr[:, g, :], Bti, Bti[:, g, :])
                cmat(Fg[k], b % 2, PWr, PWr[:, g, 1:17], PWi, PWi[:, g, 1:17], s5c, cre[:, g, :], s5c, cim[:, g, :],
                     ncre_b, ncre_b[:, g, :], ncim_b, ncim_b[:, g, :])
                cmat(Gg[k], b % 2, PMr, PMr[:, g, :], PMi, PMi[:, g, :], s5c, cre[:, g, :], s5c, cim[:, g, :],
                     ncre_b, ncre_b[:, g, :], ncim_b, ncim_b[:, g, :])
                wt2 = WT[k][:].rearrange("p a b -> p (a b)")
                gg2 = Gg[k][:].rearrange("p a b -> p (a b)")
                p0 = ps[0]
                p0b = p0[:].bitcast(BF16)

                def f_tr():
                    ins = None
                    for kt in range(2):
                        ins = nc.tensor.transpose(out=p0b[:, kt * 128:(kt + 1) * 128],
                                                  in_=wt2[:, kt * 128:(kt + 1) * 128], identity=identb[:])
                    return ins
                pe([WT[k], identb], [p0], f_tr)
                evac(Wg[k], Wg[k][:].rearrange("p a b -> p (a b)"), p0, p0b[:, 0:256])
                p1 = ps[1]

                def f_m0():
                    ins = None
                    for kt in range(2):
                        ins = nc.tensor.matmul(p1[:, kt * 256:(kt + 1) * 256], lhsT=wt2[:, kt * 128:(kt + 1) * 128],
                                               rhs=gg2, start=True, stop=True)
                    return ins
                pe([WT[k], Gg[k]], [p1], f_m0)
                tt("dve", mtmp, mtmp[:], p1, p1[:], s5c, s5c[:, C_MK:C_MK + 512], ALU.mult)
                kb.op("dve", lambda e: e.scalar_tensor_tensor(out=M0g[k][:].rearrange("p a b -> p (a b)"),
                                                              in0=s5c[:, C_DI:C_DI + 512],
                                                              scalar=s5c[:, C_DS + g:C_DS + g + 1],
                                                              in1=mtmp[:], op0=ALU.mult, op1=ALU.add),
                      reads=[s5c, mtmp], writes=[M0g[k]])
                for l in range(9):
                    at = atmp[l % 2]
                    kb.op("act", lambda e: e.activation(out=at[:], in_=swapf, func=AF.Copy, scale=LVs[:, g, l:l + 1]),
                          reads=[cstt, LVs], writes=[at])
                    kb.op("dve", lambda e: e.scalar_tensor_tensor(out=ALg[k][:, l, :], in0=identf,
                                                                  scalar=LVr[:, g, l:l + 1],
                                                                  in1=at[:], op0=ALU.mult, op1=ALU.add),
                          reads=[cstt, LVr, at], writes=[ALg[k]])

        def do_relayout(m, half, um):
            slots = slots_of(m, half)
            for (b, gl, g, k) in slots:
                for s8 in range(8):
                    kb.dma("sp", Ug[b], Ug[b][s8 * 16:(s8 + 1) * 16, :, :], UT[16 * g:16 * g + 16, s8:16:8, :],
                           writes=[Ug[b]])

        curs = {}

        def do_ds(m, half):
            slots = slots_of(m, half)
            cur = [0] * NB
            for (b, gl, g, k) in slots:
                pz = ps[4 + b]

                def f_ds():
                    ins = None
                    for kt in range(2):
                        ins = nc.tensor.matmul(pz[:], lhsT=Wg[k][:, kt, :], rhs=Ug[b][:, kt, :],
                                               start=(kt == 0), stop=(kt == 1))
                    return ins
                pe([Wg[k], Ug[b]], [pz], f_ds)
                evac(Z[b][0], Z[b][0][:, 1:NJ + 1], pz, pz[:])
            return cur

        def do_ks(m, half, cur):
            slots = slots_of(m, half)
            for l in range(9):
                d = 1 << l
                for (b, gl, g, k) in slots:
                    pz = ps[4 + b]
                    src = Z[b][cur[b]]

                    def f_ks():
                        nc.tensor.matmul(pz[:], lhsT=identb[:], rhs=src[:, 1:NJ + 1], start=True, stop=False)
                        return nc.tensor.matmul(pz[:, d:NJ], lhsT=ALg[k][:, l, :], rhs=src[:, 1:NJ + 1 - d],
                                                start=False, stop=True)
                    pe([identb, ALg[k], src], [pz], f_ks)
                    cur[b] = 1 - cur[b]
                    dstz = Z[b][cur[b]]
                    evac(dstz, dstz[:, 1:NJ + 1], pz, pz[:])
            return cur

        def do_ymm(m, half, cur):
            slots = slots_of(m, half)
            for (b, gl, g, k) in slots:
                zf = Z[b][cur[b]]
                fg2 = Fg[k][:].rearrange("p a b -> p (a b)")
                for mt in range(2):
                    py = ps[4 + b]

                    def f_y():
                        for kt in range(mt + 1):
                            nc.tensor.matmul(py[:, 0:NO], lhsT=M0g[k][:, kt, mt * 128:(mt + 1) * 128],
                                             rhs=Ug[b][:, kt, NO:NJ], start=(kt == 0), stop=False)
                        return nc.tensor.matmul(py[:, 0:NO], lhsT=fg2[:, mt * 128:(mt + 1) * 128], rhs=zf[:, NO:NJ],
                                                start=False, stop=True)
                    pe([M0g[k], Ug[b], Fg[k], zf], [py], f_y)
                    ys = ysb[b * 2 + mt]
                    kb.op("act", lambda e: e.activation(out=ys[:], in_=py[:, 0:NO], func=AF.Copy),
                          reads=[py], writes=[ys])

        def do_gelu(m, half):
            slots = slots_of(m, half)
            for (b, gl, g, k) in slots:
                for mt in range(2):
                    ys = ysb[b * 2 + mt]
                    yi = (b * 2 + mt) % 4
                    yt2, yg = yt_[yi], ysg[yi]
                    tt("dve", yt2, yt2[:], ys, ys[:], ys, ys[:], ALU.mult)
                    kb.op("dve", lambda e: e.tensor_scalar(out=yt2[:], in0=yt2[:], scalar1=0.044715, scalar2=1.0,
                                                           op0=ALU.mult, op1=ALU.add), reads=[yt2], writes=[yt2])
                    tt("dve", yt2, yt2[:], yt2, yt2[:], ys, ys[:], ALU.mult)
                    kb.op("act", lambda e: e.activation(out=yg[:], in_=yt2[:], func=AF.Sigmoid, scale=1.5957691216),
                          reads=[yt2], writes=[yg])
                    tt("pool", ygel, ygel[:, gl, mt, :], ys, ys[:], yg, yg[:], ALU.mult)

        def do_reverse(m, ym):
            for mt in range(2):
                for t8 in range(8):
                    pr = ps[(mt * 8 + t8) % 2]

                    def f_r():
                        ins = None
                        for gl in range(8):
                            ins = nc.tensor.matmul(pr[:, 0:NO], lhsT=SelR[:, gl * 8 + t8, :], rhs=ygel[:, gl, mt, :],
                                                   start=(gl == 0), stop=(gl == 7))
                        return ins
                    pe([SelR, ygel], [pr], f_r)
                    off = mt * 8 + t8
                    evac(ym, ym[:, off:HALF:16], pr, pr[:, 0:NO])
            kb.dma("act", ym, YT[m * 128:(m + 1) * 128, :], ym[:], reads=[ym])

        do_setup(*batches[0])
        um = None
        for bi, (m, half) in enumerate(batches):
            for b_ in range(NB):
                Ug[b_] = Ug2[(bi % 2) * 4 + b_]
            do_relayout(m, half, um)
            cur = do_ds(m, half)
            cur = do_ks(m, half, cur)
            do_ymm(m, half, cur)
            if bi + 1 < len(batches):
                do_setup(*batches[bi + 1])
            do_gelu(m, half)
            if half == 1:
                do_reverse(m, YTm[0])
        kb.drain_dma()


def make_consts():
    c = np.zeros((128, 260), np.float32)
    c[:, 0:128] = np.eye(128, dtype=np.float32)
    for m in range(128):
        c[(m + 64) % 128, 128 + m] = 1.0
    c[:, 256] = np.arange(128) % 64
    c[:, 257] = np.where(np.arange(128) < 64, -1.0, 1.0)
    c[:, 258] = np.where(np.arange(128) < 64, 1.0, -1.0)
    return c


def make_vbq(hf):
    v = np.full((32, 32), -1e30, np.float32)
    for qt in range(32):
        i = qt // 2
        for n in range(32):
            if n < 16 + i and (n >= 16 or hf == 1):
                v[qt, n] = 0.0
    return np.ascontiguousarray(np.broadcast_to(v[None], (128, 32, 32)))


def make_oneh():
    o = np.zeros((32, 32, 128), np.float32)
    for n in range(32):
        o[n, n, :] = 1.0
    return o.reshape(32, 32 * 128)


def rot_perm():
    idx = []
    for base in (0, 1024):
        for h in range(NH):
            for d in range(HD):
                idx.append(base + h * HD + (d + 64) % HD)
    return np.array(idx)


def make_s5c(inputs):
    p = np.arange(128)
    n = p % 64
    lam_re = f32c(inputs["lam_re"])[0]
    lam_im = f32c(inputs["lam_im"])[0]
    log_step = f32c(inputs["log_step"])[0]
    b_re = f32c(inputs["b_re"])[0]
    b_im = f32c(inputs["b_im"])[0]
    c_re = f32c(inputs["c_re"])[0]
    c_im = f32c(inputs["c_im"])[0]
    d_skip = f32c(inputs["d_skip"])[0]
    out = np.zeros((128, S5C_COLS), np.float32)
    out[:, 0:32] = lam_re[:, n].T
    out[:, 32:64] = lam_im[:, n].T
    out[:, 64:96] = log_step[None, :]
    out[:, 96:128] = d_skip[:, p % 16].T
    out[:, 128:640] = b_re[:, n, :].transpose(1, 0, 2).reshape(128, 512)
    out[:, 640:1152] = b_im[:, n, :].transpose(1, 0, 2).reshape(128, 512)
    out[:, 1152:1664] = c_re[:, :, n].transpose(2, 0, 1).reshape(128, 512)
    out[:, 1664:2176] = c_im[:, :, n].transpose(2, 0, 1).reshape(128, 512)
    mk = np.zeros((128, 2, 16, 16), np.float32)
    di = np.zeros((128, 2, 256), np.float32)
    for r in range(128):
        s8 = r // 16
        for kt in range(2):
            sfull = kt * 8 + s8
            mk[r, kt, sfull:, :] = 1.0
            di[r, kt, kt * 128 + r] = 1.0
    out[:, 2176:2688] = mk.reshape(128, 512)
    out[:, 2688:3200] = di.reshape(128, 512)
    return out


def f32c(a):
    return np.ascontiguousarray(np.asarray(a, np.float32))


def make_in_maps(inputs):
    x = np.asarray(inputs["x"], np.float32)
    w_in = np.ascontiguousarray(np.asarray(inputs["w_in"], np.float32)[0])
    s5c = make_s5c(inputs)
    maps = []
    for core in range(8):
        b, hf = core // 2, core % 2
        xa = np.zeros((SEQ, D), np.float32)
        if hf == 1:
            xa[:] = x[b]
            pos = np.arange(SEQ, dtype=np.float32)
        else:
            xa[HALF:] = x[b, :HALF]
            pos = np.concatenate([np.zeros(HALF, np.float32), np.arange(HALF, dtype=np.float32)])
        maps.append({
            "x_all": xa, "w_in": w_in,
            "g_mix": np.ascontiguousarray(np.asarray(inputs["norm_mix_g"], np.float32)[0].reshape(8, 128).T),
            "pos_all": pos, "cst": make_consts(),
            "vbq": make_vbq(hf), "tri": np.triu(np.ones((128, 128), np.float32)),
            "w_glu": f32c(inputs["w_glu"][0]), "w_out": f32c(inputs["w_out"][0]),
            "w_up": f32c(inputs["w_up"][0]), "w_down": f32c(inputs["w_down"][0]),
            "g_mlp": np.ascontiguousarray(np.asarray(inputs["norm_mlp_g"], np.float32)[0].reshape(8, 128).T),
            "g_fin": f32c(inputs["norm_final_g"]), "s5c": s5c,
        })
    return maps


def kernel(**inputs):
    nc = build(debug=False)
    maps = make_in_maps(inputs)
    res = run_bass_kernel_spmd(nc, maps, core_ids=list(range(8)))
    out = np.zeros((4, SEQ, D), np.float32)
    for core in range(8):
        b, hf = core // 2, core % 2
        out[b, hf * HALF:(hf + 1) * HALF] = np.asarray(res.results[core]["out"], np.float32)
    return out
```
